# Optimizing a Trainium2 kernel written in Bass

```python
import jax, jax.numpy as jnp
from jax import lax
import numpy as np

D_MODEL = 1024
BATCH = 8
SEQ = 4096
DEPTH = 4

CHUNK = 64

D_RWKV = D_MODEL // 2
HEAD_SIZE = 64
N_HEADS = D_RWKV // HEAD_SIZE
D_DECAY_LORA = max(32, int(round(1.8 * D_MODEL ** 0.5 / 32)) * 32)
D_AAA_LORA = max(32, int(round(1.8 * D_MODEL ** 0.5 / 32)) * 32)
D_MV_LORA = max(32, int(round(0.7 * D_MODEL ** 0.5 / 32)) * 32)
D_GATE_LORA = max(32, int(round(0.6 * D_MODEL ** 0.8 / 32)) * 32)

D_POOL = D_MODEL // 2
POOL_WINDOWS = (2, 4, 8, 16)
N_POOL_GROUPS = len(POOL_WINDOWS)
POOL_GROUP = D_POOL // N_POOL_GROUPS

D_FF = ((8 * D_MODEL // 3 + 255) // 256) * 256

N_SHIFT = 3 * D_RWKV + D_DECAY_LORA + D_AAA_LORA + D_GATE_LORA
N_IN = N_SHIFT + D_POOL + 2 * D_MODEL
N_MOD = 6 * D_MODEL

EPS_RMS = 1e-6
EPS_GN = 64e-5

kernel_name = "hybrid_rwkv7_multipool_adaln_trunk"


def rms_norm(x, g):
    xf = x.astype(jnp.float32)
    y = xf * lax.rsqrt(jnp.mean(xf * xf, axis=-1, keepdims=True) + EPS_RMS)
    return (y * g.astype(jnp.float32)).astype(x.dtype)


def token_shift(z, mu):
    prev = jnp.pad(z, ((0, 0), (1, 0), (0, 0)))[:, :-1]
    return z + (prev - z) * mu


def wkv7_scan(r, w, k, v, kk, a):
    B, S, H, N = r.shape

    def to_chunks(t):
        return jnp.moveaxis(t, 1, 0).reshape(S // CHUNK, CHUNK, B, H, N)

    def step(state, inp):
        r_t, w_t, k_t, v_t, kk_t, a_t = inp
        sk = jnp.einsum('bhvk,bhk->bhv', state, kk_t)
        state = (state * w_t[:, :, None, :]
                 - sk[..., None] * (a_t * kk_t)[:, :, None, :]
                 + v_t[..., None] * k_t[:, :, None, :])
        return state, jnp.einsum('bhvk,bhk->bhv', state, r_t)

    def chunk_step(state, chunk_inp):
        return lax.scan(step, state, chunk_inp)

    state0 = jnp.zeros((B, H, N, N), jnp.float32)
    _, ys = lax.scan(chunk_step, state0, tuple(to_chunks(t) for t in (r, w, k, v, kk, a)))
    return jnp.moveaxis(ys.reshape(S, B, H, N), 0, 1)


def rwkv7_time_mix(ps, vl, v_first, w0, w_up, a0, a_up, v0, v_up, g_up, k_k, k_a, r_k, lnx_g, lnx_b):
    B, S, _ = ps.shape
    dt = ps.dtype
    f32 = jnp.float32
    o = 0
    r = ps[..., o:o + D_RWKV]; o += D_RWKV
    k = ps[..., o:o + D_RWKV]; o += D_RWKV
    v = ps[..., o:o + D_RWKV]; o += D_RWKV
    wl = ps[..., o:o + D_DECAY_LORA]; o += D_DECAY_LORA
    al = ps[..., o:o + D_AAA_LORA]; o += D_AAA_LORA
    gl = ps[..., o:o + D_GATE_LORA]

    log_w = -jax.nn.softplus(-(w0 + jnp.tanh(wl) @ w_up)) - 0.5
    decay = jnp.exp(-jnp.exp(log_w.astype(f32)))
    a = jax.nn.sigmoid(a0 + al @ a_up)
    g = jax.nn.sigmoid(gl) @ g_up
    if vl is None:
        v_first = v
    else:
        v = v + (v_first - v) * jax.nn.sigmoid(v0 + vl @ v_up)

    def heads(t):
        return t.reshape(B, S, N_HEADS, HEAD_SIZE).astype(f32)

    kk = heads(k * k_k)
    kk = kk / jnp.maximum(jnp.sqrt(jnp.sum(kk * kk, axis=-1, keepdims=True)), 1e-12)
    k = k * (1 + (a - 1) * k_a)
    r_h, k_h, v_h, a_h, w_h = heads(r), heads(k), heads(v), heads(a), heads(decay)

    y = wkv7_scan(r_h, w_h, k_h, v_h, kk, a_h)
    mean = jnp.mean(y, axis=-1, keepdims=True)
    var = jnp.mean(jnp.square(y - mean), axis=-1, keepdims=True)
    y = (y - mean) * lax.rsqrt(var + EPS_GN)
    y = y.reshape(B, S, D_RWKV) * lnx_g.astype(f32) + lnx_b.astype(f32)
    bonus = jnp.sum(r_h * k_h * r_k.astype(f32), axis=-1, keepdims=True) * v_h
    y = (y + bonus.reshape(B, S, D_RWKV)) * g.astype(f32)
    return y.astype(dt), v_first


def multiscale_pool(z, pool_w, pool_scale):
    B, S, _ = z.shape
    zg = z.reshape(B, S, N_POOL_GROUPS, POOL_GROUP).astype(jnp.float32)
    cs = jnp.cumsum(zg, axis=1)
    pos = jnp.arange(1, S + 1, dtype=jnp.float32)
    means = []
    for gi, win in enumerate(POOL_WINDOWS):
        cg = cs[:, :, gi]
        lower = jnp.pad(cg, ((0, 0), (win, 0), (0, 0)))[:, :S]
        count = jnp.minimum(pos, float(win))[None, :, None]
        means.append((cg - lower) / count)
    pooled = (jnp.stack(means, axis=2) - zg).astype(z.dtype)
    y = jnp.einsum('bsgc,gcd->bsgd', pooled, pool_w)
    return y.reshape(B, S, D_POOL) * pool_scale


def setup_inputs(seed: int = 0) -> dict:
    key = jax.random.key(seed)
    ks = iter(jax.random.split(key, 32))
    f32 = jnp.float32

    def nrm(shape, s):
        return jax.random.normal(next(ks), shape, f32) * s

    def uni(shape, lo, hi):
        return jax.random.uniform(next(ks), shape, f32, lo, hi)

    L, Lv = DEPTH, DEPTH - 1
    return {
        "x": nrm((BATCH, SEQ, D_MODEL), 1.0),
        "c": nrm((BATCH, D_MODEL), 1.0),
        "ada_w": nrm((L, D_MODEL, N_MOD), 0.5 * D_MODEL ** -0.5),
        "ada_b": nrm((L, N_MOD), 0.1),
        "norm1_g": 1.0 + nrm((L, D_MODEL), 0.1),
        "w_in": nrm((L, D_MODEL, N_IN), D_MODEL ** -0.5),
        "v_down": nrm((Lv, D_MODEL, D_MV_LORA), D_MODEL ** -0.5),
        "mu_shift": uni((L, N_SHIFT), 0.0, 1.0),
        "mu_v": uni((Lv, D_MV_LORA), 0.0, 1.0),
        "w0": uni((L, D_RWKV), -6.0, -1.0),
        "w_up": nrm((L, D_DECAY_LORA, D_RWKV), 0.5 * D_DECAY_LORA ** -0.5),
        "a0": nrm((L, D_RWKV), 0.1),
        "a_up": nrm((L, D_AAA_LORA, D_RWKV), 0.5 * D_AAA_LORA ** -0.5),
        "v0": nrm((Lv, D_RWKV), 0.1),
        "v_up": nrm((Lv, D_MV_LORA, D_RWKV), 0.5 * D_MV_LORA ** -0.5),
        "g_up": nrm((L, D_GATE_LORA, D_RWKV), D_GATE_LORA ** -0.5),
        "k_k": 0.85 + nrm((L, D_RWKV), 0.05),
        "k_a": 1.0 + nrm((L, D_RWKV), 0.05),
        "r_k": nrm((L, N_HEADS, HEAD_SIZE), 0.1),
        "lnx_g": 1.0 + nrm((L, D_RWKV), 0.1),
        "lnx_b": nrm((L, D_RWKV), 0.02),
        "pool_w": nrm((L, N_POOL_GROUPS, POOL_GROUP, POOL_GROUP), POOL_GROUP ** -0.5),
        "pool_scale": 1.0 + nrm((L, D_POOL), 0.1),
        "proj_a": nrm((L, D_RWKV, D_MODEL), D_RWKV ** -0.5),
        "proj_b": nrm((L, D_POOL, D_MODEL), D_POOL ** -0.5),
        "w_out": nrm((L, D_MODEL, D_MODEL), D_MODEL ** -0.5),
        "norm2_g": 1.0 + nrm((L, D_MODEL), 0.1),
        "w_gu": nrm((L, D_MODEL, 2 * D_FF), D_MODEL ** -0.5),
        "w_down": nrm((L, D_FF, D_MODEL), D_FF ** -0.5),
        "final_g": 1.0 + nrm((D_MODEL,), 0.1),
    }


def reference(x, c, ada_w, ada_b, norm1_g, w_in, v_down, mu_shift, mu_v, w0, w_up, a0, a_up, v0, v_up,
              g_up, k_k, k_a, r_k, lnx_g, lnx_b, pool_w, pool_scale, proj_a, proj_b, w_out, norm2_g,
              w_gu, w_down, final_g):
    B, S, D = x.shape
    c_act = jax.nn.silu(c)
    v_first = None
    for l in range(DEPTH):
        mod = c_act @ ada_w[l] + ada_b[l]
        sh1, sc1, gt1, sh2, sc2, gt2 = [m[:, None, :] for m in jnp.split(mod, 6, axis=-1)]

        h = rms_norm(x, norm1_g[l]) * (1 + sc1) + sh1
        if l == 0:
            p = h @ w_in[0]
            vl = None
        else:
            p = h @ jnp.concatenate([w_in[l], v_down[l - 1]], axis=1)
            vl = token_shift(p[..., N_IN:], mu_v[l - 1])
        ps = token_shift(p[..., :N_SHIFT], mu_shift[l])
        z_pool = p[..., N_SHIFT:N_SHIFT + D_POOL]
        br_gate = jax.nn.sigmoid(p[..., N_SHIFT + D_POOL:N_IN]).reshape(B, S, 2, D)

        y_a, v_first = rwkv7_time_mix(ps, vl, v_first, w0[l], w_up[l], a0[l], a_up[l],
                                      None if l == 0 else v0[l - 1], None if l == 0 else v_up[l - 1],
                                      g_up[l], k_k[l], k_a[l], r_k[l], lnx_g[l], lnx_b[l])
        y_b = multiscale_pool(z_pool, pool_w[l], pool_scale[l])
        mixed = br_gate[:, :, 0] * (y_a @ proj_a[l]) + br_gate[:, :, 1] * (y_b @ proj_b[l])
        x = x + gt1 * (mixed @ w_out[l])

        h2 = rms_norm(x, norm2_g[l]) * (1 + sc2) + sh2
        gate, up = jnp.split(h2 @ w_gu[l], 2, axis=-1)
        x = x + gt2 * ((jax.nn.silu(gate) * up) @ w_down[l])
    return rms_norm(x, final_g)
```

```python
import os
import sys
import numpy as np
from contextlib import ExitStack
import concourse.bass as bass
import concourse.mybir as mybir
from concourse.bass_utils import run_bass_kernel_spmd

F32 = mybir.dt.float32
BF16 = mybir.dt.bfloat16
ALU = mybir.AluOpType
AF = mybir.ActivationFunctionType
AX = mybir.AxisListType

D = 1024
DR = 512
NH = 8
HS = 64
DFF = 2816
NFF = DFF // 128
N_SHIFT = 1824
N_IN = 4384
NINX = 4416
EPS_RMS = 1e-6
EPS_GN = 64e-5
CH = 128
KDEC = -0.6065306597126334
POOL_W = (2, 4, 8, 16)


class Prog:
    ENG = ("pe", "act", "dve", "pool", "sp")

    def __init__(self, nc, es):
        self.nc = nc
        self.es = es
        self.ops = {e: [] for e in self.ENG}
        self.cnt = {}
        self.sems = {}
        self.waited = {e: {} for e in self.ENG}
        self.recs = {}
        for e in ("pe", "act", "dve", "pool"):
            self.newsem("c_" + e)
        self.n_ops = 0

    def newsem(self, name):
        self.sems[name] = self.es.enter_context(self.nc.semaphore(name))
        self.cnt[name] = 0
        return name

    @staticmethod
    def region(ap):
        t = ap.tensor
        cls = type(t).__name__
        if not ("SB" in cls or "PSum" in cls):
            return None
        a = ap.ap
        off = ap.offset
        pst = a[0][0]
        if pst == 0:
            pst = 1 << 30
        p0 = off // pst
        f0 = off % pst
        p1 = p0 + a[0][1]
        ext = 1
        for s, c in a[1:]:
            ext += (c - 1) * abs(s)
        esz = 2 if ap.dtype == BF16 else 4
        if "PSum" in cls:
            return (t.name, 0, 128, 0, 1 << 20)
        return (t.name, p0, p1, f0 * esz, (f0 + ext) * esz)

    def op(self, eng, fn, reads, writes, inc=True, dma_sem=None, extra_waits=(), group_final=None):
        deps = {}

        def add(sem, val):
            if val > deps.get(sem, 0):
                deps[sem] = val

        rregs = [r for r in (self.region(a) for a in reads) if r is not None]
        wregs = [r for r in (self.region(a) for a in writes) if r is not None]
        for (nm, p0, p1, f0, f1) in rregs:
            psum = f1 >= (1 << 20)
            for rec in self.recs.get(nm, ()):
                if (rec[4] == "w" or (psum and rec[7] != eng)) and rec[0] < p1 and p0 < rec[1] \
                        and rec[2] < f1 and f0 < rec[3]:
                    add(rec[5], rec[6])
        for (nm, p0, p1, f0, f1) in wregs:
            for rec in self.recs.get(nm, ()):
                if rec[0] < p1 and p0 < rec[1] and rec[2] < f1 and f0 < rec[3]:
                    add(rec[5], rec[6])
        for s, v in extra_waits:
            add(s, v)
        if eng == "pe":
            deps.pop("c_pe", None)
        if dma_sem is not None:
            self.cnt[dma_sem] += 16
            tok = (dma_sem, self.cnt[dma_sem] if group_final is None else group_final)
            incspec = (dma_sem, 16)
        else:
            s = "c_" + eng
            if inc:
                self.cnt[s] += 1
                tok = (s, self.cnt[s])
                incspec = (s, 1)
            else:
                tok = (s, self.cnt[s] + 1)
                incspec = None
        waits = []
        wd = self.waited[eng]
        for s, v in deps.items():
            if v > wd.get(s, 0):
                waits.append((s, v))
                wd[s] = v
        site = ""
        if os.environ.get("KDBG"):
            f = sys._getframe(1)
            while f is not None and f.f_code.co_name != "build_program" and f.f_back is not None:
                if f.f_back.f_code.co_name == "build_program" or f.f_code.co_name in ("layer_body", "rmsnorm_mod", "proj_chunk", "shift_chunk", "squares", "compute_mod"):
                    site += f"{f.f_code.co_name}:{f.f_lineno} "
                f = f.f_back
            if f is not None:
                site += f"bp:{f.f_lineno}"
        self.ops[eng].append((waits, fn, incspec, site))
        self.n_ops += 1
        for (nm, p0, p1, f0, f1) in wregs:
            lst = self.recs.setdefault(nm, [])
            lst[:] = [r for r in lst if not (p0 <= r[0] and r[1] <= p1 and f0 <= r[2] and r[3] <= f1)]
            lst.append((p0, p1, f0, f1, "w", tok[0], tok[1], eng))
        for (nm, p0, p1, f0, f1) in rregs:
            lst = self.recs.setdefault(nm, [])
            lst[:] = [r for r in lst if not (r[4] == "r" and r[7] == eng and p0 <= r[0] and r[1] <= p1
                                             and f0 <= r[2] and r[3] <= f1)]
            lst.append((p0, p1, f0, f1, "r", tok[0], tok[1], eng))
        return tok

    def mm(self, out, lhsT, rhs, start=True, stop=True):
        return self.op("pe", lambda e: e.matmul(out, lhsT, rhs, start=start, stop=stop),
                       [lhsT, rhs], [out], inc=stop)

    def tr(self, out, in_, ident, inc=True):
        return self.op("pe", lambda e: e.transpose(out, in_, ident), [in_, ident], [out], inc=inc)

    def act(self, out, in_, func, bias=None, scale=None, eng="act"):
        kw = {}
        rd = [in_]
        if bias is not None:
            kw["bias"] = bias
            if not isinstance(bias, (int, float)):
                rd.append(bias)
        if scale is not None:
            kw["scale"] = scale
            if not isinstance(scale, (int, float)):
                rd.append(scale)
        return self.op("act", lambda e: e.activation(out=out, in_=in_, func=func, **kw), rd, [out])

    def tt(self, out, a, b, op, eng="dve"):
        return self.op(eng, lambda e: e.tensor_tensor(out=out, in0=a, in1=b, op=op), [a, b], [out])

    def ts(self, out, a, s1, s2, op0, op1=None, eng="dve"):
        rd = [a] + [s for s in (s1, s2) if s is not None and not isinstance(s, (int, float))]
        if op1 is None:
            return self.op(eng, lambda e: e.tensor_scalar(out=out, in0=a, scalar1=s1, scalar2=None, op0=op0),
                           rd, [out])
        return self.op(eng, lambda e: e.tensor_scalar(out=out, in0=a, scalar1=s1, scalar2=s2, op0=op0, op1=op1),
                       rd, [out])

    def stt(self, out, a, s, b, op0, op1, eng="dve"):
        rd = [a, b] + ([] if isinstance(s, (int, float)) else [s])
        return self.op(eng, lambda e: e.scalar_tensor_tensor(out=out, in0=a, scalar=s, in1=b, op0=op0, op1=op1),
                       rd, [out])

    def copy(self, out, in_, eng="dve"):
        if eng == "act":
            return self.op("act", lambda e: e.copy(out=out, in_=in_), [in_], [out])
        return self.op(eng, lambda e: e.tensor_copy(out=out, in_=in_), [in_], [out])

    def memset(self, out, val, eng="dve"):
        return self.op(eng, lambda e: e.memset(out, val), [], [out])

    def scan(self, out, d0, d1, init, op0, op1):
        return self.op("dve", lambda e: e.tensor_tensor_scan(out=out, data0=d0, data1=d1, initial=init,
                                                             op0=op0, op1=op1), [d0, d1], [out])

    def reduce(self, out, in_, op, axis=AX.X, eng="dve"):
        return self.op(eng, lambda e: e.tensor_reduce(out=out, in_=in_, axis=axis, op=op), [in_], [out])

    def recip(self, out, in_):
        return self.op("dve", lambda e: e.reciprocal(out=out, in_=in_), [in_], [out])

    def dma(self, out, in_, sem, eng="sp", extra_waits=(), group_final=None, **kw):
        return self.op(eng, lambda e: e.dma_start(out=out, in_=in_, **kw), [in_], [out], dma_sem=sem,
                       extra_waits=extra_waits, group_final=group_final)

    def dma_group(self, sem, reqs, eng="sp", extra_waits=()):
        final = self.cnt[sem] + 16 * len(reqs)
        for (out, in_, kw) in reqs:
            self.dma(out, in_, sem, eng=eng, extra_waits=extra_waits, group_final=final, **kw)

    def emit(self, final_waits):
        nc = self.nc
        sems = self.sems

        def run(e, ops, tail=()):
            for waits, fn, incspec, site in ops:
                for s, v in waits:
                    e.wait_ge(sems[s], v)
                ins = fn(e)
                if site:
                    ins.annotate(site)
                if incspec is not None:
                    ins.then_inc(sems[incspec[0]], incspec[1])
            for s, v in tail:
                e.wait_ge(sems[s], v)

        with nc.Block() as block:
            @block.tensor
            def _(e):
                run(e, self.ops["pe"])

            @block.scalar
            def _(e):
                run(e, self.ops["act"])

            @block.vector
            def _(e):
                run(e, self.ops["dve"])

            @block.gpsimd
            def _(e):
                run(e, self.ops["pool"])

            @block.sync
            def _(e):
                run(e, self.ops["sp"], tail=final_waits)


def build_program(S, L, T=256, NW=4, debug_taps=False, STAGE=99):
    nc = bass.Bass("TRN2", target_bir_lowering=False)
    NT = S // T
    SUB = int(os.environ.get('KSUB', '99'))
    NB = T // 128
    es = ExitStack()
    P = Prog(nc, es)

    def din(name, shape):
        return nc.dram_tensor(name, list(shape), F32, kind="ExternalInput").ap()

    Lv = max(L - 1, 1)
    x_d = din("x", (S, D))
    c_d = din("c", (1, D))
    ada_w = din("ada_w", (L, D, 6 * D))
    ada_b = din("ada_b", (L, 6 * D))
    norm1_g = din("norm1_g", (L, D))
    w_in = din("w_in", (L, D, N_IN))
    v_down = din("v_down", (Lv, D, 32))
    mu_shift = din("mu_shift", (L, N_SHIFT))
    mu_v = din("mu_v", (Lv, 32))
    w0 = din("w0", (L, DR))
    w_up = din("w_up", (L, 64, DR))
    a0 = din("a0", (L, DR))
    a_up = din("a_up", (L, 64, DR))
    v0 = din("v0", (Lv, DR))
    v_up = din("v_up", (Lv, 32, DR))
    g_up = din("g_up", (L, 160, DR))
    k_k = din("k_k", (L, DR))
    k_a = din("k_a", (L, DR))
    r_k = din("r_k", (L, DR))
    lnx_g = din("lnx_g", (L, DR))
    lnx_b = din("lnx_b", (L, DR))
    pool_w = din("pool_w", (L, 4 * 128, 128))
    pool_scale = din("pool_scale", (L, DR))
    proj_a = din("proj_a", (L, DR, D))
    proj_b = din("proj_b", (L, DR, D))
    w_out = din("w_out", (L, D, D))
    norm2_g = din("norm2_g", (L, D))
    w_gu = din("w_gu", (L, D, 2 * DFF))
    w_down = din("w_down", (L, DFF, D))
    final_g = din("final_g", (1, D))
    out_d = nc.dram_tensor("out", [S, D], F32, kind="ExternalOutput").ap()

    def dscr(name, shape):
        return nc.dram_tensor(name, list(shape), BF16, kind="Internal").ap()

    win_b = dscr("win_b", (L, D, NINX))
    wup_b = dscr("wup_b", (L, 64, DR))
    aup_b = dscr("aup_b", (L, 64, DR))
    vup_b = dscr("vup_b", (Lv, 32, DR))
    gup_b = dscr("gup_b", (L, 160, DR))
    poolw_b = dscr("poolw_b", (L, 512, 128))
    pa_b = dscr("pa_b", (L, DR, D))
    pb_b = dscr("pb_b", (L, DR, D))
    wout_b = dscr("wout_b", (L, D, D))
    wgu_b = dscr("wgu_b", (L, D, 2 * DFF))
    wdn_b = dscr("wdn_b", (L, DFF, D))

    def sb(name, shape, dt=F32):
        return es.enter_context(nc.sbuf_tensor(name, list(shape), dt))

    def ps(name, shape, dt=F32):
        return es.enter_context(nc.psum_tensor(name, list(shape), dt))

    xs = sb("xs", (128, 8, T))
    actb = sb("actb", (128, NFF, T), BF16)
    sqb = actb
    hb = sb("hb", (128, 8, T), BF16)
    rstd = sb("rstd", (128, T))
    wring = [sb(f"wr{i}", (128, 4096), BF16) for i in range(NW)]
    misc = sb("misc", (128, 6, 512), BF16)
    praw = [sb(f"praw{i}", (128, T + 1)) for i in range(3)]
    rsh = sb("rsh", (128, 4, T))
    ksh = sb("ksh", (128, 4, T))
    vsh = sb("vsh", (128, 4, T))
    vfirst = sb("vfirst", (128, 4, T))
    wab = sb("wab", (128, T), BF16)
    glb = sb("glb", (128, 2, T), BF16)
    vlb = sb("vlb", (32, T), BF16)
    ZW = 4 * (T + 16)
    PW = T + 16
    parena = sb("parena", (128, max(ZW + 2 * PW + 2 * T, 8 * T)))
    zext = parena[:, 0:ZW].rearrange("p (g n) -> p g n", g=4)
    pscr = [parena[:, ZW + i * PW:ZW + (i + 1) * PW] for i in range(2)]
    pooled = parena[:, ZW + 2 * PW:ZW + 2 * PW + 2 * T].bitcast(BF16).rearrange("p (g n) -> p g n", g=4)
    sgab = parena[:, 0:8 * T].bitcast(BF16).rearrange("p (a j n) -> p a j n", a=2, j=8)
    ybb = sb("ybb", (128, 4, T), BF16)
    rs = [sb(f"rs{i}", (128, T)) for i in range(10)]
    rsB = [sb(f"rsB{i}", (128, T)) for i in range(10)]
    rt_b = sb("rt_b", (128, 4, T), BF16)
    kt_b = sb("kt_b", (128, 4, T), BF16)
    bt_b = sb("bt_b", (128, 4, T), BF16)
    at_b = sb("at_b", (128, 4, T), BF16)
    fmtmp = [sb(f"fmtmp{i}", (128, T), BF16) for i in range(3)]
    fmtmpB = [sb(f"fmtmpB{i}", (128, T), BF16) for i in range(3)]
    atm = sb("atm", (128, NB, 512), BF16)
    kendtm = sb("kendtm", (128, NB, 512), BF16)
    bendtm = sb("bendtm", (128, NB, 512), BF16)
    vtm = sb("vtm", (128, NB, 512), BF16)
    pcs = sb("pcs", (128, NB, 4))
    rkb = sb("rkb", (128, 4, T), BF16)
    Xf = sb("Xf", (128, 8, 128), BF16)
    XTf = sb("XTf", (128, 8, 128), BF16)
    Xb = [sb(f"Xb{i}", (128, 8, 128), BF16) for i in range(2)]
    XTb = [sb(f"XTb{i}", (128, 8, 128), BF16) for i in range(2)]
    Tb = [sb(f"Tb{i}", (128, 8, 128), BF16) for i in range(2)]
    TTb2 = [sb(f"TTb{i}", (128, 8, 128), BF16) for i in range(2)]
    Qb = sb("Qb", (128, 8, 128), BF16)
    Pb = sb("Pb", (128, 8, 128), BF16)
    akT = sb("akT", (128, 8, 128), BF16)
    rbT = sb("rbT", (128, 8, 128), BF16)
    rkT = sb("rkT", (128, 8, 128), BF16)
    akv = sb("akv", (128, 512), BF16)
    vhat = sb("vhat", (128, 512))
    ahT = sb("ahT", (128, 4, 128), BF16)
    ub = sb("ub", (128, 512), BF16)
    Hs = [sb(f"Hs{l}", (128, 4, 128)) for l in range(L)]
    Hb1 = sb("Hb1", (128, 4, 128), BF16)
    Hb = [Hb1 for l in range(L)]
    hx = sb("hx", (128, 4, 128))
    ysb = sb("ysb", (128, 512))
    ysq = sb("ysq", (128, 512))
    ynb = sb("ynb", (128, 512), BF16)
    gst = sb("gst", (128, 8, 4))
    yab = sb("yab", (128, 4, T), BF16)
    ya_s = rsB[4:6]
    mixed = sb("mixed", (128, 8, T), BF16)
    sg = rsB[0:2]
    mx = rsB[2:4]
    fsc = [sb(f"fsc{i}", (128, T)) for i in range(2)]
    xstage = actb[:, :, :].rearrange("p a b -> p (a b)").bitcast(F32)[:, 0:NB * D].rearrange("p (a b) -> p a b", b=D)
    ostage = xstage
    identb = sb("identb", (128, 128), BF16)
    identf = sb("identf", (128, 128))
    onesblk = sb("onesblk", (128, 128), BF16)
    onesmean = sb("onesmean", (128, 128), BF16)
    m_strict = sb("m_strict", (128, 4, 128), BF16)
    m_incl = sb("m_incl", (128, 4, 128), BF16)
    m_lower = sb("m_lower", (128, 4, 128), BF16)
    m_bd = sb("m_bd", (128, 4, 128), BF16)
    scanmask = sb("scanmask", (128, T))
    iota_p = sb("iota_p", (128, 1))
    iota_f = sb("iota_f", (128, 128))
    invc0 = sb("invc0", (128, 4, 16))
    cact = sb("cact", (128, 8))
    NCOL = 96
    cols = [sb(f"cols{l}", (128, NCOL)) for l in range(L)]
    fgc = sb("fgc", (128, 8))
    carry = [sb(f"carry{l}", (128, 16)) for l in range(L)]
    zcarry = [sb(f"zcarry{l}", (128, 4, 16)) for l in range(L)]
    ADN = NB * D // 8
    adast = xstage[:, :, :].rearrange("p a b -> p (a b)").rearrange("p (k n) -> p k n", k=8)

    pbank = [ps(f"pb{i}", (128, 512)) for i in range(8)]
    pctr = [0]

    def bank():
        b = pbank[pctr[0] % 8]
        pctr[0] += 1
        return b

    def sidx(h):
        return (h % 2) * 4 + h // 2

    def bank_bf(b):
        return b[:, :].bitcast(BF16)

    C_G1, C_G2 = 0, 8
    C_SH1, C_SH2 = 16, 24
    C_GT1, C_GT2 = 32, 40
    C_MU = 48
    C_MUV = 63
    C_W0, C_A0, C_V0, C_KK, C_KA, C_RK, C_LG, C_LB = 64, 68, 72, 76, 80, 84, 88, 92
    cols2 = [sb(f"cols2_{l}", (128, 16)) for l in range(L)]

    sem_cv = P.newsem("d_cv")
    sem_cst = P.newsem("d_cst")
    sem_x = [P.newsem(f"d_x{i}") for i in range(NB)]
    sem_o = [P.newsem(f"d_o{i}") for i in range(NB)]
    sem_w = [P.newsem(f"d_w{i}") for i in range(NW)]
    sem_misc = P.newsem("d_misc")
    sem_ada = P.newsem("d_ada")

    sem_cvl = [P.newsem(f"d_cv{l}") for l in range(L)]
    curl = [0]

    def cast(dst, src):
        P.op("pool", lambda e: e.dma_start(out=dst, in_=src), [], [], dma_sem=sem_cvl[curl[0]])

    for l in range(L):
        curl[0] = l
        for i in range(8):
            cast(win_b[l, i * 128:(i + 1) * 128, 0:N_IN], w_in[l, i * 128:(i + 1) * 128, :])
        if l >= 1:
            cast(win_b[l, :, N_IN:NINX], v_down[l - 1])
            cast(vup_b[l - 1], v_up[l - 1])
        cast(wup_b[l], w_up[l])
        cast(aup_b[l], a_up[l])
        cast(gup_b[l], g_up[l])
        cast(poolw_b[l], pool_w[l])
        for i in range(4):
            cast(pa_b[l, i * 128:(i + 1) * 128, :], proj_a[l, i * 128:(i + 1) * 128, :])
            cast(pb_b[l, i * 128:(i + 1) * 128, :], proj_b[l, i * 128:(i + 1) * 128, :])
        for i in range(8):
            cast(wout_b[l, i * 128:(i + 1) * 128, :], w_out[l, i * 128:(i + 1) * 128, :])
            cast(wgu_b[l, i * 128:(i + 1) * 128, :], w_gu[l, i * 128:(i + 1) * 128, :])
        for i in range(NFF):
            cast(wdn_b[l, i * 128:(i + 1) * 128, :], w_down[l, i * 128:(i + 1) * 128, :])
    CVL = [[(sem_cvl[l], P.cnt[sem_cvl[l]])] for l in range(L)]

    P.op("pool", lambda e: e.iota(iota_p[:, :], pattern=[[0, 1]], base=0, channel_multiplier=1, allow_small_or_imprecise_dtypes=True), [], [iota_p[:, :]])
    P.op("pool", lambda e: e.iota(iota_f[:, :], pattern=[[1, 128]], base=0, channel_multiplier=0, allow_small_or_imprecise_dtypes=True), [], [iota_f[:, :]])
    P.ts(identf[:, :], iota_f[:, :], iota_p[:, 0:1], None, ALU.is_equal)
    P.copy(identb[:, :], identf[:, :])
    P.ts(m_strict[:, 0, :], iota_f[:, :], iota_p[:, 0:1], None, ALU.is_gt)
    P.ts(m_incl[:, 0, :], iota_f[:, :], iota_p[:, 0:1], None, ALU.is_ge)
    P.ts(m_lower[:, 0, :], iota_f[:, :], iota_p[:, 0:1], None, ALU.is_lt)
    P.ts(rs[0][:, 0:128], iota_f[:, :], 64.0, None, ALU.is_ge)
    P.ts(rs[1][:, 0:1], iota_p[:, 0:1], 64.0, None, ALU.is_ge)
    P.ts(m_bd[:, 0, :], rs[0][:, 0:128], rs[1][:, 0:1], None, ALU.is_equal)
    for m in (m_strict, m_incl, m_lower, m_bd):
        for j in range(1, 4):
            P.copy(m[:, j, :], m[:, 0, :])
    P.copy(onesblk[:, :], m_bd[:, 0, :])
    HB = (8, 16, 32, 64)
    scr = [rs[i][:, 0:128] for i in range(10)] + [rsB[i][:, 0:128] for i in range(10)]
    Fb, PBt = {}, {}
    si = 0
    for b_ in HB:
        F = scr[si]
        si += 1
        P.op("pool", lambda e, F=F, b_=b_: e.iota(F, pattern=[[1, 128 // b_], [0, b_]], base=0, channel_multiplier=0,
                                                  allow_small_or_imprecise_dtypes=True), [], [F])
        Fb[b_] = F
        bk = bank()
        P.tr(bk[:, 0:128], F, identf[:, :])
        PB = scr[si]
        si += 1
        P.copy(PB, bk[:, 0:128])
        PBt[b_] = PB
    t1, t2 = scr[si], scr[si + 1]
    mD_l = sb("mD_l", (128, 128), BF16)
    mD_u = sb("mD_u", (128, 128), BF16)
    P.tt(t1, Fb[8], PBt[8], ALU.is_equal)
    P.tt(mD_l[:, :], t1, m_lower[:, 0, :], ALU.mult)
    P.tt(mD_u[:, :], t1, m_strict[:, 0, :], ALU.mult)
    ML, MU = {}, {}
    for b_ in HB:
        if b_ < 64:
            P.tt(t1, Fb[2 * b_], PBt[2 * b_], ALU.is_equal)
        else:
            P.memset(t1, 1.0)
        if b_ < 64:
            ML[b_] = sb(f"ML{b_}", (128, 128), BF16)
            P.tt(t2, Fb[b_], PBt[b_], ALU.is_lt)
            P.tt(ML[b_][:, :], t1, t2, ALU.mult)
        MU[b_] = sb(f"MU{b_}", (128, 128), BF16)
        P.tt(t2, Fb[b_], PBt[b_], ALU.is_gt)
        P.tt(MU[b_][:, :], t1, t2, ALU.mult)

    def bc4(m):
        return m[:, :].unsqueeze(1).broadcast_to([128, 4, 128])
    P.memset(onesmean[:, :], 1.0 / D)
    P.memset(scanmask[:, :], 1.0)
    for j in range(NB):
        P.memset(scanmask[:, j * CH:j * CH + 1], 0.0)
    for gi, w in enumerate(POOL_W):
        P.memset(invc0[:, gi, :], 1.0 / w)
        for t in range(w - 1):
            P.memset(invc0[:, gi, t:t + 1], 1.0 / (t + 1))
    P.memset(pscr[0][:, :], 0.0)
    P.memset(pscr[1][:, :], 0.0)
    for l in range(L):
        P.memset(carry[l][:, :], 0.0)
        P.memset(zcarry[l][:, :, :], 0.0)
        P.memset(Hs[l][:, :, :], 0.0)

    cst_reqs = []
    NC_KW = dict(allow_slow_non_contiguous=True)

    def colload(dst, src_row, n):
        for k0 in range(0, n // 128, 8):
            k1 = min(k0 + 8, n // 128)
            cst_reqs.append((dst[:, k0:k1], src_row[k0 * 128:k1 * 128].rearrange("(k p) -> p k", p=128), NC_KW))

    cst_reqs.append((cact[:, :], c_d[0].rearrange("(k p) -> p k", p=128), NC_KW))
    cst_reqs.append((fgc[:, :], final_g[0].rearrange("(k p) -> p k", p=128), NC_KW))
    mcols = [sb(f"mcol{l}", (128, 48)) for l in range(L)]
    for l in range(L):
        cl = cols[l]
        c2 = cols2[l]
        colload(c2[:, 8:16], norm1_g[l], D)
        colload(cl[:, C_G2:C_G2 + 8], norm2_g[l], D)
        colload(cl[:, C_MU:C_MU + 14], mu_shift[l, 0:1792], 1792)
        cst_reqs.append((cl[0:32, C_MU + 14:C_MU + 15], mu_shift[l, 1792:1824].rearrange("(p k) -> p k", k=1), NC_KW))
        if l >= 1:
            cst_reqs.append((cl[0:32, C_MUV:C_MUV + 1], mu_v[l - 1].rearrange("(p k) -> p k", k=1), NC_KW))
            colload(cl[:, C_V0:C_V0 + 4], v0[l - 1], DR)
        colload(cl[:, C_W0:C_W0 + 4], w0[l], DR)
        colload(cl[:, C_A0:C_A0 + 4], a0[l], DR)
        colload(cl[:, C_KK:C_KK + 4], k_k[l], DR)
        colload(cl[:, C_KA:C_KA + 4], k_a[l], DR)
        colload(cl[:, C_RK:C_RK + 4], r_k[l], DR)
        colload(cl[:, C_LG:C_LG + 4], lnx_g[l], DR)
        colload(cl[:, C_LB:C_LB + 4], lnx_b[l], DR)
        colload(c2[:, 0:4], pool_scale[l], DR)
        colload(mcols[l][:, :], ada_b[l], 6 * D)
    for l in range(L):
        P.memset(cols[l][:, :], 0.0)
    P.dma_group(sem_cst, cst_reqs)
    for l in range(L):
        P.ts(cols2[l][:, 4:8], cols[l][:, C_KA:C_KA + 4], -1.0, 1.0, ALU.mult, ALU.add)
    P.act(cact[:, :], cact[:, :], AF.Silu)
    def layer_units(l):
        u = []
        wl = win_b[l]

        def kview(w, c0, n):
            return w[:, c0:c0 + n].rearrange("(k p) n -> p k n", p=128)
        u.append(("r", [(kview(wl, 0, 512), 8, 512, 0)]))
        u.append(("k", [(kview(wl, 512, 512), 8, 512, 0)]))
        u.append(("v", [(kview(wl, 1024, 512), 8, 512, 0)]))
        u.append(("lora", [(kview(wl, 1536, 288), 8, 512, 0)]))
        if l >= 1:
            u.append(("vl", [(kview(wl, N_IN, 32), 8, 32, 0)]))
        u.append(("pool", [(kview(wl, 1824, 512), 8, 512, 0)]))
        u.append(("ga0", [(kview(wl, 2336, 512), 8, 512, 0)]))
        u.append(("gb0", [(kview(wl, 3360, 512), 8, 512, 0)]))
        u.append(("ga1", [(kview(wl, 2336 + 512, 512), 8, 512, 0)]))
        u.append(("gb1", [(kview(wl, 3360 + 512, 512), 8, 512, 0)]))
        u.append(("pa", [(pa_b[l].rearrange("(k p) n -> p k n", p=128), 4, 1024, 0)]))
        u.append(("pb", [(pb_b[l].rearrange("(k p) n -> p k n", p=128), 4, 1024, 0)]))
        for h in range(2):
            u.append((f"wo{h}", [(kview(wout_b[l], h * 512, 512), 8, 512, 0)]))
        for f0 in range(0, NFF, 4):
            n = min(4, NFF - f0) * 128
            u.append((f"fg{f0}", [(kview(wgu_b[l], f0 * 128, n), 8, 512, 0)]))
            u.append((f"fu{f0}", [(kview(wgu_b[l], DFF + f0 * 128, n), 8, 512, 0)]))
        for h in range(2):
            for k0 in range(0, NFF, 8):
                nk = min(8, NFF - k0)
                src = wdn_b[l, k0 * 128:(k0 + nk) * 128, h * 512:(h + 1) * 512].rearrange("(k p) n -> p k n", p=128)
                u.append((f"fd{h}_{k0}", [(src, nk, 512, 0)]))
        return u

    units = []
    for i in range(NT):
        for l in range(L):
            units += [(l, nm, lst) for nm, lst in layer_units(l)]
    wstate = {"loaded": 0, "cur": 0}

    def wload_next():
        n = wstate["loaded"]
        if n >= len(units):
            return
        l, nm, lst = units[n]
        slot = wring[n % NW]
        reqs = []
        for (src, nk, ncol, c0) in lst:
            w = src.shape[2]
            dst = slot[:, 0:nk * ncol].rearrange("p (k n) -> p k n", n=ncol)[:, :, c0:c0 + w]
            reqs.append((dst, src, {}))
        P.dma_group(sem_w[n % NW], reqs, extra_waits=CVL[l])
        wstate["loaded"] = n + 1

    for _ in range(NW):
        wload_next()

    def wget(l, nm):
        for n in range(wstate["cur"], min(wstate["cur"] + NW, len(units))):
            ul, unm, lst = units[n]
            if (ul, unm) == (l, nm):
                assert n < wstate["loaded"]
                return wring[n % NW]
        raise AssertionError((l, nm, wstate))

    def wdone():
        wstate["cur"] += 1
        wload_next()

    def load_misc(l):
        reqs = [(misc[0:64, 0, :], wup_b[l], {}), (misc[64:128, 1, :], aup_b[l], {})]
        if l >= 1:
            reqs.append((misc[0:32, 2, :], vup_b[l - 1], {}))
        reqs.append((misc[:, 3, :], gup_b[l, 0:128, :], {}))
        reqs.append((misc[0:32, 4, :], gup_b[l, 128:160, :], {}))
        reqs.append((misc[:, 5, :].rearrange("p (g n) -> p g n", n=128),
                     poolw_b[l].rearrange("(g p) n -> p g n", p=128), {}))
        P.dma_group(sem_misc, reqs, extra_waits=CVL[l])

    def compute_mod(l):
        cl = cols[l]
        mcol = mcols[l]
        NJ = ADN // 128
        for g in range(6 * D // ADN):
            P.dma(adast, ada_w[l, :, g * ADN:(g + 1) * ADN].rearrange("(k p) n -> p k n", p=128), sem_ada)
            b = bank()
            for j in range(NJ):
                for k in range(8):
                    P.mm(b[:, j:j + 1], adast[:, k, j * 128:(j + 1) * 128], cact[:, k:k + 1],
                         start=(k == 0), stop=(k == 7))
            P.tt(mcol[:, g * NJ:(g + 1) * NJ], mcol[:, g * NJ:(g + 1) * NJ], b[:, 0:NJ], ALU.add)
        P.copy(cl[:, C_SH1:C_SH1 + 8], mcol[:, 0:8])
        P.copy(cl[:, C_GT1:C_GT1 + 8], mcol[:, 16:24])
        P.copy(cl[:, C_SH2:C_SH2 + 8], mcol[:, 24:32])
        P.copy(cl[:, C_GT2:C_GT2 + 8], mcol[:, 40:48])
        P.stt(cl[:, C_G1:C_G1 + 8], mcol[:, 8:16], 1.0, cols2[l][:, 8:16], ALU.add, ALU.mult)
        P.stt(cl[:, C_G2:C_G2 + 8], mcol[:, 32:40], 1.0, cl[:, C_G2:C_G2 + 8], ALU.add, ALU.mult)


    eps_t = sb("eps_t", (128, 3))
    P.memset(eps_t[:, 2:3], 1e-24)
    P.memset(eps_t[:, 0:1], EPS_RMS)
    P.memset(eps_t[:, 1:2], EPS_GN)

    def squares():
        for k in range(8):
            if k % 2 == 0:
                P.act(sqb[:, k, :], xs[:, k, :], AF.Square)
            else:
                P.tt(sqb[:, k, :], xs[:, k, :], xs[:, k, :], ALU.mult, eng=("dve" if k % 4 == 1 else "pool"))

    def rmsnorm_mod(gcol, shcol, cl):
        squares()
        b = bank()
        for k in range(8):
            P.mm(b[:, 0:T], onesmean[:, :], sqb[:, k, :], start=(k == 0), stop=(k == 7))
        P.act(rstd[:, :], b[:, 0:T], AF.Ln, bias=eps_t[:, 0:1])
        P.act(rstd[:, :], rstd[:, :], AF.Exp, scale=-0.5)
        for k in range(8):
            sc = fsc[k % 2]
            P.stt(sc[:, :], xs[:, k, :], cl[:, gcol + k:gcol + k + 1], rstd[:, :], ALU.mult, ALU.mult)
            P.act(hb[:, k, :], sc[:, :], AF.Identity, bias=cl[:, shcol + k:shcol + k + 1])

    def proj_chunk(slot, c0, ncol, nk=8, kstride=512, rhs=None):
        b = bank()
        sv = slot[:, 0:nk * kstride].rearrange("p (k n) -> p k n", n=kstride)
        for k in range(nk):
            P.mm(b[0:ncol, 0:T], sv[:, k, c0:c0 + ncol], (hb if rhs is None else rhs)[:, k, :],
                 start=(k == 0), stop=(k == nk - 1))
        return b

    pr_ctr = [0]

    def shift_chunk(b, nrow, mucol, cidx, cl, l, first_tile, out_ap=None, post=None):
        pr = praw[pr_ctr[0] % 3]
        pr_ctr[0] += 1
        P.copy(pr[0:nrow, 0:1], carry[l][0:nrow, cidx:cidx + 1], eng="pool")
        P.copy(pr[0:nrow, 1:T + 1], b[0:nrow, 0:T], eng="act")
        P.copy(carry[l][0:nrow, cidx:cidx + 1], pr[0:nrow, T:T + 1], eng="pool")
        d = fsc[pr_ctr[0] % 2]
        P.tt(d[0:nrow, :], pr[0:nrow, 0:T], pr[0:nrow, 1:T + 1], ALU.subtract)
        P.stt(out_ap, d[0:nrow, :], cl[0:nrow, mucol:mucol + 1], pr[0:nrow, 1:T + 1], ALU.mult, ALU.add)

    n_out = [0]
    for ti in range(NT):
        t0 = ti * T
        for tb in range(NB):
            P.dma(xstage[:, tb, :], x_d[t0 + tb * 128:t0 + (tb + 1) * 128, :], sem_x[tb])
        for k in range(8):
            b = bank()
            for tb in range(NB):
                P.tr(b[:, tb * 128:(tb + 1) * 128], xstage[:, tb, k * 128:(k + 1) * 128], identf[:, :],
                     inc=(tb == NB - 1))
            P.copy(xs[:, k, :], b[:, 0:T], eng=("act" if k % 2 else "dve"))

        def layer_body(l):
            done0 = wstate['cur']
            nunits = len(layer_units(l))
            def skip_rest():
                while wstate['cur'] < done0 + nunits:
                    wdone()
            cl = cols[l]
            c2 = cols2[l]
            if ti == 0:
                compute_mod(l)
            load_misc(l)
            rmsnorm_mod(C_G1, C_SH1, cl)
            for nm, dstbuf, cbase in (("r", rsh, 0), ("k", ksh, 4), ("v", vsh, 8)):
                slot = wget(l, nm)
                for j in range(4):
                    b = proj_chunk(slot, j * 128, 128)
                    shift_chunk(b, 128, C_MU + cbase + j, cbase + j, cl, l, ti == 0, out_ap=dstbuf[:, j, :])
                wdone()
            slot = wget(l, "lora")
            b = proj_chunk(slot, 0, 128)
            shift_chunk(b, 128, C_MU + 12, 12, cl, l, ti == 0, out_ap=rs[0][:, :])
            P.act(wab[0:64, :], rs[0][0:64, :], AF.Tanh)
            P.copy(wab[64:128, :], rs[0][64:128, :], eng="dve")
            b = proj_chunk(slot, 128, 128)
            shift_chunk(b, 128, C_MU + 13, 13, cl, l, ti == 0, out_ap=rs[1][:, :])
            P.act(glb[:, 0, :], rs[1][:, :], AF.Sigmoid)
            b = proj_chunk(slot, 256, 32)
            shift_chunk(b, 32, C_MU + 14, 14, cl, l, ti == 0, out_ap=rs[2][0:32, :])
            P.act(glb[0:32, 1, :], rs[2][0:32, :], AF.Sigmoid)
            wdone()
            if l >= 1:
                slot = wget(l, "vl")
                b = proj_chunk(slot, 0, 32, kstride=32)
                shift_chunk(b, 32, C_MUV, 15, cl, l, ti == 0, out_ap=rs[3][0:32, :])
                P.copy(vlb[0:32, :], rs[3][0:32, :], eng="dve")
                wdone()
            if STAGE < 4:
                skip_rest()
                return
            slot = wget(l, "pool")
            for g in range(4):
                b = proj_chunk(slot, g * 128, 128)
                P.copy(zext[:, g, 0:16], zcarry[l][:, g, :], eng="pool")
                P.copy(zext[:, g, 16:16 + T], b[:, 0:T], eng="act")
                P.copy(zcarry[l][:, g, :], zext[:, g, T:T + 16], eng="pool")
            wdone()
            for g, w in enumerate(POOL_W):
                src = zext[:, g, :]
                sh = 1
                n = 0
                W_ = T + 16
                while sh < w:
                    dst = pscr[n % 2]
                    P.tt(dst[:, sh:W_], src[:, sh:W_], src[:, 0:W_ - sh], ALU.add, eng="pool")
                    src = dst
                    sh *= 2
                    n += 1
                P.stt(pooled[:, g, :], src[:, 16:16 + T], 1.0 / w, zext[:, g, 16:16 + T], ALU.mult, ALU.subtract)
                if ti == 0:
                    P.tt(rs[4][:, 0:16], src[:, 16:32], invc0[:, g, :], ALU.mult)
                    P.tt(pooled[:, g, 0:16], rs[4][:, 0:16], zext[:, g, 16:32], ALU.subtract)
            pw = misc[:, 5, :].rearrange("p (g n) -> p g n", n=128)
            for g in range(4):
                b = bank()
                P.mm(b[:, 0:T], pw[:, g, :], pooled[:, g, :])
                P.act(ybb[:, g, :], b[:, 0:T], AF.Identity, scale=c2[:, g:g + 1])

            if STAGE < 5:
                skip_rest()
                return
            for cp in range(2):
                cs = (2 * cp, 2 * cp + 1)
                RS = {cs[0]: rs, cs[1]: rsB}
                FM = {cs[0]: fmtmp, cs[1]: fmtmpB}

                def col(base, c):
                    return cl[:, base + c:base + c + 1]
                bsw, bav, bvg, b4 = {}, {}, {}, {}
                for c in cs:
                    bsw[c] = bank()
                    P.mm(bsw[c][:, 0:T], misc[0:64, 0, c * 128:(c + 1) * 128], wab[0:64, :])
                for c in cs:
                    bav[c] = bank()
                    P.mm(bav[c][:, 0:T], misc[64:128, 1, c * 128:(c + 1) * 128], wab[64:128, :])
                if l >= 1:
                    for c in cs:
                        bvg[c] = bank()
                        P.mm(bvg[c][:, 0:T], misc[0:32, 2, c * 128:(c + 1) * 128], vlb[0:32, :])
                for c in cs:
                    sw, cum, cumx, Ep, Em, Epv, kkn, av, mm_, bb = RS[c]
                    P.act(sw[:, :], bsw[c][:, 0:T], AF.Sigmoid, bias=col(C_W0, c))
                for c in cs:
                    sw, cum, cumx, Ep, Em, Epv, kkn, av, mm_, bb = RS[c]
                    P.act(av[:, :], bav[c][:, 0:T], AF.Sigmoid, bias=col(C_A0, c))
                if l >= 1:
                    for c in cs:
                        sw, cum, cumx, Ep, Em, Epv, kkn, av, mm_, bb = RS[c]
                        P.act(mm_[:, :], bvg[c][:, 0:T], AF.Sigmoid, bias=col(C_V0, c))
                for c in cs:
                    P.act(FM[c][0][:, :], ksh[:, c, :], AF.Square, scale=col(C_KK, c))
                for c in cs:
                    sw, cum, cumx, Ep, Em, Epv, kkn, av, mm_, bb = RS[c]
                    P.scan(cum[:, :], scanmask[:, :], sw[:, :], 0.0, ALU.mult, ALU.add)
                for c in cs:
                    sw, cum, cumx, Ep, Em, Epv, kkn, av, mm_, bb = RS[c]
                    P.tt(cumx[:, :], cum[:, :], sw[:, :], ALU.subtract, eng="pool")
                for c in cs:
                    b4[c] = bank()
                    P.mm(b4[c][:, 0:T], onesblk[:, :], FM[c][0][:, :])
                for c in cs:
                    sw, cum, cumx, Ep, Em, Epv, kkn, av, mm_, bb = RS[c]
                    if l == 0:
                        P.copy(vfirst[:, c, :], vsh[:, c, :], eng="pool")
                    else:
                        P.tt(bb[:, :], vfirst[:, c, :], vsh[:, c, :], ALU.subtract, eng="pool")
                        P.tt(bb[:, :], bb[:, :], mm_[:, :], ALU.mult)
                        P.tt(vsh[:, c, :], vsh[:, c, :], bb[:, :], ALU.add, eng="pool")
                for c in cs:
                    sw, cum, cumx, Ep, Em, Epv, kkn, av, mm_, bb = RS[c]
                    P.act(Ep[:, :], cum[:, :], AF.Exp, scale=KDEC)
                    P.act(Em[:, :], cum[:, :], AF.Exp, scale=-KDEC)
                    P.act(Epv[:, :], cumx[:, :], AF.Exp, scale=KDEC)
                for c in cs:
                    sw, cum, cumx, Ep, Em, Epv, kkn, av, mm_, bb = RS[c]
                    P.act(kkn[:, :], b4[c][:, 0:T], AF.Ln, bias=eps_t[:, 2:3])
                    P.act(kkn[:, :], kkn[:, :], AF.Exp, scale=-0.5)
                slot_ga = wget(l, f"ga{cp}")
                slot_gb = wget(l, f"gb{cp}")
                for jj in range(4):
                    jo = cp * 4 + jj
                    bga = proj_chunk(slot_ga, jj * 128, 128)
                    bgb = proj_chunk(slot_gb, jj * 128, 128)
                    P.act(sgab[:, 0, jo, :], bga[:, 0:T], AF.Sigmoid)
                    P.act(sgab[:, 1, jo, :], bgb[:, 0:T], AF.Sigmoid)
                wdone()
                wdone()
                for c in cs:
                    sw, cum, cumx, Ep, Em, Epv, kkn, av, mm_, bb = RS[c]
                    for j in range(NB):
                        P.copy(pcs[:, j, c:c + 1], Ep[:, (j + 1) * CH - 1:(j + 1) * CH], eng="pool")
                for c in cs:
                    sw, cum, cumx, Ep, Em, Epv, kkn, av, mm_, bb = RS[c]
                    P.stt(kkn[:, :], ksh[:, c, :], col(C_KK, c), kkn[:, :], ALU.mult, ALU.mult)
                for c in cs:
                    sw, cum, cumx, Ep, Em, Epv, kkn, av, mm_, bb = RS[c]
                    P.ts(mm_[:, :], av[:, :], col(C_KA, c), c2[:, 4 + c:5 + c], ALU.mult, ALU.add)
                    P.tt(mm_[:, :], mm_[:, :], ksh[:, c, :], ALU.mult)
                    P.stt(rkb[:, c, :], rsh[:, c, :], col(C_RK, c), mm_[:, :], ALU.mult, ALU.mult)
                for c in cs:
                    sw, cum, cumx, Ep, Em, Epv, kkn, av, mm_, bb = RS[c]
                    P.tt(bb[:, :], av[:, :], kkn[:, :], ALU.mult, eng="pool")
                for c in cs:
                    sw, cum, cumx, Ep, Em, Epv, kkn, av, mm_, bb = RS[c]
                    P.tt(rt_b[:, c, :], rsh[:, c, :], Ep[:, :], ALU.mult)
                    P.tt(kt_b[:, c, :], mm_[:, :], Em[:, :], ALU.mult)
                    P.stt(at_b[:, c, :], kkn[:, :], -1.0, Epv[:, :], ALU.mult, ALU.mult)
                for c in cs:
                    sw, cum, cumx, Ep, Em, Epv, kkn, av, mm_, bb = RS[c]
                    P.tt(bt_b[:, c, :], bb[:, :], Em[:, :], ALU.mult, eng="pool")
                for c in cs:
                    sw, cum, cumx, Ep, Em, Epv, kkn, av, mm_, bb = RS[c]
                    for j in range(NB):
                        js = slice(j * CH, (j + 1) * CH)
                        P.ts(cumx[:, js], Em[:, js], pcs[:, j, c:c + 1], None, ALU.mult)
                    P.tt(FM[c][0][:, :], mm_[:, :], cumx[:, :], ALU.mult)
                for c in cs:
                    sw, cum, cumx, Ep, Em, Epv, kkn, av, mm_, bb = RS[c]
                    P.tt(FM[c][1][:, :], bb[:, :], cumx[:, :], ALU.mult, eng="pool")
                    P.copy(FM[c][2][:, :], vsh[:, c, :], eng="pool")
                for c in cs:
                    for srcb, dstb in ((at_b[:, c, :], atm), (FM[c][0][:, :], kendtm), (FM[c][1][:, :], bendtm),
                                       (FM[c][2][:, :], vtm)):
                        bt = bank_bf(bank())
                        for j in range(NB):
                            P.tr(bt[:, j * 128:(j + 1) * 128], srcb[:, j * CH:(j + 1) * CH], identb[:, :],
                                 inc=(j == NB - 1))
                        P.copy(dstb[:, :, c * 128:(c + 1) * 128],
                               bt[:, 0:NB * 128].rearrange("p (j n) -> p j n", n=128), eng="act")

            if STAGE < 6:
                skip_rest()
                return
            H = Hs[l]
            Hbf = Hb[l]
            P.copy(Hbf[:, :, :], H[:, :, :], eng="pool")
            for j in range(NB):
                js = slice(j * CH, (j + 1) * CH)
                X, XT = Xf, XTf
                for hg in range(2):
                    bX, bXT, bAK, bRB, bRK = bank(), bank(), bank(), bank(), bank()
                    for hh in range(4):
                        h = 2 * hh + hg
                        c, hp = h // 2, (h % 2) * 64
                        A_ = at_b[hp:hp + 64, c, js]
                        B_ = bt_b[hp:hp + 64, c, js]
                        K_ = kt_b[hp:hp + 64, c, js]
                        R_ = rt_b[hp:hp + 64, c, js]
                        o = slice(hh * 128, (hh + 1) * 128)
                        P.mm(bX[:, o], A_, B_)
                        P.mm(bXT[:, o], B_, A_)
                        P.mm(bAK[:, o], K_, A_)
                        P.mm(bRB[:, o], B_, R_)
                        P.mm(bRK[:, o], K_, R_)
                    hsl = slice(hg * 4, hg * 4 + 4)
                    P.tt(X[:, hsl, :], bX[:, :].rearrange("p (h n) -> p h n", n=128), m_lower[:, :, :], ALU.mult)
                    P.tt(XT[:, hsl, :], bXT[:, :].rearrange("p (h n) -> p h n", n=128), m_strict[:, :, :], ALU.mult)
                    P.tt(Xb[0][:, hsl, :], bX[:, :].rearrange("p (h n) -> p h n", n=128), bc4(mD_l), ALU.mult)
                    P.tt(XTb[0][:, hsl, :], bXT[:, :].rearrange("p (h n) -> p h n", n=128), bc4(mD_u), ALU.mult)
                    P.tt(akT[:, hsl, :], bAK[:, :].rearrange("p (h n) -> p h n", n=128), m_strict[:, :, :], ALU.mult)
                    P.tt(rbT[:, hsl, :], bRB[:, :].rearrange("p (h n) -> p h n", n=128), m_incl[:, :, :], ALU.mult)
                    P.tt(rkT[:, hsl, :], bRK[:, :].rearrange("p (h n) -> p h n", n=128), m_incl[:, :, :], ALU.mult)
                idb8 = identb[:, :].unsqueeze(1).broadcast_to([128, 8, 128])
                P.tt(Tb[0][:, :, :], Xb[0][:, :, :], idb8, ALU.add, eng="pool")
                P.tt(TTb2[0][:, :, :], XTb[0][:, :, :], idb8, ALU.add, eng="pool")
                tcur = 0

                def v4(bk_):
                    return bk_[:, :].rearrange("p (h n) -> p h n", n=128)

                def acc_group(bk_, o, lhs, rhs_t, base=None):
                    P.mm(bk_[:, o], identb[:, :], rhs_t if base is None else base, start=True, stop=False)
                    P.mm(bk_[:, o], lhs, rhs_t, start=False, stop=True)
                for lev in (1, 2):
                    Xc, XTc, Xn, XTn = Xb[(lev - 1) % 2], XTb[(lev - 1) % 2], Xb[lev % 2], XTb[lev % 2]
                    Tc, TTc, Tn, TTn = Tb[tcur], TTb2[tcur], Tb[1 - tcur], TTb2[1 - tcur]
                    for hg in range(2):
                        hsl = slice(hg * 4, hg * 4 + 4)
                        bX = bank()
                        for hh in range(4):
                            h = hg * 4 + hh
                            P.mm(bX[:, hh * 128:(hh + 1) * 128], XTc[:, h, :], Xc[:, h, :])
                        P.copy(Xn[:, hsl, :], v4(bX), eng="act")
                        bXT = bank()
                        for hh in range(4):
                            h = hg * 4 + hh
                            P.mm(bXT[:, hh * 128:(hh + 1) * 128], Xc[:, h, :], XTc[:, h, :])
                        P.copy(XTn[:, hsl, :], v4(bXT), eng="dve")
                        bT = bank()
                        for hh in range(4):
                            h = hg * 4 + hh
                            acc_group(bT, slice(hh * 128, (hh + 1) * 128), XTn[:, h, :], Tc[:, h, :])
                        P.copy(Tn[:, hsl, :], v4(bT), eng="act")
                        bTT = bank()
                        for hh in range(4):
                            h = hg * 4 + hh
                            acc_group(bTT, slice(hh * 128, (hh + 1) * 128), Xn[:, h, :], TTc[:, h, :])
                        P.copy(TTn[:, hsl, :], v4(bTT), eng="dve")
                    tcur = 1 - tcur
                for b_ in HB:
                    Tc, TTc, Tn, TTn = Tb[tcur], TTb2[tcur], Tb[1 - tcur], TTb2[1 - tcur]
                    last = (b_ == 64)
                    for hg in range(2):
                        hsl = slice(hg * 4, hg * 4 + 4)
                        if not last:
                            bQ = bank()
                            for hh in range(4):
                                h = hg * 4 + hh
                                P.mm(bQ[:, hh * 128:(hh + 1) * 128], XTf[:, h, :], Tc[:, h, :])
                            P.tt(Qb[:, hsl, :], v4(bQ), bc4(ML[b_]), ALU.mult)
                        bP = bank()
                        for hh in range(4):
                            h = hg * 4 + hh
                            P.mm(bP[:, hh * 128:(hh + 1) * 128], Xf[:, h, :], TTc[:, h, :])
                        P.tt(Pb[:, hsl, :], v4(bP), bc4(MU[b_]), ALU.mult)
                        if not last:
                            bT = bank()
                            for hh in range(4):
                                h = hg * 4 + hh
                                acc_group(bT, slice(hh * 128, (hh + 1) * 128), TTc[:, h, :], Qb[:, h, :], base=Tc[:, h, :])
                            P.copy(Tn[:, hsl, :], v4(bT), eng="act")
                        bTT = bank()
                        for hh in range(4):
                            h = hg * 4 + hh
                            acc_group(bTT, slice(hh * 128, (hh + 1) * 128), Tc[:, h, :], Pb[:, h, :], base=TTc[:, h, :])
                        P.copy(TTn[:, hsl, :], v4(bTT), eng=("act" if last and hg else "dve"))
                    tcur = 1 - tcur
                TT = TTb2[tcur]
                bA = bank()
                for h in range(8):
                    P.mm(bA[:, h * 64:(h + 1) * 64], akT[:, sidx(h), :], vtm[:, j, h * 64:(h + 1) * 64])
                P.copy(akv[:, :], bA[:, :], eng="act")
                bV = bank()
                for h in range(8):
                    P.mm(bV[:, h * 64:(h + 1) * 64], TT[:, sidx(h), :], akv[:, h * 64:(h + 1) * 64])
                P.copy(vhat[:, :], bV[:, :], eng="dve")
                if SUB < 4:
                    continue
                for hg in range(2):
                    bH = bank()
                    for hh in range(4):
                        c = hh
                        P.mm(bH[:, hh * 128:(hh + 1) * 128], atm[:, j, c * 128:(c + 1) * 128], TT[:, hg * 4 + hh, :])
                    rows = slice(hg * 64, hg * 64 + 64)
                    P.copy(ahT[rows, :, :], bH[rows, :].rearrange("p (c n) -> p c n", n=128),
                           eng=("act" if hg else "dve"))
                if STAGE < 7:
                    continue
                bU = bank()
                for c in range(4):
                    P.mm(bU[:, c * 128:(c + 1) * 128], ahT[:, c, :], Hbf[:, c, :])
                P.tt(ub[:, :], bU[:, :], vhat[:, :], ALU.add)
                bY = bank()
                for c in range(4):
                    P.mm(bY[:, c * 128:(c + 1) * 128], rt_b[:, c, js], Hbf[:, c, :], start=True, stop=False)
                    for h in (2 * c, 2 * c + 1):
                        o = slice(h * 64, (h + 1) * 64)
                        P.mm(bY[:, o], rbT[:, sidx(h), :], ub[:, o], start=False, stop=False)
                        P.mm(bY[:, o], rkT[:, sidx(h), :], vtm[:, j, o], start=False, stop=(h == 2 * c + 1))
                bS = bank()
                for c in range(4):
                    o = slice(c * 128, (c + 1) * 128)
                    P.mm(bS[:, o], kendtm[:, j, o], vtm[:, j, o], start=True, stop=False)
                    P.mm(bS[:, o], bendtm[:, j, o], ub[:, o], start=False, stop=True)
                P.tt(hx[:, :, :], bS[:, :].rearrange("p (c n) -> p c n", n=128), m_bd[:, :, :], ALU.mult)
                P.tt(H[:, :, :], H[:, :, :], pcs[:, j, :].unsqueeze(2).broadcast_to([128, 4, 128]), ALU.mult,
                     eng="pool")
                P.tt(H[:, :, :], H[:, :, :], hx[:, :, :], ALU.add, eng="pool")
                P.copy(Hbf[:, :, :], H[:, :, :], eng="pool")
                P.copy(ysb[:, :], bY[:, :], eng="act")
                P.act(ysq[:, :], bY[:, :], AF.Square)
                y3 = ysb[:, :].rearrange("p (h n) -> p h n", n=64)
                q3 = ysq[:, :].rearrange("p (h n) -> p h n", n=64)
                P.reduce(gst[:, :, 0], y3, ALU.add)
                P.reduce(gst[:, :, 1], q3, ALU.add)
                P.ts(gst[:, :, 0], gst[:, :, 0], 1.0 / 64, None, ALU.mult)
                P.tt(gst[:, :, 2], gst[:, :, 0], gst[:, :, 0], ALU.mult)
                P.stt(gst[:, :, 1], gst[:, :, 1], 1.0 / 64, gst[:, :, 2], ALU.mult, ALU.subtract)
                P.act(gst[:, :, 3], gst[:, :, 1], AF.Ln, bias=eps_t[:, 1:2])
                P.act(gst[:, :, 3], gst[:, :, 3], AF.Exp, scale=-0.5)
                P.tt(y3, y3, gst[:, :, 0].unsqueeze(2).broadcast_to([128, 8, 64]), ALU.subtract)
                P.tt(ynb[:, :].rearrange("p (h n) -> p h n", n=64), y3,
                     gst[:, :, 3].unsqueeze(2).broadcast_to([128, 8, 64]), ALU.mult)
                bt = bank_bf(bank())
                for c in range(4):
                    P.tr(bt[:, c * 128:(c + 1) * 128], ynb[:, c * 128:(c + 1) * 128], identb[:, :], inc=(c == 3))
                for c in range(4):
                    P.act(yab[:, c, js], bt[:, c * 128:(c + 1) * 128], AF.Identity,
                          bias=cl[:, C_LB + c:C_LB + c + 1], scale=cl[:, C_LG + c:C_LG + c + 1])
            if STAGE < 8:
                skip_rest()
                return
            for c in range(4):
                bb_ = bank()
                P.mm(bb_[:, 0:T], onesblk[:, :], rkb[:, c, :])
                bg = bank()
                P.mm(bg[:, 0:T], misc[:, 3, c * 128:(c + 1) * 128], glb[:, 0, :], start=True, stop=False)
                P.mm(bg[:, 0:T], misc[0:32, 4, c * 128:(c + 1) * 128], glb[0:32, 1, :], start=False, stop=True)
                s0 = ya_s[0]
                P.tt(s0[:, :], bb_[:, 0:T], vsh[:, c, :], ALU.mult)
                P.tt(s0[:, :], s0[:, :], yab[:, c, :], ALU.add, eng="pool")
                P.tt(yab[:, c, :], s0[:, :], bg[:, 0:T], ALU.mult)
            slot_pa = wget(l, "pa")
            slot_pb = wget(l, "pb")
            for jo in range(8):
                bpa = proj_chunk(slot_pa, jo * 128, 128, nk=4, kstride=1024, rhs=yab)
                bpb = proj_chunk(slot_pb, jo * 128, 128, nk=4, kstride=1024, rhs=ybb)
                P.tt(mx[0][:, :], sgab[:, 0, jo, :], bpa[:, 0:T], ALU.mult)
                P.tt(mx[1][:, :], sgab[:, 1, jo, :], bpb[:, 0:T], ALU.mult)
                P.tt(mixed[:, jo, :], mx[0][:, :], mx[1][:, :], ALU.add, eng="pool")
            wdone()
            wdone()
            for hh in range(2):
                slot = wget(l, f"wo{hh}")
                for jj in range(4):
                    jo = hh * 4 + jj
                    b = proj_chunk(slot, jj * 128, 128, rhs=mixed)
                    P.stt(xs[:, jo, :], b[:, 0:T], cl[:, C_GT1 + jo:C_GT1 + jo + 1], xs[:, jo, :], ALU.mult, ALU.add)
                wdone()

            if STAGE < 9:
                skip_rest()
                return
            rmsnorm_mod(C_G2, C_SH2, cl)
            for f0 in range(0, NFF, 4):
                nf = min(4, NFF - f0)
                slot_g = wget(l, f"fg{f0}")
                slot_u = wget(l, f"fu{f0}")
                for ff in range(nf):
                    bg = proj_chunk(slot_g, ff * 128, 128)
                    bu = proj_chunk(slot_u, ff * 128, 128)
                    s = fsc[ff % 2]
                    P.act(s[:, :], bg[:, 0:T], AF.Silu)
                    P.tt(actb[:, f0 + ff, :], s[:, :], bu[:, 0:T], ALU.mult)
                wdone()
                wdone()
            for hh in range(2):
                accs = [bank() for _ in range(4)]
                for k0 in range(0, NFF, 8):
                    nk = min(8, NFF - k0)
                    slot = wget(l, f"fd{hh}_{k0}")
                    sv = slot[:, 0:nk * 512].rearrange("p (k n) -> p k n", n=512)
                    for jj in range(4):
                        for k in range(nk):
                            P.mm(accs[jj][:, 0:T], sv[:, k, jj * 128:(jj + 1) * 128], actb[:, k0 + k, :],
                                 start=(k0 + k == 0), stop=(k0 + k == NFF - 1))
                    wdone()
                for jj in range(4):
                    jo = hh * 4 + jj
                    P.stt(xs[:, jo, :], accs[jj][:, 0:T], cl[:, C_GT2 + jo:C_GT2 + jo + 1], xs[:, jo, :],
                          ALU.mult, ALU.add)


        for l in range(L if STAGE >= 2 else 0):
            layer_body(l)
        squares()
        b = bank()
        for k in range(8):
            P.mm(b[:, 0:T], onesmean[:, :], sqb[:, k, :], start=(k == 0), stop=(k == 7))
        P.act(rstd[:, :], b[:, 0:T], AF.Ln, bias=eps_t[:, 0:1])
        P.act(rstd[:, :], rstd[:, :], AF.Exp, scale=-0.5)
        for k in range(8):
            P.stt(xs[:, k, :], xs[:, k, :], fgc[:, k:k + 1], rstd[:, :], ALU.mult, ALU.mult)
        for tb in range(NB):
            for kh in range(2):
                b = bank()
                for kk in range(4):
                    k = kh * 4 + kk
                    P.tr(b[:, kk * 128:(kk + 1) * 128], xs[:, k, tb * 128:(tb + 1) * 128], identf[:, :],
                         inc=(kk == 3))
                P.copy(ostage[:, tb, kh * 512:(kh + 1) * 512], b[:, :], eng=("act" if kh else "dve"))
            P.dma(out_d[t0 + tb * 128:t0 + (tb + 1) * 128, :], ostage[:, tb, :], sem_o[tb])
            n_out[0] += 1

    assert STAGE < 2 or wstate["cur"] == len(units)
    if debug_taps:
        dbg = nc.dram_tensor("dbg", [128, 96 + 10 * T + 512 * 3 + 8 * T], F32, kind="ExternalOutput").ap()
        sem_dbg = P.newsem("d_dbg")
        reqs = [(dbg[:, 0:96], cols[0][:, :], {})]
        for i in range(10):
            reqs.append((dbg[:, 96 + i * T:96 + (i + 1) * T], rs[i][:, :], {}))
        o = 96 + 10 * T
        reqs.append((dbg[:, o:o + 512], ysb[:, :], {}))
        reqs.append((dbg[:, o + 512:o + 1024], vhat[:, :], {}))
        reqs.append((dbg[:, o + 1024:o + 1536], Hs[0][:, :, :].rearrange("p a b -> p (a b)"), {}))
        o += 1536
        reqs.append((dbg[:, o:o + 8 * T].rearrange("p (k t) -> p k t", k=8), xs[:, :, :], {}))
        P.dma_group(sem_dbg, reqs)
        final_extra = [(sem_dbg, P.cnt[sem_dbg])]
    else:
        final_extra = []
    P.emit(final_waits=[(sem_o[tb], P.cnt[sem_o[tb]]) for tb in range(NB)] + final_extra)
    es.close()
    return nc, P


S_FULL, L_FULL = 4096, 4
_CACHE = {}


def run_cores(inputs, S, L, T=256):
    key = (S, L, T)
    if key not in _CACHE:
        _CACHE[key] = build_program(S, L, T)[0]
    nc = _CACHE[key]
    B = inputs["x"].shape[0]
    shared = {}
    for k, v in inputs.items():
        if k in ("x", "c"):
            continue
        a = np.ascontiguousarray(np.asarray(v, dtype=np.float32))
        if k == "r_k":
            a = a.reshape(a.shape[0], DR)
        elif k == "pool_w":
            a = a.reshape(a.shape[0], 512, 128)
        elif k == "final_g":
            a = a.reshape(1, D)
        shared[k] = a
    if L == 1:
        for k, shp in (("v_down", (1, D, 32)), ("mu_v", (1, 32)), ("v0", (1, DR)), ("v_up", (1, 32, DR))):
            if shared[k].shape[0] == 0:
                shared[k] = np.zeros(shp, np.float32)
    x = np.asarray(inputs["x"], dtype=np.float32)
    c = np.asarray(inputs["c"], dtype=np.float32)
    in_maps = []
    for b in range(B):
        m = dict(shared)
        m["x"] = np.ascontiguousarray(x[b])
        m["c"] = np.ascontiguousarray(c[b:b + 1])
        in_maps.append(m)
    res = run_bass_kernel_spmd(nc, in_maps, core_ids=list(range(B)))
    return np.stack([np.asarray(r["out"]) for r in res.results], axis=0).astype(np.float32)


def kernel(**inputs):
    return run_cores(inputs, S_FULL, L_FULL)
```

```python
import os
import sys
import numpy as np
from contextlib import ExitStack
import concourse.bass as bass
import concourse.mybir as mybir
from concourse.bass_utils import run_bass_kernel_spmd

F32 = mybir.dt.float32
BF16 = mybir.dt.bfloat16
ALU = mybir.AluOpType
AF = mybir.ActivationFunctionType
AX = mybir.AxisListType

D = 1024
DR = 512
NH = 8
HS = 64
DFF = 2816
NFF = DFF // 128
N_SHIFT = 1824
N_IN = 4384
NINX = 4416
EPS_RMS = 1e-6
EPS_GN = 64e-5
CH = 128
KDEC = -0.6065306597126334
POOL_W = (2, 4, 8, 16)


class Prog:
    ENG = ("pe", "act", "dve", "pool", "sp")

    def __init__(self, nc, es):
        self.nc = nc
        self.es = es
        self.ops = {e: [] for e in self.ENG}
        self.cnt = {}
        self.sems = {}
        self.waited = {e: {} for e in self.ENG}
        self.recs = {}
        for e in ("pe", "act", "dve", "pool"):
            self.newsem("c_" + e)
        self.n_ops = 0

    def newsem(self, name):
        self.sems[name] = self.es.enter_context(self.nc.semaphore(name))
        self.cnt[name] = 0
        return name

    @staticmethod
    def region(ap):
        t = ap.tensor
        cls = type(t).__name__
        if not ("SB" in cls or "PSum" in cls):
            return None
        a = ap.ap
        off = ap.offset
        pst = a[0][0]
        if pst == 0:
            pst = 1 << 30
        p0 = off // pst
        f0 = off % pst
        p1 = p0 + a[0][1]
        ext = 1
        for s, c in a[1:]:
            ext += (c - 1) * abs(s)
        esz = 2 if ap.dtype == BF16 else 4
        if "PSum" in cls:
            return (t.name, 0, 128, 0, 1 << 20)
        return (t.name, p0, p1, f0 * esz, (f0 + ext) * esz)

    def op(self, eng, fn, reads, writes, inc=True, dma_sem=None, extra_waits=(), group_final=None):
        deps = {}

        def add(sem, val):
            if val > deps.get(sem, 0):
                deps[sem] = val

        rregs = [r for r in (self.region(a) for a in reads) if r is not None]
        wregs = [r for r in (self.region(a) for a in writes) if r is not None]
        for (nm, p0, p1, f0, f1) in rregs:
            psum = f1 >= (1 << 20)
            for rec in self.recs.get(nm, ()):
                if (rec[4] == "w" or (psum and rec[7] != eng)) and rec[0] < p1 and p0 < rec[1] \
                        and rec[2] < f1 and f0 < rec[3]:
                    add(rec[5], rec[6])
        for (nm, p0, p1, f0, f1) in wregs:
            for rec in self.recs.get(nm, ()):
                if rec[0] < p1 and p0 < rec[1] and rec[2] < f1 and f0 < rec[3]:
                    add(rec[5], rec[6])
        for s, v in extra_waits:
            add(s, v)
        if eng == "pe":
            deps.pop("c_pe", None)
        if dma_sem is not None:
            self.cnt[dma_sem] += 16
            tok = (dma_sem, self.cnt[dma_sem] if group_final is None else group_final)
            incspec = (dma_sem, 16)
        else:
            s = "c_" + eng
            if inc:
                self.cnt[s] += 1
                tok = (s, self.cnt[s])
                incspec = (s, 1)
            else:
                tok = (s, self.cnt[s] + 1)
                incspec = None
        waits = []
        wd = self.waited[eng]
        for s, v in deps.items():
            if v > wd.get(s, 0):
                waits.append((s, v))
                wd[s] = v
        site = ""
        if os.environ.get("KDBG"):
            f = sys._getframe(1)
            while f is not None and f.f_code.co_name != "build_program" and f.f_back is not None:
                if f.f_back.f_code.co_name == "build_program" or f.f_code.co_name in ("layer_body", "rmsnorm_mod", "proj_chunk", "shift_chunk", "squares", "compute_mod"):
                    site += f"{f.f_code.co_name}:{f.f_lineno} "
                f = f.f_back
            if f is not None:
                site += f"bp:{f.f_lineno}"
        self.ops[eng].append((waits, fn, incspec, site))
        self.n_ops += 1
        for (nm, p0, p1, f0, f1) in wregs:
            lst = self.recs.setdefault(nm, [])
            lst[:] = [r for r in lst if not (p0 <= r[0] and r[1] <= p1 and f0 <= r[2] and r[3] <= f1)]
            lst.append((p0, p1, f0, f1, "w", tok[0], tok[1], eng))
        for (nm, p0, p1, f0, f1) in rregs:
            lst = self.recs.setdefault(nm, [])
            lst[:] = [r for r in lst if not (r[4] == "r" and r[7] == eng and p0 <= r[0] and r[1] <= p1
                                             and f0 <= r[2] and r[3] <= f1)]
            lst.append((p0, p1, f0, f1, "r", tok[0], tok[1], eng))
        return tok

    def mm(self, out, lhsT, rhs, start=True, stop=True):
        return self.op("pe", lambda e: e.matmul(out, lhsT, rhs, start=start, stop=stop),
                       [lhsT, rhs], [out], inc=stop)

    def tr(self, out, in_, ident, inc=True):
        return self.op("pe", lambda e: e.transpose(out, in_, ident), [in_, ident], [out], inc=inc)

    def act(self, out, in_, func, bias=None, scale=None, eng="act"):
        kw = {}
        rd = [in_]
        if bias is not None:
            kw["bias"] = bias
            if not isinstance(bias, (int, float)):
                rd.append(bias)
        if scale is not None:
            kw["scale"] = scale
            if not isinstance(scale, (int, float)):
                rd.append(scale)
        return self.op("act", lambda e: e.activation(out=out, in_=in_, func=func, **kw), rd, [out])

    def tt(self, out, a, b, op, eng="dve"):
        return self.op(eng, lambda e: e.tensor_tensor(out=out, in0=a, in1=b, op=op), [a, b], [out])

    def ts(self, out, a, s1, s2, op0, op1=None, eng="dve"):
        rd = [a] + [s for s in (s1, s2) if s is not None and not isinstance(s, (int, float))]
        if op1 is None:
            return self.op(eng, lambda e: e.tensor_scalar(out=out, in0=a, scalar1=s1, scalar2=None, op0=op0),
                           rd, [out])
        return self.op(eng, lambda e: e.tensor_scalar(out=out, in0=a, scalar1=s1, scalar2=s2, op0=op0, op1=op1),
                       rd, [out])

    def stt(self, out, a, s, b, op0, op1, eng="dve"):
        rd = [a, b] + ([] if isinstance(s, (int, float)) else [s])
        return self.op(eng, lambda e: e.scalar_tensor_tensor(out=out, in0=a, scalar=s, in1=b, op0=op0, op1=op1),
                       rd, [out])

    def copy(self, out, in_, eng="dve"):
        if eng == "act":
            return self.op("act", lambda e: e.copy(out=out, in_=in_), [in_], [out])
        return self.op(eng, lambda e: e.tensor_copy(out=out, in_=in_), [in_], [out])

    def memset(self, out, val, eng="dve"):
        return self.op(eng, lambda e: e.memset(out, val), [], [out])

    def scan(self, out, d0, d1, init, op0, op1):
        return self.op("dve", lambda e: e.tensor_tensor_scan(out=out, data0=d0, data1=d1, initial=init,
                                                             op0=op0, op1=op1), [d0, d1], [out])

    def reduce(self, out, in_, op, axis=AX.X, eng="dve"):
        return self.op(eng, lambda e: e.tensor_reduce(out=out, in_=in_, axis=axis, op=op), [in_], [out])

    def recip(self, out, in_):
        return self.op("dve", lambda e: e.reciprocal(out=out, in_=in_), [in_], [out])

    def dma(self, out, in_, sem, eng="sp", extra_waits=(), group_final=None, **kw):
        return self.op(eng, lambda e: e.dma_start(out=out, in_=in_, **kw), [in_], [out], dma_sem=sem,
                       extra_waits=extra_waits, group_final=group_final)

    def dma_group(self, sem, reqs, eng="sp", extra_waits=()):
        final = self.cnt[sem] + 16 * len(reqs)
        for (out, in_, kw) in reqs:
            self.dma(out, in_, sem, eng=eng, extra_waits=extra_waits, group_final=final, **kw)

    def emit(self, final_waits):
        nc = self.nc
        sems = self.sems

        def run(e, ops, tail=()):
            for waits, fn, incspec, site in ops:
                for s, v in waits:
                    e.wait_ge(sems[s], v)
                ins = fn(e)
                if site:
                    ins.annotate(site)
                if incspec is not None:
                    ins.then_inc(sems[incspec[0]], incspec[1])
            for s, v in tail:
                e.wait_ge(sems[s], v)

        with nc.Block() as block:
            @block.tensor
            def _(e):
                run(e, self.ops["pe"])

            @block.scalar
            def _(e):
                run(e, self.ops["act"])

            @block.vector
            def _(e):
                run(e, self.ops["dve"])

            @block.gpsimd
            def _(e):
                run(e, self.ops["pool"])

            @block.sync
            def _(e):
                run(e, self.ops["sp"], tail=final_waits)


def build_program(S, L, T=256, NW=4, debug_taps=False, STAGE=99):
    nc = bass.Bass("TRN2", target_bir_lowering=False)
    NT = S // T
    SUB = int(os.environ.get('KSUB', '99'))
    NB = T // 128
    es = ExitStack()
    P = Prog(nc, es)

    def din(name, shape):
        return nc.dram_tensor(name, list(shape), F32, kind="ExternalInput").ap()

    Lv = max(L - 1, 1)
    x_d = din("x", (S, D))
    c_d = din("c", (1, D))
    ada_w = din("ada_w", (L, D, 6 * D))
    ada_b = din("ada_b", (L, 6 * D))
    norm1_g = din("norm1_g", (L, D))
    w_in = din("w_in", (L, D, N_IN))
    v_down = din("v_down", (Lv, D, 32))
    mu_shift = din("mu_shift", (L, N_SHIFT))
    mu_v = din("mu_v", (Lv, 32))
    w0 = din("w0", (L, DR))
    w_up = din("w_up", (L, 64, DR))
    a0 = din("a0", (L, DR))
    a_up = din("a_up", (L, 64, DR))
    v0 = din("v0", (Lv, DR))
    v_up = din("v_up", (Lv, 32, DR))
    g_up = din("g_up", (L, 160, DR))
    k_k = din("k_k", (L, DR))
    k_a = din("k_a", (L, DR))
    r_k = din("r_k", (L, DR))
    lnx_g = din("lnx_g", (L, DR))
    lnx_b = din("lnx_b", (L, DR))
    pool_w = din("pool_w", (L, 4 * 128, 128))
    pool_scale = din("pool_scale", (L, DR))
    proj_a = din("proj_a", (L, DR, D))
    proj_b = din("proj_b", (L, DR, D))
    w_out = din("w_out", (L, D, D))
    norm2_g = din("norm2_g", (L, D))
    w_gu = din("w_gu", (L, D, 2 * DFF))
    w_down = din("w_down", (L, DFF, D))
    final_g = din("final_g", (1, D))
    out_d = nc.dram_tensor("out", [S, D], F32, kind="ExternalOutput").ap()

    def dscr(name, shape):
        return nc.dram_tensor(name, list(shape), BF16, kind="Internal").ap()

    win_b = dscr("win_b", (L, D, NINX))
    wup_b = dscr("wup_b", (L, 64, DR))
    aup_b = dscr("aup_b", (L, 64, DR))
    vup_b = dscr("vup_b", (Lv, 32, DR))
    gup_b = dscr("gup_b", (L, 160, DR))
    poolw_b = dscr("poolw_b", (L, 512, 128))
    pa_b = dscr("pa_b", (L, DR, D))
    pb_b = dscr("pb_b", (L, DR, D))
    wout_b = dscr("wout_b", (L, D, D))
    wgu_b = dscr("wgu_b", (L, D, 2 * DFF))
    wdn_b = dscr("wdn_b", (L, DFF, D))

    def sb(name, shape, dt=F32):
        return es.enter_context(nc.sbuf_tensor(name, list(shape), dt))

    def ps(name, shape, dt=F32):
        return es.enter_context(nc.psum_tensor(name, list(shape), dt))

    xs = sb("xs", (128, 8, T))
    actb = sb("actb", (128, NFF, T), BF16)
    sqb = actb
    hb = sb("hb", (128, 8, T), BF16)
    rstd = sb("rstd", (128, T))
    wring = [sb(f"wr{i}", (128, 4096), BF16) for i in range(NW)]
    misc = sb("misc", (128, 6, 512), BF16)
    praw = [sb(f"praw{i}", (128, T + 1)) for i in range(3)]
    rsh = sb("rsh", (128, 4, T))
    ksh = sb("ksh", (128, 4, T))
    vsh = sb("vsh", (128, 4, T))
    vfirst = sb("vfirst", (128, 4, T))
    wab = sb("wab", (128, T), BF16)
    glb = sb("glb", (128, 2, T), BF16)
    vlb = sb("vlb", (32, T), BF16)
    ZW = 4 * (T + 16)
    PW = T + 16
    parena = sb("parena", (128, max(ZW + 2 * PW + 2 * T, 8 * T)))
    zext = parena[:, 0:ZW].rearrange("p (g n) -> p g n", g=4)
    pscr = [parena[:, ZW + i * PW:ZW + (i + 1) * PW] for i in range(2)]
    pooled = parena[:, ZW + 2 * PW:ZW + 2 * PW + 2 * T].bitcast(BF16).rearrange("p (g n) -> p g n", g=4)
    sgab = parena[:, 0:8 * T].bitcast(BF16).rearrange("p (a j n) -> p a j n", a=2, j=8)
    ybb = sb("ybb", (128, 4, T), BF16)
    rs = [sb(f"rs{i}", (128, T)) for i in range(10)]
    rsB = [sb(f"rsB{i}", (128, T)) for i in range(10)]
    rt_b = sb("rt_b", (128, 4, T), BF16)
    kt_b = sb("kt_b", (128, 4, T), BF16)
    bt_b = sb("bt_b", (128, 4, T), BF16)
    at_b = sb("at_b", (128, 4, T), BF16)
    fmtmp = [sb(f"fmtmp{i}", (128, T), BF16) for i in range(3)]
    fmtmpB = [sb(f"fmtmpB{i}", (128, T), BF16) for i in range(3)]
    atm = sb("atm", (128, NB, 512), BF16)
    kendtm = sb("kendtm", (128, NB, 512), BF16)
    bendtm = sb("bendtm", (128, NB, 512), BF16)
    vtm = sb("vtm", (128, NB, 512), BF16)
    pcs = sb("pcs", (128, NB, 4))
    rkb = sb("rkb", (128, 4, T), BF16)
    Xf = sb("Xf", (128, 8, 128), BF16)
    XTf = sb("XTf", (128, 8, 128), BF16)
    Xb = [sb(f"Xb{i}", (128, 8, 128), BF16) for i in range(2)]
    XTb = [sb(f"XTb{i}", (128, 8, 128), BF16) for i in range(2)]
    Tb = [sb(f"Tb{i}", (128, 8, 128), BF16) for i in range(2)]
    TTb2 = [sb(f"TTb{i}", (128, 8, 128), BF16) for i in range(2)]
    Qb = sb("Qb", (128, 8, 128), BF16)
    Pb = sb("Pb", (128, 8, 128), BF16)
    akT = sb("akT", (128, 8, 128), BF16)
    rbT = sb("rbT", (128, 8, 128), BF16)
    rkT = sb("rkT", (128, 8, 128), BF16)
    akv = sb("akv", (128, 512), BF16)
    vhat = sb("vhat", (128, 512))
    ahT = sb("ahT", (128, 4, 128), BF16)
    ub = sb("ub", (128, 512), BF16)
    Hs = [sb(f"Hs{l}", (128, 4, 128)) for l in range(L)]
    Hb1 = sb("Hb1", (128, 4, 128), BF16)
    Hb = [Hb1 for l in range(L)]
    hx = sb("hx", (128, 4, 128))
    ysb = sb("ysb", (128, 512))
    ysq = sb("ysq", (128, 512))
    ynb = sb("ynb", (128, 512), BF16)
    gst = sb("gst", (128, 8, 4))
    yab = sb("yab", (128, 4, T), BF16)
    ya_s = rsB[4:6]
    mixed = sb("mixed", (128, 8, T), BF16)
    sg = rsB[0:2]
    mx = rsB[2:4]
    fsc = [sb(f"fsc{i}", (128, T)) for i in range(2)]
    xstage = actb[:, :, :].rearrange("p a b -> p (a b)").bitcast(F32)[:, 0:NB * D].rearrange("p (a b) -> p a b", b=D)
    ostage = xstage
    identb = sb("identb", (128, 128), BF16)
    identf = sb("identf", (128, 128))
    onesblk = sb("onesblk", (128, 128), BF16)
    onesmean = sb("onesmean", (128, 128), BF16)
    m_strict = sb("m_strict", (128, 4, 128), BF16)
    m_incl = sb("m_incl", (128, 4, 128), BF16)
    m_lower = sb("m_lower", (128, 4, 128), BF16)
    m_bd = sb("m_bd", (128, 4, 128), BF16)
    scanmask = sb("scanmask", (128, T))
    iota_p = sb("iota_p", (128, 1))
    iota_f = sb("iota_f", (128, 128))
    invc0 = sb("invc0", (128, 4, 16))
    cact = sb("cact", (128, 8))
    NCOL = 96
    cols = [sb(f"cols{l}", (128, NCOL)) for l in range(L)]
    fgc = sb("fgc", (128, 8))
    carry = [sb(f"carry{l}", (128, 16)) for l in range(L)]
    zcarry = [sb(f"zcarry{l}", (128, 4, 16)) for l in range(L)]
    ADN = NB * D // 8
    adast = xstage[:, :, :].rearrange("p a b -> p (a b)").rearrange("p (k n) -> p k n", k=8)

    pbank = [ps(f"pb{i}", (128, 512)) for i in range(8)]
    pctr = [0]

    def bank():
        b = pbank[pctr[0] % 8]
        pctr[0] += 1
        return b

    def sidx(h):
        return (h % 2) * 4 + h // 2

    def bank_bf(b):
        return b[:, :].bitcast(BF16)

    C_G1, C_G2 = 0, 8
    C_SH1, C_SH2 = 16, 24
    C_GT1, C_GT2 = 32, 40
    C_MU = 48
    C_MUV = 63
    C_W0, C_A0, C_V0, C_KK, C_KA, C_RK, C_LG, C_LB = 64, 68, 72, 76, 80, 84, 88, 92
    cols2 = [sb(f"cols2_{l}", (128, 16)) for l in range(L)]

    sem_cv = P.newsem("d_cv")
    sem_cst = P.newsem("d_cst")
    sem_x = [P.newsem(f"d_x{i}") for i in range(NB)]
    sem_o = [P.newsem(f"d_o{i}") for i in range(NB)]
    sem_w = [P.newsem(f"d_w{i}") for i in range(NW)]
    sem_misc = P.newsem("d_misc")
    sem_ada = P.newsem("d_ada")

    sem_cvl = [P.newsem(f"d_cv{l}") for l in range(L)]
    curl = [0]

    def cast(dst, src):
        P.op("pool", lambda e: e.dma_start(out=dst, in_=src), [], [], dma_sem=sem_cvl[curl[0]])

    for l in range(L):
        curl[0] = l
        for i in range(8):
            cast(win_b[l, i * 128:(i + 1) * 128, 0:N_IN], w_in[l, i * 128:(i + 1) * 128, :])
        if l >= 1:
            cast(win_b[l, :, N_IN:NINX], v_down[l - 1])
            cast(vup_b[l - 1], v_up[l - 1])
        cast(wup_b[l], w_up[l])
        cast(aup_b[l], a_up[l])
        cast(gup_b[l], g_up[l])
        cast(poolw_b[l], pool_w[l])
        for i in range(4):
            cast(pa_b[l, i * 128:(i + 1) * 128, :], proj_a[l, i * 128:(i + 1) * 128, :])
            cast(pb_b[l, i * 128:(i + 1) * 128, :], proj_b[l, i * 128:(i + 1) * 128, :])
        for i in range(8):
            cast(wout_b[l, i * 128:(i + 1) * 128, :], w_out[l, i * 128:(i + 1) * 128, :])
            cast(wgu_b[l, i * 128:(i + 1) * 128, :], w_gu[l, i * 128:(i + 1) * 128, :])
        for i in range(NFF):
            cast(wdn_b[l, i * 128:(i + 1) * 128, :], w_down[l, i * 128:(i + 1) * 128, :])
    CVL = [[(sem_cvl[l], P.cnt[sem_cvl[l]])] for l in range(L)]

    P.op("pool", lambda e: e.iota(iota_p[:, :], pattern=[[0, 1]], base=0, channel_multiplier=1, allow_small_or_imprecise_dtypes=True), [], [iota_p[:, :]])
    P.op("pool", lambda e: e.iota(iota_f[:, :], pattern=[[1, 128]], base=0, channel_multiplier=0, allow_small_or_imprecise_dtypes=True), [], [iota_f[:, :]])
    P.ts(identf[:, :], iota_f[:, :], iota_p[:, 0:1], None, ALU.is_equal)
    P.copy(identb[:, :], identf[:, :])
    P.ts(m_strict[:, 0, :], iota_f[:, :], iota_p[:, 0:1], None, ALU.is_gt)
    P.ts(m_incl[:, 0, :], iota_f[:, :], iota_p[:, 0:1], None, ALU.is_ge)
    P.ts(m_lower[:, 0, :], iota_f[:, :], iota_p[:, 0:1], None, ALU.is_lt)
    P.ts(rs[0][:, 0:128], iota_f[:, :], 64.0, None, ALU.is_ge)
    P.ts(rs[1][:, 0:1], iota_p[:, 0:1], 64.0, None, ALU.is_ge)
    P.ts(m_bd[:, 0, :], rs[0][:, 0:128], rs[1][:, 0:1], None, ALU.is_equal)
    for m in (m_strict, m_incl, m_lower, m_bd):
        for j in range(1, 4):
            P.copy(m[:, j, :], m[:, 0, :])
    P.copy(onesblk[:, :], m_bd[:, 0, :])
    HB = (8, 16, 32, 64)
    scr = [rs[i][:, 0:128] for i in range(10)] + [rsB[i][:, 0:128] for i in range(10)]
    Fb, PBt = {}, {}
    si = 0
    for b_ in HB:
        F = scr[si]
        si += 1
        P.op("pool", lambda e, F=F, b_=b_: e.iota(F, pattern=[[1, 128 // b_], [0, b_]], base=0, channel_multiplier=0,
                                                  allow_small_or_imprecise_dtypes=True), [], [F])
        Fb[b_] = F
        bk = bank()
        P.tr(bk[:, 0:128], F, identf[:, :])
        PB = scr[si]
        si += 1
        P.copy(PB, bk[:, 0:128])
        PBt[b_] = PB
    t1, t2 = scr[si], scr[si + 1]
    mD_l = sb("mD_l", (128, 128), BF16)
    mD_u = sb("mD_u", (128, 128), BF16)
    P.tt(t1, Fb[8], PBt[8], ALU.is_equal)
    mD8 = sb("mD8", (128, 128), BF16)
    P.copy(mD8[:, :], t1)
    P.tt(mD_l[:, :], t1, m_lower[:, 0, :], ALU.mult)
    P.tt(mD_u[:, :], t1, m_strict[:, 0, :], ALU.mult)
    ML, MU = {}, {}
    for b_ in HB:
        if b_ < 64:
            P.tt(t1, Fb[2 * b_], PBt[2 * b_], ALU.is_equal)
        else:
            P.memset(t1, 1.0)
        if b_ < 64:
            ML[b_] = sb(f"ML{b_}", (128, 128), BF16)
            P.tt(t2, Fb[b_], PBt[b_], ALU.is_lt)
            P.tt(ML[b_][:, :], t1, t2, ALU.mult)
        MU[b_] = sb(f"MU{b_}", (128, 128), BF16)
        P.tt(t2, Fb[b_], PBt[b_], ALU.is_gt)
        P.tt(MU[b_][:, :], t1, t2, ALU.mult)

    def bc4(m):
        return m[:, :].unsqueeze(1).broadcast_to([128, 4, 128])
    P.memset(onesmean[:, :], 1.0 / D)
    P.memset(scanmask[:, :], 1.0)
    for j in range(NB):
        P.memset(scanmask[:, j * CH:j * CH + 1], 0.0)
    for gi, w in enumerate(POOL_W):
        P.memset(invc0[:, gi, :], 1.0 / w)
        for t in range(w - 1):
            P.memset(invc0[:, gi, t:t + 1], 1.0 / (t + 1))
    P.memset(pscr[0][:, :], 0.0)
    P.memset(pscr[1][:, :], 0.0)
    for l in range(L):
        P.memset(carry[l][:, :], 0.0)
        P.memset(zcarry[l][:, :, :], 0.0)
        P.memset(Hs[l][:, :, :], 0.0)

    cst_reqs = []
    NC_KW = dict(allow_slow_non_contiguous=True)

    def colload(dst, src_row, n):
        for k0 in range(0, n // 128, 8):
            k1 = min(k0 + 8, n // 128)
            cst_reqs.append((dst[:, k0:k1], src_row[k0 * 128:k1 * 128].rearrange("(k p) -> p k", p=128), NC_KW))

    cst_reqs.append((cact[:, :], c_d[0].rearrange("(k p) -> p k", p=128), NC_KW))
    cst_reqs.append((fgc[:, :], final_g[0].rearrange("(k p) -> p k", p=128), NC_KW))
    mcols = [sb(f"mcol{l}", (128, 48)) for l in range(L)]
    for l in range(L):
        cl = cols[l]
        c2 = cols2[l]
        colload(c2[:, 8:16], norm1_g[l], D)
        colload(cl[:, C_G2:C_G2 + 8], norm2_g[l], D)
        colload(cl[:, C_MU:C_MU + 14], mu_shift[l, 0:1792], 1792)
        cst_reqs.append((cl[0:32, C_MU + 14:C_MU + 15], mu_shift[l, 1792:1824].rearrange("(p k) -> p k", k=1), NC_KW))
        if l >= 1:
            cst_reqs.append((cl[0:32, C_MUV:C_MUV + 1], mu_v[l - 1].rearrange("(p k) -> p k", k=1), NC_KW))
            colload(cl[:, C_V0:C_V0 + 4], v0[l - 1], DR)
        colload(cl[:, C_W0:C_W0 + 4], w0[l], DR)
        colload(cl[:, C_A0:C_A0 + 4], a0[l], DR)
        colload(cl[:, C_KK:C_KK + 4], k_k[l], DR)
        colload(cl[:, C_KA:C_KA + 4], k_a[l], DR)
        colload(cl[:, C_RK:C_RK + 4], r_k[l], DR)
        colload(cl[:, C_LG:C_LG + 4], lnx_g[l], DR)
        colload(cl[:, C_LB:C_LB + 4], lnx_b[l], DR)
        colload(c2[:, 0:4], pool_scale[l], DR)
        colload(mcols[l][:, :], ada_b[l], 6 * D)
    for l in range(L):
        P.memset(cols[l][:, :], 0.0)
    P.dma_group(sem_cst, cst_reqs)
    for l in range(L):
        P.ts(cols2[l][:, 4:8], cols[l][:, C_KA:C_KA + 4], -1.0, 1.0, ALU.mult, ALU.add)
    P.act(cact[:, :], cact[:, :], AF.Silu)
    def layer_units(l):
        u = []
        wl = win_b[l]

        def kview(w, c0, n):
            return w[:, c0:c0 + n].rearrange("(k p) n -> p k n", p=128)
        u.append(("r", [(kview(wl, 0, 512), 8, 512, 0)]))
        u.append(("k", [(kview(wl, 512, 512), 8, 512, 0)]))
        u.append(("v", [(kview(wl, 1024, 512), 8, 512, 0)]))
        u.append(("lora", [(kview(wl, 1536, 288), 8, 512, 0)]))
        if l >= 1:
            u.append(("vl", [(kview(wl, N_IN, 32), 8, 32, 0)]))
        u.append(("pool", [(kview(wl, 1824, 512), 8, 512, 0)]))
        u.append(("ga0", [(kview(wl, 2336, 512), 8, 512, 0)]))
        u.append(("gb0", [(kview(wl, 3360, 512), 8, 512, 0)]))
        u.append(("ga1", [(kview(wl, 2336 + 512, 512), 8, 512, 0)]))
        u.append(("gb1", [(kview(wl, 3360 + 512, 512), 8, 512, 0)]))
        u.append(("pa", [(pa_b[l].rearrange("(k p) n -> p k n", p=128), 4, 1024, 0)]))
        u.append(("pb", [(pb_b[l].rearrange("(k p) n -> p k n", p=128), 4, 1024, 0)]))
        for h in range(2):
            u.append((f"wo{h}", [(kview(wout_b[l], h * 512, 512), 8, 512, 0)]))
        for f0 in range(0, NFF, 4):
            n = min(4, NFF - f0) * 128
            u.append((f"fg{f0}", [(kview(wgu_b[l], f0 * 128, n), 8, 512, 0)]))
            u.append((f"fu{f0}", [(kview(wgu_b[l], DFF + f0 * 128, n), 8, 512, 0)]))
        for h in range(2):
            for k0 in range(0, NFF, 8):
                nk = min(8, NFF - k0)
                src = wdn_b[l, k0 * 128:(k0 + nk) * 128, h * 512:(h + 1) * 512].rearrange("(k p) n -> p k n", p=128)
                u.append((f"fd{h}_{k0}", [(src, nk, 512, 0)]))
        return u

    units = []
    for i in range(NT):
        for l in range(L):
            units += [(l, nm, lst) for nm, lst in layer_units(l)]
    wstate = {"loaded": 0, "cur": 0}

    def wload_next():
        n = wstate["loaded"]
        if n >= len(units):
            return
        l, nm, lst = units[n]
        slot = wring[n % NW]
        reqs = []
        for (src, nk, ncol, c0) in lst:
            w = src.shape[2]
            dst = slot[:, 0:nk * ncol].rearrange("p (k n) -> p k n", n=ncol)[:, :, c0:c0 + w]
            reqs.append((dst, src, {}))
        P.dma_group(sem_w[n % NW], reqs, extra_waits=CVL[l])
        wstate["loaded"] = n + 1

    for _ in range(NW):
        wload_next()

    def wget(l, nm):
        for n in range(wstate["cur"], min(wstate["cur"] + NW, len(units))):
            ul, unm, lst = units[n]
            if (ul, unm) == (l, nm):
                assert n < wstate["loaded"]
                return wring[n % NW]
        raise AssertionError((l, nm, wstate))

    def wdone():
        wstate["cur"] += 1
        wload_next()

    def load_misc(l):
        reqs = [(misc[0:64, 0, :], wup_b[l], {}), (misc[64:128, 1, :], aup_b[l], {})]
        if l >= 1:
            reqs.append((misc[0:32, 2, :], vup_b[l - 1], {}))
        reqs.append((misc[:, 3, :], gup_b[l, 0:128, :], {}))
        reqs.append((misc[0:32, 4, :], gup_b[l, 128:160, :], {}))
        reqs.append((misc[:, 5, :].rearrange("p (g n) -> p g n", n=128),
                     poolw_b[l].rearrange("(g p) n -> p g n", p=128), {}))
        P.dma_group(sem_misc, reqs, extra_waits=CVL[l])

    def compute_mod(l):
        cl = cols[l]
        mcol = mcols[l]
        NJ = ADN // 128
        for g in range(6 * D // ADN):
            P.dma(adast, ada_w[l, :, g * ADN:(g + 1) * ADN].rearrange("(k p) n -> p k n", p=128), sem_ada)
            b = bank()
            for j in range(NJ):
                for k in range(8):
                    P.mm(b[:, j:j + 1], adast[:, k, j * 128:(j + 1) * 128], cact[:, k:k + 1],
                         start=(k == 0), stop=(k == 7))
            P.tt(mcol[:, g * NJ:(g + 1) * NJ], mcol[:, g * NJ:(g + 1) * NJ], b[:, 0:NJ], ALU.add)
        P.copy(cl[:, C_SH1:C_SH1 + 8], mcol[:, 0:8])
        P.copy(cl[:, C_GT1:C_GT1 + 8], mcol[:, 16:24])
        P.copy(cl[:, C_SH2:C_SH2 + 8], mcol[:, 24:32])
        P.copy(cl[:, C_GT2:C_GT2 + 8], mcol[:, 40:48])
        P.stt(cl[:, C_G1:C_G1 + 8], mcol[:, 8:16], 1.0, cols2[l][:, 8:16], ALU.add, ALU.mult)
        P.stt(cl[:, C_G2:C_G2 + 8], mcol[:, 32:40], 1.0, cl[:, C_G2:C_G2 + 8], ALU.add, ALU.mult)


    eps_t = sb("eps_t", (128, 3))
    P.memset(eps_t[:, 2:3], 1e-24)
    P.memset(eps_t[:, 0:1], EPS_RMS)
    P.memset(eps_t[:, 1:2], EPS_GN)

    def squares():
        for k in range(8):
            if k % 2 == 0:
                P.act(sqb[:, k, :], xs[:, k, :], AF.Square)
            else:
                P.tt(sqb[:, k, :], xs[:, k, :], xs[:, k, :], ALU.mult, eng=("dve" if k % 4 == 1 else "pool"))

    def rmsnorm_mod(gcol, shcol, cl):
        squares()
        b = bank()
        for k in range(8):
            P.mm(b[:, 0:T], onesmean[:, :], sqb[:, k, :], start=(k == 0), stop=(k == 7))
        P.act(rstd[:, :], b[:, 0:T], AF.Ln, bias=eps_t[:, 0:1])
        P.act(rstd[:, :], rstd[:, :], AF.Exp, scale=-0.5)
        for k in range(8):
            sc = fsc[k % 2]
            P.stt(sc[:, :], xs[:, k, :], cl[:, gcol + k:gcol + k + 1], rstd[:, :], ALU.mult, ALU.mult)
            P.act(hb[:, k, :], sc[:, :], AF.Identity, bias=cl[:, shcol + k:shcol + k + 1])

    def proj_chunk(slot, c0, ncol, nk=8, kstride=512, rhs=None):
        b = bank()
        sv = slot[:, 0:nk * kstride].rearrange("p (k n) -> p k n", n=kstride)
        for k in range(nk):
            P.mm(b[0:ncol, 0:T], sv[:, k, c0:c0 + ncol], (hb if rhs is None else rhs)[:, k, :],
                 start=(k == 0), stop=(k == nk - 1))
        return b

    pr_ctr = [0]

    def shift_chunk(b, nrow, mucol, cidx, cl, l, first_tile, out_ap=None, post=None):
        pr = praw[pr_ctr[0] % 3]
        pr_ctr[0] += 1
        P.copy(pr[0:nrow, 0:1], carry[l][0:nrow, cidx:cidx + 1], eng="pool")
        P.copy(pr[0:nrow, 1:T + 1], b[0:nrow, 0:T], eng="act")
        P.copy(carry[l][0:nrow, cidx:cidx + 1], pr[0:nrow, T:T + 1], eng="pool")
        d = fsc[pr_ctr[0] % 2]
        P.tt(d[0:nrow, :], pr[0:nrow, 0:T], pr[0:nrow, 1:T + 1], ALU.subtract)
        P.stt(out_ap, d[0:nrow, :], cl[0:nrow, mucol:mucol + 1], pr[0:nrow, 1:T + 1], ALU.mult, ALU.add)

    n_out = [0]
    for ti in range(NT):
        t0 = ti * T
        for tb in range(NB):
            P.dma(xstage[:, tb, :], x_d[t0 + tb * 128:t0 + (tb + 1) * 128, :], sem_x[tb])
        for k in range(8):
            b = bank()
            for tb in range(NB):
                P.tr(b[:, tb * 128:(tb + 1) * 128], xstage[:, tb, k * 128:(k + 1) * 128], identf[:, :],
                     inc=(tb == NB - 1))
            P.copy(xs[:, k, :], b[:, 0:T], eng=("act" if k % 2 else "dve"))

        def layer_body(l):
            done0 = wstate['cur']
            nunits = len(layer_units(l))
            def skip_rest():
                while wstate['cur'] < done0 + nunits:
                    wdone()
            cl = cols[l]
            c2 = cols2[l]
            if ti == 0:
                compute_mod(l)
            load_misc(l)
            rmsnorm_mod(C_G1, C_SH1, cl)
            for nm, dstbuf, cbase in (("r", rsh, 0), ("k", ksh, 4), ("v", vsh, 8)):
                slot = wget(l, nm)
                for j in range(4):
                    b = proj_chunk(slot, j * 128, 128)
                    shift_chunk(b, 128, C_MU + cbase + j, cbase + j, cl, l, ti == 0, out_ap=dstbuf[:, j, :])
                wdone()
            slot = wget(l, "lora")
            b = proj_chunk(slot, 0, 128)
            shift_chunk(b, 128, C_MU + 12, 12, cl, l, ti == 0, out_ap=rs[0][:, :])
            P.act(wab[0:64, :], rs[0][0:64, :], AF.Tanh)
            P.copy(wab[64:128, :], rs[0][64:128, :], eng="dve")
            b = proj_chunk(slot, 128, 128)
            shift_chunk(b, 128, C_MU + 13, 13, cl, l, ti == 0, out_ap=rs[1][:, :])
            P.act(glb[:, 0, :], rs[1][:, :], AF.Sigmoid)
            b = proj_chunk(slot, 256, 32)
            shift_chunk(b, 32, C_MU + 14, 14, cl, l, ti == 0, out_ap=rs[2][0:32, :])
            P.act(glb[0:32, 1, :], rs[2][0:32, :], AF.Sigmoid)
            wdone()
            if l >= 1:
                slot = wget(l, "vl")
                b = proj_chunk(slot, 0, 32, kstride=32)
                shift_chunk(b, 32, C_MUV, 15, cl, l, ti == 0, out_ap=rs[3][0:32, :])
                P.copy(vlb[0:32, :], rs[3][0:32, :], eng="dve")
                wdone()
            if STAGE < 4:
                skip_rest()
                return
            slot = wget(l, "pool")
            for g in range(4):
                b = proj_chunk(slot, g * 128, 128)
                P.copy(zext[:, g, 0:16], zcarry[l][:, g, :], eng="pool")
                P.copy(zext[:, g, 16:16 + T], b[:, 0:T], eng="act")
                P.copy(zcarry[l][:, g, :], zext[:, g, T:T + 16], eng="pool")
            wdone()
            for g, w in enumerate(POOL_W):
                src = zext[:, g, :]
                sh = 1
                n = 0
                W_ = T + 16
                while sh < w:
                    dst = pscr[n % 2]
                    P.tt(dst[:, sh:W_], src[:, sh:W_], src[:, 0:W_ - sh], ALU.add, eng="pool")
                    src = dst
                    sh *= 2
                    n += 1
                P.stt(pooled[:, g, :], src[:, 16:16 + T], 1.0 / w, zext[:, g, 16:16 + T], ALU.mult, ALU.subtract)
                if ti == 0:
                    P.tt(rs[4][:, 0:16], src[:, 16:32], invc0[:, g, :], ALU.mult)
                    P.tt(pooled[:, g, 0:16], rs[4][:, 0:16], zext[:, g, 16:32], ALU.subtract)
            pw = misc[:, 5, :].rearrange("p (g n) -> p g n", n=128)
            for g in range(4):
                b = bank()
                P.mm(b[:, 0:T], pw[:, g, :], pooled[:, g, :])
                P.act(ybb[:, g, :], b[:, 0:T], AF.Identity, scale=c2[:, g:g + 1])

            if STAGE < 5:
                skip_rest()
                return
            for cp in range(2):
                cs = (2 * cp, 2 * cp + 1)
                RS = {cs[0]: rs, cs[1]: rsB}
                FM = {cs[0]: fmtmp, cs[1]: fmtmpB}

                def col(base, c):
                    return cl[:, base + c:base + c + 1]
                bsw, bav, bvg, b4 = {}, {}, {}, {}
                for c in cs:
                    bsw[c] = bank()
                    P.mm(bsw[c][:, 0:T], misc[0:64, 0, c * 128:(c + 1) * 128], wab[0:64, :])
                for c in cs:
                    bav[c] = bank()
                    P.mm(bav[c][:, 0:T], misc[64:128, 1, c * 128:(c + 1) * 128], wab[64:128, :])
                if l >= 1:
                    for c in cs:
                        bvg[c] = bank()
                        P.mm(bvg[c][:, 0:T], misc[0:32, 2, c * 128:(c + 1) * 128], vlb[0:32, :])
                for c in cs:
                    sw, cum, cumx, Ep, Em, Epv, kkn, av, mm_, bb = RS[c]
                    P.act(sw[:, :], bsw[c][:, 0:T], AF.Sigmoid, bias=col(C_W0, c))
                for c in cs:
                    sw, cum, cumx, Ep, Em, Epv, kkn, av, mm_, bb = RS[c]
                    P.act(av[:, :], bav[c][:, 0:T], AF.Sigmoid, bias=col(C_A0, c))
                if l >= 1:
                    for c in cs:
                        sw, cum, cumx, Ep, Em, Epv, kkn, av, mm_, bb = RS[c]
                        P.act(mm_[:, :], bvg[c][:, 0:T], AF.Sigmoid, bias=col(C_V0, c))
                for c in cs:
                    P.act(FM[c][0][:, :], ksh[:, c, :], AF.Square, scale=col(C_KK, c))
                for c in cs:
                    sw, cum, cumx, Ep, Em, Epv, kkn, av, mm_, bb = RS[c]
                    P.scan(cum[:, :], scanmask[:, :], sw[:, :], 0.0, ALU.mult, ALU.add)
                for c in cs:
                    sw, cum, cumx, Ep, Em, Epv, kkn, av, mm_, bb = RS[c]
                    P.tt(cumx[:, :], cum[:, :], sw[:, :], ALU.subtract, eng="pool")
                for c in cs:
                    b4[c] = bank()
                    P.mm(b4[c][:, 0:T], onesblk[:, :], FM[c][0][:, :])
                for c in cs:
                    sw, cum, cumx, Ep, Em, Epv, kkn, av, mm_, bb = RS[c]
                    if l == 0:
                        P.copy(vfirst[:, c, :], vsh[:, c, :], eng="pool")
                    else:
                        P.tt(bb[:, :], vfirst[:, c, :], vsh[:, c, :], ALU.subtract, eng="pool")
                        P.tt(bb[:, :], bb[:, :], mm_[:, :], ALU.mult)
                        P.tt(vsh[:, c, :], vsh[:, c, :], bb[:, :], ALU.add, eng="pool")
                for c in cs:
                    sw, cum, cumx, Ep, Em, Epv, kkn, av, mm_, bb = RS[c]
                    P.act(Ep[:, :], cum[:, :], AF.Exp, scale=KDEC)
                    P.act(Em[:, :], cum[:, :], AF.Exp, scale=-KDEC)
                    P.act(Epv[:, :], cumx[:, :], AF.Exp, scale=KDEC)
                for c in cs:
                    sw, cum, cumx, Ep, Em, Epv, kkn, av, mm_, bb = RS[c]
                    P.act(kkn[:, :], b4[c][:, 0:T], AF.Ln, bias=eps_t[:, 2:3])
                    P.act(kkn[:, :], kkn[:, :], AF.Exp, scale=-0.5)
                slot_ga = wget(l, f"ga{cp}")
                slot_gb = wget(l, f"gb{cp}")
                for jj in range(4):
                    jo = cp * 4 + jj
                    bga = proj_chunk(slot_ga, jj * 128, 128)
                    bgb = proj_chunk(slot_gb, jj * 128, 128)
                    P.act(sgab[:, 0, jo, :], bga[:, 0:T], AF.Sigmoid)
                    P.act(sgab[:, 1, jo, :], bgb[:, 0:T], AF.Sigmoid)
                wdone()
                wdone()
                for c in cs:
                    sw, cum, cumx, Ep, Em, Epv, kkn, av, mm_, bb = RS[c]
                    for j in range(NB):
                        P.copy(pcs[:, j, c:c + 1], Ep[:, (j + 1) * CH - 1:(j + 1) * CH], eng="pool")
                for c in cs:
                    sw, cum, cumx, Ep, Em, Epv, kkn, av, mm_, bb = RS[c]
                    P.stt(kkn[:, :], ksh[:, c, :], col(C_KK, c), kkn[:, :], ALU.mult, ALU.mult)
                for c in cs:
                    sw, cum, cumx, Ep, Em, Epv, kkn, av, mm_, bb = RS[c]
                    P.ts(mm_[:, :], av[:, :], col(C_KA, c), c2[:, 4 + c:5 + c], ALU.mult, ALU.add)
                    P.tt(mm_[:, :], mm_[:, :], ksh[:, c, :], ALU.mult)
                    P.stt(rkb[:, c, :], rsh[:, c, :], col(C_RK, c), mm_[:, :], ALU.mult, ALU.mult)
                for c in cs:
                    sw, cum, cumx, Ep, Em, Epv, kkn, av, mm_, bb = RS[c]
                    P.tt(bb[:, :], av[:, :], kkn[:, :], ALU.mult, eng="pool")
                for c in cs:
                    sw, cum, cumx, Ep, Em, Epv, kkn, av, mm_, bb = RS[c]
                    P.tt(rt_b[:, c, :], rsh[:, c, :], Ep[:, :], ALU.mult)
                    P.tt(kt_b[:, c, :], mm_[:, :], Em[:, :], ALU.mult)
                    P.stt(at_b[:, c, :], kkn[:, :], -1.0, Epv[:, :], ALU.mult, ALU.mult)
                for c in cs:
                    sw, cum, cumx, Ep, Em, Epv, kkn, av, mm_, bb = RS[c]
                    P.tt(bt_b[:, c, :], bb[:, :], Em[:, :], ALU.mult, eng="pool")
                for c in cs:
                    sw, cum, cumx, Ep, Em, Epv, kkn, av, mm_, bb = RS[c]
                    for j in range(NB):
                        js = slice(j * CH, (j + 1) * CH)
                        P.ts(cumx[:, js], Em[:, js], pcs[:, j, c:c + 1], None, ALU.mult)
                    P.tt(FM[c][0][:, :], mm_[:, :], cumx[:, :], ALU.mult)
                for c in cs:
                    sw, cum, cumx, Ep, Em, Epv, kkn, av, mm_, bb = RS[c]
                    P.tt(FM[c][1][:, :], bb[:, :], cumx[:, :], ALU.mult, eng="pool")
                    P.copy(FM[c][2][:, :], vsh[:, c, :], eng="pool")
                for c in cs:
                    for srcb, dstb in ((at_b[:, c, :], atm), (FM[c][0][:, :], kendtm), (FM[c][1][:, :], bendtm),
                                       (FM[c][2][:, :], vtm)):
                        bt = bank_bf(bank())
                        for j in range(NB):
                            P.tr(bt[:, j * 128:(j + 1) * 128], srcb[:, j * CH:(j + 1) * CH], identb[:, :],
                                 inc=(j == NB - 1))
                        P.copy(dstb[:, :, c * 128:(c + 1) * 128],
                               bt[:, 0:NB * 128].rearrange("p (j n) -> p j n", n=128), eng="act")

            if STAGE < 6:
                skip_rest()
                return
            H = Hs[l]
            Hbf = Hb[l]
            P.copy(Hbf[:, :, :], H[:, :, :], eng="pool")
            for j in range(NB):
                js = slice(j * CH, (j + 1) * CH)
                X, XT = Xf, XTf
                for hg in range(2):
                    bX, bXT, bAK, bRB, bRK = bank(), bank(), bank(), bank(), bank()
                    for hh in range(4):
                        h = 2 * hh + hg
                        c, hp = h // 2, (h % 2) * 64
                        A_ = at_b[hp:hp + 64, c, js]
                        B_ = bt_b[hp:hp + 64, c, js]
                        K_ = kt_b[hp:hp + 64, c, js]
                        R_ = rt_b[hp:hp + 64, c, js]
                        o = slice(hh * 128, (hh + 1) * 128)
                        P.mm(bX[:, o], A_, B_)
                        P.mm(bXT[:, o], B_, A_)
                        P.mm(bAK[:, o], K_, A_)
                        P.mm(bRB[:, o], B_, R_)
                        P.mm(bRK[:, o], K_, R_)
                    hsl = slice(hg * 4, hg * 4 + 4)
                    P.tt(X[:, hsl, :], bX[:, :].rearrange("p (h n) -> p h n", n=128), m_lower[:, :, :], ALU.mult)
                    P.tt(XT[:, hsl, :], bXT[:, :].rearrange("p (h n) -> p h n", n=128), m_strict[:, :, :], ALU.mult)
                    P.tt(Xb[0][:, hsl, :], X[:, hsl, :], bc4(mD8), ALU.mult, eng="pool")
                    P.tt(XTb[0][:, hsl, :], XT[:, hsl, :], bc4(mD8), ALU.mult, eng="pool")
                    P.tt(akT[:, hsl, :], bAK[:, :].rearrange("p (h n) -> p h n", n=128), m_strict[:, :, :], ALU.mult)
                    P.tt(rbT[:, hsl, :], bRB[:, :].rearrange("p (h n) -> p h n", n=128), m_incl[:, :, :], ALU.mult)
                    P.tt(rkT[:, hsl, :], bRK[:, :].rearrange("p (h n) -> p h n", n=128), m_incl[:, :, :], ALU.mult)
                idb8 = identb[:, :].unsqueeze(1).broadcast_to([128, 8, 128])
                P.tt(Tb[0][:, :, :], Xb[0][:, :, :], idb8, ALU.add, eng="pool")
                P.tt(TTb2[0][:, :, :], XTb[0][:, :, :], idb8, ALU.add, eng="pool")
                tcur = 0

                def v4(bk_):
                    return bk_[:, :].rearrange("p (h n) -> p h n", n=128)

                def acc_group(bk_, o, lhs, rhs_t, base=None):
                    P.mm(bk_[:, o], identb[:, :], rhs_t if base is None else base, start=True, stop=False)
                    P.mm(bk_[:, o], lhs, rhs_t, start=False, stop=True)
                for lev in (1, 2):
                    Xc, XTc, Xn, XTn = Xb[(lev - 1) % 2], XTb[(lev - 1) % 2], Xb[lev % 2], XTb[lev % 2]
                    Tc, TTc, Tn, TTn = Tb[tcur], TTb2[tcur], Tb[1 - tcur], TTb2[1 - tcur]
                    for hg in range(2):
                        hsl = slice(hg * 4, hg * 4 + 4)
                        bX = bank()
                        for hh in range(4):
                            h = hg * 4 + hh
                            P.mm(bX[:, hh * 128:(hh + 1) * 128], XTc[:, h, :], Xc[:, h, :])
                        P.copy(Xn[:, hsl, :], v4(bX), eng="act")
                        bXT = bank()
                        for hh in range(4):
                            h = hg * 4 + hh
                            P.mm(bXT[:, hh * 128:(hh + 1) * 128], Xc[:, h, :], XTc[:, h, :])
                        P.copy(XTn[:, hsl, :], v4(bXT), eng="act")
                        bT = bank()
                        for hh in range(4):
                            h = hg * 4 + hh
                            acc_group(bT, slice(hh * 128, (hh + 1) * 128), XTn[:, h, :], Tc[:, h, :])
                        P.copy(Tn[:, hsl, :], v4(bT), eng="act")
                        bTT = bank()
                        for hh in range(4):
                            h = hg * 4 + hh
                            acc_group(bTT, slice(hh * 128, (hh + 1) * 128), Xn[:, h, :], TTc[:, h, :])
                        P.copy(TTn[:, hsl, :], v4(bTT), eng="act")
                    tcur = 1 - tcur
                for b_ in HB:
                    Tc, TTc, Tn, TTn = Tb[tcur], TTb2[tcur], Tb[1 - tcur], TTb2[1 - tcur]
                    last = (b_ == 64)
                    for hg in range(2):
                        hsl = slice(hg * 4, hg * 4 + 4)
                        if not last:
                            bQ = bank()
                            for hh in range(4):
                                h = hg * 4 + hh
                                P.mm(bQ[:, hh * 128:(hh + 1) * 128], XTf[:, h, :], Tc[:, h, :])
                            P.tt(Qb[:, hsl, :], v4(bQ), bc4(ML[b_]), ALU.mult)
                        bP = bank()
                        for hh in range(4):
                            h = hg * 4 + hh
                            P.mm(bP[:, hh * 128:(hh + 1) * 128], Xf[:, h, :], TTc[:, h, :])
                        P.tt(Pb[:, hsl, :], v4(bP), bc4(MU[b_]), ALU.mult)
                        if not last:
                            bT = bank()
                            for hh in range(4):
                                h = hg * 4 + hh
                                acc_group(bT, slice(hh * 128, (hh + 1) * 128), TTc[:, h, :], Qb[:, h, :], base=Tc[:, h, :])
                            P.copy(Tn[:, hsl, :], v4(bT), eng="act")
                        bTT = bank()
                        for hh in range(4):
                            h = hg * 4 + hh
                            acc_group(bTT, slice(hh * 128, (hh + 1) * 128), Tc[:, h, :], Pb[:, h, :], base=TTc[:, h, :])
                        P.copy(TTn[:, hsl, :], v4(bTT), eng="act")
                    tcur = 1 - tcur
                TT = TTb2[tcur]
                bA = bank()
                for h in range(8):
                    P.mm(bA[:, h * 64:(h + 1) * 64], akT[:, sidx(h), :], vtm[:, j, h * 64:(h + 1) * 64])
                P.copy(akv[:, :], bA[:, :], eng="act")
                bV = bank()
                for h in range(8):
                    P.mm(bV[:, h * 64:(h + 1) * 64], TT[:, sidx(h), :], akv[:, h * 64:(h + 1) * 64])
                P.copy(vhat[:, :], bV[:, :], eng="dve")
                if SUB < 4:
                    continue
                for hg in range(2):
                    bH = bank()
                    for hh in range(4):
                        c = hh
                        P.mm(bH[:, hh * 128:(hh + 1) * 128], atm[:, j, c * 128:(c + 1) * 128], TT[:, hg * 4 + hh, :])
                    rows = slice(hg * 64, hg * 64 + 64)
                    P.copy(ahT[rows, :, :], bH[rows, :].rearrange("p (c n) -> p c n", n=128),
                           eng=("act" if hg else "dve"))
                if STAGE < 7:
                    continue
                bU = bank()
                for c in range(4):
                    P.mm(bU[:, c * 128:(c + 1) * 128], ahT[:, c, :], Hbf[:, c, :])
                P.tt(ub[:, :], bU[:, :], vhat[:, :], ALU.add)
                bY = bank()
                for c in range(4):
                    P.mm(bY[:, c * 128:(c + 1) * 128], rt_b[:, c, js], Hbf[:, c, :], start=True, stop=False)
                    for h in (2 * c, 2 * c + 1):
                        o = slice(h * 64, (h + 1) * 64)
                        P.mm(bY[:, o], rbT[:, sidx(h), :], ub[:, o], start=False, stop=False)
                        P.mm(bY[:, o], rkT[:, sidx(h), :], vtm[:, j, o], start=False, stop=(h == 2 * c + 1))
                bS = bank()
                for c in range(4):
                    o = slice(c * 128, (c + 1) * 128)
                    P.mm(bS[:, o], kendtm[:, j, o], vtm[:, j, o], start=True, stop=False)
                    P.mm(bS[:, o], bendtm[:, j, o], ub[:, o], start=False, stop=True)
                P.tt(hx[:, :, :], bS[:, :].rearrange("p (c n) -> p c n", n=128), m_bd[:, :, :], ALU.mult)
                P.tt(H[:, :, :], H[:, :, :], pcs[:, j, :].unsqueeze(2).broadcast_to([128, 4, 128]), ALU.mult,
                     eng="pool")
                P.tt(H[:, :, :], H[:, :, :], hx[:, :, :], ALU.add, eng="pool")
                P.copy(Hbf[:, :, :], H[:, :, :], eng="pool")
                P.copy(ysb[:, :], bY[:, :], eng="act")
                P.act(ysq[:, :], bY[:, :], AF.Square)
                y3 = ysb[:, :].rearrange("p (h n) -> p h n", n=64)
                q3 = ysq[:, :].rearrange("p (h n) -> p h n", n=64)
                P.reduce(gst[:, :, 0], y3, ALU.add)
                P.reduce(gst[:, :, 1], q3, ALU.add)
                P.ts(gst[:, :, 0], gst[:, :, 0], 1.0 / 64, None, ALU.mult)
                P.tt(gst[:, :, 2], gst[:, :, 0], gst[:, :, 0], ALU.mult)
                P.stt(gst[:, :, 1], gst[:, :, 1], 1.0 / 64, gst[:, :, 2], ALU.mult, ALU.subtract)
                P.act(gst[:, :, 3], gst[:, :, 1], AF.Ln, bias=eps_t[:, 1:2])
                P.act(gst[:, :, 3], gst[:, :, 3], AF.Exp, scale=-0.5)
                P.tt(y3, y3, gst[:, :, 0].unsqueeze(2).broadcast_to([128, 8, 64]), ALU.subtract)
                P.tt(ynb[:, :].rearrange("p (h n) -> p h n", n=64), y3,
                     gst[:, :, 3].unsqueeze(2).broadcast_to([128, 8, 64]), ALU.mult)
                bt = bank_bf(bank())
                for c in range(4):
                    P.tr(bt[:, c * 128:(c + 1) * 128], ynb[:, c * 128:(c + 1) * 128], identb[:, :], inc=(c == 3))
                for c in range(4):
                    P.act(yab[:, c, js], bt[:, c * 128:(c + 1) * 128], AF.Identity,
                          bias=cl[:, C_LB + c:C_LB + c + 1], scale=cl[:, C_LG + c:C_LG + c + 1])
            if STAGE < 8:
                skip_rest()
                return
            for c in range(4):
                bb_ = bank()
                P.mm(bb_[:, 0:T], onesblk[:, :], rkb[:, c, :])
                bg = bank()
                P.mm(bg[:, 0:T], misc[:, 3, c * 128:(c + 1) * 128], glb[:, 0, :], start=True, stop=False)
                P.mm(bg[:, 0:T], misc[0:32, 4, c * 128:(c + 1) * 128], glb[0:32, 1, :], start=False, stop=True)
                s0 = ya_s[0]
                P.tt(s0[:, :], bb_[:, 0:T], vsh[:, c, :], ALU.mult)
                P.tt(s0[:, :], s0[:, :], yab[:, c, :], ALU.add, eng="pool")
                P.tt(yab[:, c, :], s0[:, :], bg[:, 0:T], ALU.mult)
            slot_pa = wget(l, "pa")
            slot_pb = wget(l, "pb")
            for jo in range(8):
                bpa = proj_chunk(slot_pa, jo * 128, 128, nk=4, kstride=1024, rhs=yab)
                bpb = proj_chunk(slot_pb, jo * 128, 128, nk=4, kstride=1024, rhs=ybb)
                P.tt(mx[0][:, :], sgab[:, 0, jo, :], bpa[:, 0:T], ALU.mult)
                P.tt(mx[1][:, :], sgab[:, 1, jo, :], bpb[:, 0:T], ALU.mult)
                P.tt(mixed[:, jo, :], mx[0][:, :], mx[1][:, :], ALU.add, eng="pool")
            wdone()
            wdone()
            for hh in range(2):
                slot = wget(l, f"wo{hh}")
                for jj in range(4):
                    jo = hh * 4 + jj
                    b = proj_chunk(slot, jj * 128, 128, rhs=mixed)
                    P.stt(xs[:, jo, :], b[:, 0:T], cl[:, C_GT1 + jo:C_GT1 + jo + 1], xs[:, jo, :], ALU.mult, ALU.add)
                wdone()

            if STAGE < 9:
                skip_rest()
                return
            rmsnorm_mod(C_G2, C_SH2, cl)
            for f0 in range(0, NFF, 4):
                nf = min(4, NFF - f0)
                slot_g = wget(l, f"fg{f0}")
                slot_u = wget(l, f"fu{f0}")
                for ff in range(nf):
                    bg = proj_chunk(slot_g, ff * 128, 128)
                    bu = proj_chunk(slot_u, ff * 128, 128)
                    s = fsc[ff % 2]
                    P.act(s[:, :], bg[:, 0:T], AF.Silu)
                    P.tt(actb[:, f0 + ff, :], s[:, :], bu[:, 0:T], ALU.mult)
                wdone()
                wdone()
            for hh in range(2):
                accs = [bank() for _ in range(4)]
                for k0 in range(0, NFF, 8):
                    nk = min(8, NFF - k0)
                    slot = wget(l, f"fd{hh}_{k0}")
                    sv = slot[:, 0:nk * 512].rearrange("p (k n) -> p k n", n=512)
                    for jj in range(4):
                        for k in range(nk):
                            P.mm(accs[jj][:, 0:T], sv[:, k, jj * 128:(jj + 1) * 128], actb[:, k0 + k, :],
                                 start=(k0 + k == 0), stop=(k0 + k == NFF - 1))
                    wdone()
                for jj in range(4):
                    jo = hh * 4 + jj
                    P.stt(xs[:, jo, :], accs[jj][:, 0:T], cl[:, C_GT2 + jo:C_GT2 + jo + 1], xs[:, jo, :],
                          ALU.mult, ALU.add)


        for l in range(L if STAGE >= 2 else 0):
            layer_body(l)
        squares()
        b = bank()
        for k in range(8):
            P.mm(b[:, 0:T], onesmean[:, :], sqb[:, k, :], start=(k == 0), stop=(k == 7))
        P.act(rstd[:, :], b[:, 0:T], AF.Ln, bias=eps_t[:, 0:1])
        P.act(rstd[:, :], rstd[:, :], AF.Exp, scale=-0.5)
        for k in range(8):
            P.stt(xs[:, k, :], xs[:, k, :], fgc[:, k:k + 1], rstd[:, :], ALU.mult, ALU.mult)
        for tb in range(NB):
            for kh in range(2):
                b = bank()
                for kk in range(4):
                    k = kh * 4 + kk
                    P.tr(b[:, kk * 128:(kk + 1) * 128], xs[:, k, tb * 128:(tb + 1) * 128], identf[:, :],
                         inc=(kk == 3))
                P.copy(ostage[:, tb, kh * 512:(kh + 1) * 512], b[:, :], eng=("act" if kh else "dve"))
            P.dma(out_d[t0 + tb * 128:t0 + (tb + 1) * 128, :], ostage[:, tb, :], sem_o[tb])
            n_out[0] += 1

    assert STAGE < 2 or wstate["cur"] == len(units)
    if debug_taps:
        dbg = nc.dram_tensor("dbg", [128, 96 + 10 * T + 512 * 3 + 8 * T], F32, kind="ExternalOutput").ap()
        sem_dbg = P.newsem("d_dbg")
        reqs = [(dbg[:, 0:96], cols[0][:, :], {})]
        for i in range(10):
            reqs.append((dbg[:, 96 + i * T:96 + (i + 1) * T], rs[i][:, :], {}))
        o = 96 + 10 * T
        reqs.append((dbg[:, o:o + 512], ysb[:, :], {}))
        reqs.append((dbg[:, o + 512:o + 1024], vhat[:, :], {}))
        reqs.append((dbg[:, o + 1024:o + 1536], Hs[0][:, :, :].rearrange("p a b -> p (a b)"), {}))
        o += 1536
        reqs.append((dbg[:, o:o + 8 * T].rearrange("p (k t) -> p k t", k=8), xs[:, :, :], {}))
        P.dma_group(sem_dbg, reqs)
        final_extra = [(sem_dbg, P.cnt[sem_dbg])]
    else:
        final_extra = []
    P.emit(final_waits=[(sem_o[tb], P.cnt[sem_o[tb]]) for tb in range(NB)] + final_extra)
    es.close()
    return nc, P


S_FULL, L_FULL = 4096, 4
_CACHE = {}


def run_cores(inputs, S, L, T=256):
    key = (S, L, T)
    if key not in _CACHE:
        _CACHE[key] = build_program(S, L, T)[0]
    nc = _CACHE[key]
    B = inputs["x"].shape[0]
    shared = {}
    for k, v in inputs.items():
        if k in ("x", "c"):
            continue
        a = np.ascontiguousarray(np.asarray(v, dtype=np.float32))
        if k == "r_k":
            a = a.reshape(a.shape[0], DR)
        elif k == "pool_w":
            a = a.reshape(a.shape[0], 512, 128)
        elif k == "final_g":
            a = a.reshape(1, D)
        shared[k] = a
    if L == 1:
        for k, shp in (("v_down", (1, D, 32)), ("mu_v", (1, 32)), ("v0", (1, DR)), ("v_up", (1, 32, DR))):
            if shared[k].shape[0] == 0:
                shared[k] = np.zeros(shp, np.float32)
    x = np.asarray(inputs["x"], dtype=np.float32)
    c = np.asarray(inputs["c"], dtype=np.float32)
    in_maps = []
    for b in range(B):
        m = dict(shared)
        m["x"] = np.ascontiguousarray(x[b])
        m["c"] = np.ascontiguousarray(c[b:b + 1])
        in_maps.append(m)
    res = run_bass_kernel_spmd(nc, in_maps, core_ids=list(range(B)))
    return np.stack([np.asarray(r["out"]) for r in res.results], axis=0).astype(np.float32)


def kernel(**inputs):
    return run_cores(inputs, S_FULL, L_FULL)
```

```python
import os
import sys
import numpy as np
from contextlib import ExitStack
import concourse.bass as bass
import concourse.mybir as mybir
from concourse.bass_utils import run_bass_kernel_spmd

F32 = mybir.dt.float32
BF16 = mybir.dt.bfloat16
ALU = mybir.AluOpType
AF = mybir.ActivationFunctionType
AX = mybir.AxisListType

D = 1024
DR = 512
NH = 8
HS = 64
DFF = 2816
NFF = DFF // 128
N_SHIFT = 1824
N_IN = 4384
NINX = 4416
EPS_RMS = 1e-6
EPS_GN = 64e-5
CH = 128
KDEC = -0.6065306597126334
POOL_W = (2, 4, 8, 16)


class Prog:
    ENG = ("pe", "act", "dve", "pool", "sp")

    def __init__(self, nc, es):
        self.nc = nc
        self.es = es
        self.ops = {e: [] for e in self.ENG}
        self.cnt = {}
        self.sems = {}
        self.waited = {e: {} for e in self.ENG}
        self.recs = {}
        for e in ("pe", "act", "dve", "pool"):
            self.newsem("c_" + e)
        self.n_ops = 0

    def newsem(self, name):
        self.sems[name] = self.es.enter_context(self.nc.semaphore(name))
        self.cnt[name] = 0
        return name

    @staticmethod
    def region(ap):
        t = ap.tensor
        cls = type(t).__name__
        if not ("SB" in cls or "PSum" in cls):
            return None
        a = ap.ap
        off = ap.offset
        pst = a[0][0]
        if pst == 0:
            pst = 1 << 30
        p0 = off // pst
        f0 = off % pst
        p1 = p0 + a[0][1]
        ext = 1
        for s, c in a[1:]:
            ext += (c - 1) * abs(s)
        esz = 2 if ap.dtype == BF16 else 4
        if "PSum" in cls:
            return (t.name, 0, 128, 0, 1 << 20)
        return (t.name, p0, p1, f0 * esz, (f0 + ext) * esz)

    def op(self, eng, fn, reads, writes, inc=True, dma_sem=None, extra_waits=(), group_final=None):
        deps = {}

        def add(sem, val):
            if val > deps.get(sem, 0):
                deps[sem] = val

        rregs = [r for r in (self.region(a) for a in reads) if r is not None]
        wregs = [r for r in (self.region(a) for a in writes) if r is not None]
        for (nm, p0, p1, f0, f1) in rregs:
            psum = f1 >= (1 << 20)
            for rec in self.recs.get(nm, ()):
                if (rec[4] == "w" or (psum and rec[7] != eng)) and rec[0] < p1 and p0 < rec[1] \
                        and rec[2] < f1 and f0 < rec[3]:
                    add(rec[5], rec[6])
        for (nm, p0, p1, f0, f1) in wregs:
            for rec in self.recs.get(nm, ()):
                if rec[0] < p1 and p0 < rec[1] and rec[2] < f1 and f0 < rec[3]:
                    add(rec[5], rec[6])
        for s, v in extra_waits:
            add(s, v)
        if eng == "pe":
            deps.pop("c_pe", None)
        if dma_sem is not None:
            self.cnt[dma_sem] += 16
            tok = (dma_sem, self.cnt[dma_sem] if group_final is None else group_final)
            incspec = (dma_sem, 16)
        else:
            s = "c_" + eng
            if inc:
                self.cnt[s] += 1
                tok = (s, self.cnt[s])
                incspec = (s, 1)
            else:
                tok = (s, self.cnt[s] + 1)
                incspec = None
        waits = []
        wd = self.waited[eng]
        for s, v in deps.items():
            if v > wd.get(s, 0):
                waits.append((s, v))
                wd[s] = v
        site = ""
        if os.environ.get("KDBG"):
            f = sys._getframe(1)
            while f is not None and f.f_code.co_name != "build_program" and f.f_back is not None:
                if f.f_back.f_code.co_name == "build_program" or f.f_code.co_name in ("layer_body", "rmsnorm_mod", "proj_chunk", "shift_chunk", "squares", "compute_mod"):
                    site += f"{f.f_code.co_name}:{f.f_lineno} "
                f = f.f_back
            if f is not None:
                site += f"bp:{f.f_lineno}"
        self.ops[eng].append((waits, fn, incspec, site))
        self.n_ops += 1
        for (nm, p0, p1, f0, f1) in wregs:
            lst = self.recs.setdefault(nm, [])
            lst[:] = [r for r in lst if not (p0 <= r[0] and r[1] <= p1 and f0 <= r[2] and r[3] <= f1)]
            lst.append((p0, p1, f0, f1, "w", tok[0], tok[1], eng))
        for (nm, p0, p1, f0, f1) in rregs:
            lst = self.recs.setdefault(nm, [])
            lst[:] = [r for r in lst if not (r[4] == "r" and r[7] == eng and p0 <= r[0] and r[1] <= p1
                                             and f0 <= r[2] and r[3] <= f1)]
            lst.append((p0, p1, f0, f1, "r", tok[0], tok[1], eng))
        return tok

    def mm(self, out, lhsT, rhs, start=True, stop=True):
        return self.op("pe", lambda e: e.matmul(out, lhsT, rhs, start=start, stop=stop),
                       [lhsT, rhs], [out], inc=stop)

    def tr(self, out, in_, ident, inc=True):
        return self.op("pe", lambda e: e.transpose(out, in_, ident), [in_, ident], [out], inc=inc)

    def act(self, out, in_, func, bias=None, scale=None, eng="act"):
        kw = {}
        rd = [in_]
        if bias is not None:
            kw["bias"] = bias
            if not isinstance(bias, (int, float)):
                rd.append(bias)
        if scale is not None:
            kw["scale"] = scale
            if not isinstance(scale, (int, float)):
                rd.append(scale)
        return self.op("act", lambda e: e.activation(out=out, in_=in_, func=func, **kw), rd, [out])

    def tt(self, out, a, b, op, eng="dve"):
        return self.op(eng, lambda e: e.tensor_tensor(out=out, in0=a, in1=b, op=op), [a, b], [out])

    def ts(self, out, a, s1, s2, op0, op1=None, eng="dve"):
        rd = [a] + [s for s in (s1, s2) if s is not None and not isinstance(s, (int, float))]
        if op1 is None:
            return self.op(eng, lambda e: e.tensor_scalar(out=out, in0=a, scalar1=s1, scalar2=None, op0=op0),
                           rd, [out])
        return self.op(eng, lambda e: e.tensor_scalar(out=out, in0=a, scalar1=s1, scalar2=s2, op0=op0, op1=op1),
                       rd, [out])

    def stt(self, out, a, s, b, op0, op1, eng="dve"):
        rd = [a, b] + ([] if isinstance(s, (int, float)) else [s])
        return self.op(eng, lambda e: e.scalar_tensor_tensor(out=out, in0=a, scalar=s, in1=b, op0=op0, op1=op1),
                       rd, [out])

    def copy(self, out, in_, eng="dve"):
        if eng == "act":
            return self.op("act", lambda e: e.copy(out=out, in_=in_), [in_], [out])
        return self.op(eng, lambda e: e.tensor_copy(out=out, in_=in_), [in_], [out])

    def memset(self, out, val, eng="dve"):
        return self.op(eng, lambda e: e.memset(out, val), [], [out])

    def scan(self, out, d0, d1, init, op0, op1):
        return self.op("dve", lambda e: e.tensor_tensor_scan(out=out, data0=d0, data1=d1, initial=init,
                                                             op0=op0, op1=op1), [d0, d1], [out])

    def reduce(self, out, in_, op, axis=AX.X, eng="dve"):
        return self.op(eng, lambda e: e.tensor_reduce(out=out, in_=in_, axis=axis, op=op), [in_], [out])

    def recip(self, out, in_):
        return self.op("dve", lambda e: e.reciprocal(out=out, in_=in_), [in_], [out])

    def dma(self, out, in_, sem, eng="sp", extra_waits=(), group_final=None, **kw):
        return self.op(eng, lambda e: e.dma_start(out=out, in_=in_, **kw), [in_], [out], dma_sem=sem,
                       extra_waits=extra_waits, group_final=group_final)

    def dma_group(self, sem, reqs, eng="sp", extra_waits=()):
        final = self.cnt[sem] + 16 * len(reqs)
        for (out, in_, kw) in reqs:
            self.dma(out, in_, sem, eng=eng, extra_waits=extra_waits, group_final=final, **kw)

    def emit(self, final_waits):
        nc = self.nc
        sems = self.sems

        def run(e, ops, tail=()):
            for waits, fn, incspec, site in ops:
                for s, v in waits:
                    e.wait_ge(sems[s], v)
                ins = fn(e)
                if site:
                    ins.annotate(site)
                if incspec is not None:
                    ins.then_inc(sems[incspec[0]], incspec[1])
            for s, v in tail:
                e.wait_ge(sems[s], v)

        with nc.Block() as block:
            @block.tensor
            def _(e):
                run(e, self.ops["pe"])

            @block.scalar
            def _(e):
                run(e, self.ops["act"])

            @block.vector
            def _(e):
                run(e, self.ops["dve"])

            @block.gpsimd
            def _(e):
                run(e, self.ops["pool"])

            @block.sync
            def _(e):
                run(e, self.ops["sp"], tail=final_waits)


def build_program(S, L, T=256, NW=4, debug_taps=False, STAGE=99):
    nc = bass.Bass("TRN2", target_bir_lowering=False)
    NT = S // T
    SUB = int(os.environ.get('KSUB', '99'))
    NB = T // 128
    es = ExitStack()
    P = Prog(nc, es)

    def din(name, shape):
        return nc.dram_tensor(name, list(shape), F32, kind="ExternalInput").ap()

    Lv = max(L - 1, 1)
    x_d = din("x", (S, D))
    c_d = din("c", (1, D))
    ada_w = din("ada_w", (L, D, 6 * D))
    ada_b = din("ada_b", (L, 6 * D))
    norm1_g = din("norm1_g", (L, D))
    w_in = din("w_in", (L, D, N_IN))
    v_down = din("v_down", (Lv, D, 32))
    mu_shift = din("mu_shift", (L, N_SHIFT))
    mu_v = din("mu_v", (Lv, 32))
    w0 = din("w0", (L, DR))
    w_up = din("w_up", (L, 64, DR))
    a0 = din("a0", (L, DR))
    a_up = din("a_up", (L, 64, DR))
    v0 = din("v0", (Lv, DR))
    v_up = din("v_up", (Lv, 32, DR))
    g_up = din("g_up", (L, 160, DR))
    k_k = din("k_k", (L, DR))
    k_a = din("k_a", (L, DR))
    r_k = din("r_k", (L, DR))
    lnx_g = din("lnx_g", (L, DR))
    lnx_b = din("lnx_b", (L, DR))
    pool_w = din("pool_w", (L, 4 * 128, 128))
    pool_scale = din("pool_scale", (L, DR))
    proj_a = din("proj_a", (L, DR, D))
    proj_b = din("proj_b", (L, DR, D))
    w_out = din("w_out", (L, D, D))
    norm2_g = din("norm2_g", (L, D))
    w_gu = din("w_gu", (L, D, 2 * DFF))
    w_down = din("w_down", (L, DFF, D))
    final_g = din("final_g", (1, D))
    out_d = nc.dram_tensor("out", [S, D], F32, kind="ExternalOutput").ap()

    def dscr(name, shape):
        return nc.dram_tensor(name, list(shape), BF16, kind="Internal").ap()

    win_b = dscr("win_b", (L, D, NINX))
    wup_b = dscr("wup_b", (L, 64, DR))
    aup_b = dscr("aup_b", (L, 64, DR))
    vup_b = dscr("vup_b", (Lv, 32, DR))
    gup_b = dscr("gup_b", (L, 160, DR))
    poolw_b = dscr("poolw_b", (L, 512, 128))
    pa_b = dscr("pa_b", (L, DR, D))
    pb_b = dscr("pb_b", (L, DR, D))
    wout_b = dscr("wout_b", (L, D, D))
    wgu_b = dscr("wgu_b", (L, D, 2 * DFF))
    wdn_b = dscr("wdn_b", (L, DFF, D))

    def sb(name, shape, dt=F32):
        return es.enter_context(nc.sbuf_tensor(name, list(shape), dt))

    def ps(name, shape, dt=F32):
        return es.enter_context(nc.psum_tensor(name, list(shape), dt))

    xs = sb("xs", (128, 8, T))
    actb = sb("actb", (128, NFF, T), BF16)
    sqb = actb
    hb = sb("hb", (128, 8, T), BF16)
    rstd = sb("rstd", (128, T))
    wring = [sb(f"wr{i}", (128, 4096), BF16) for i in range(NW)]
    misc = sb("misc", (128, 6, 512), BF16)
    praw = [sb(f"praw{i}", (128, T + 1)) for i in range(3)]
    rsh = sb("rsh", (128, 4, T))
    ksh = sb("ksh", (128, 4, T))
    vsh = sb("vsh", (128, 4, T))
    vfirst = sb("vfirst", (128, 4, T))
    wab = sb("wab", (128, T), BF16)
    glb = sb("glb", (128, 2, T), BF16)
    vlb = sb("vlb", (32, T), BF16)
    ZW = 4 * (T + 16)
    PW = T + 16
    parena = sb("parena", (128, max(ZW + 2 * PW + 2 * T, 8 * T)))
    zext = parena[:, 0:ZW].rearrange("p (g n) -> p g n", g=4)
    pscr = [parena[:, ZW + i * PW:ZW + (i + 1) * PW] for i in range(2)]
    pooled = parena[:, ZW + 2 * PW:ZW + 2 * PW + 2 * T].bitcast(BF16).rearrange("p (g n) -> p g n", g=4)
    sgab = parena[:, 0:8 * T].bitcast(BF16).rearrange("p (a j n) -> p a j n", a=2, j=8)
    ybb = sb("ybb", (128, 4, T), BF16)
    rs = [sb(f"rs{i}", (128, T)) for i in range(10)]
    rsB = [sb(f"rsB{i}", (128, T)) for i in range(10)]
    rt_b = sb("rt_b", (128, 4, T), BF16)
    kt_b = sb("kt_b", (128, 4, T), BF16)
    bt_b = sb("bt_b", (128, 4, T), BF16)
    at_b = sb("at_b", (128, 4, T), BF16)
    fmtmp = [sb(f"fmtmp{i}", (128, T), BF16) for i in range(3)]
    fmtmpB = [sb(f"fmtmpB{i}", (128, T), BF16) for i in range(3)]
    atm = sb("atm", (128, NB, 512), BF16)
    kendtm = sb("kendtm", (128, NB, 512), BF16)
    bendtm = sb("bendtm", (128, NB, 512), BF16)
    vtm = sb("vtm", (128, NB, 512), BF16)
    pcs = sb("pcs", (128, NB, 4))
    rkb = sb("rkb", (128, 4, T), BF16)
    Xf = sb("Xf", (128, 8, 128), BF16)
    XTf = sb("XTf", (128, 8, 128), BF16)
    Xb = [sb(f"Xb{i}", (128, 8, 128), BF16) for i in range(2)]
    XTb = [sb(f"XTb{i}", (128, 8, 128), BF16) for i in range(2)]
    Tb = [sb(f"Tb{i}", (128, 8, 128), BF16) for i in range(2)]
    TTb2 = [sb(f"TTb{i}", (128, 8, 128), BF16) for i in range(2)]
    Qb = sb("Qb", (128, 8, 128), BF16)
    Pb = sb("Pb", (128, 8, 128), BF16)
    akT = sb("akT", (128, 8, 128), BF16)
    rbT = sb("rbT", (128, 8, 128), BF16)
    rkT = sb("rkT", (128, 8, 128), BF16)
    akv = sb("akv", (128, 512), BF16)
    vhat = sb("vhat", (128, 512))
    ahT = sb("ahT", (128, 4, 128), BF16)
    ub = sb("ub", (128, 512), BF16)
    Hs = [sb(f"Hs{l}", (128, 4, 128)) for l in range(L)]
    Hb1 = sb("Hb1", (128, 4, 128), BF16)
    Hb = [Hb1 for l in range(L)]
    hx = sb("hx", (128, 4, 128))
    ysb = sb("ysb", (128, 512))
    ysq = sb("ysq", (128, 512))
    ynb = sb("ynb", (128, 512), BF16)
    gst = sb("gst", (128, 8, 4))
    yab = sb("yab", (128, 4, T), BF16)
    ya_s = rsB[4:6]
    mixed = sb("mixed", (128, 8, T), BF16)
    sg = rsB[0:2]
    mx = rsB[2:4]
    fsc = [sb(f"fsc{i}", (128, T)) for i in range(2)]
    xstage = actb[:, :, :].rearrange("p a b -> p (a b)").bitcast(F32)[:, 0:NB * D].rearrange("p (a b) -> p a b", b=D)
    ostage = xstage
    identb = sb("identb", (128, 128), BF16)
    identf = sb("identf", (128, 128))
    onesblk = sb("onesblk", (128, 128), BF16)
    onesmean = sb("onesmean", (128, 128), BF16)
    m_strict = sb("m_strict", (128, 4, 128), BF16)
    m_incl = sb("m_incl", (128, 4, 128), BF16)
    m_lower = sb("m_lower", (128, 4, 128), BF16)
    m_bd = sb("m_bd", (128, 4, 128), BF16)
    scanmask = sb("scanmask", (128, T))
    iota_p = sb("iota_p", (128, 1))
    iota_f = sb("iota_f", (128, 128))
    invc0 = sb("invc0", (128, 4, 16))
    cact = sb("cact", (128, 8))
    NCOL = 96
    cols = [sb(f"cols{l}", (128, NCOL)) for l in range(L)]
    fgc = sb("fgc", (128, 8))
    carry = [sb(f"carry{l}", (128, 16)) for l in range(L)]
    zcarry = [sb(f"zcarry{l}", (128, 4, 16)) for l in range(L)]
    ADN = NB * D // 8
    adast = xstage[:, :, :].rearrange("p a b -> p (a b)").rearrange("p (k n) -> p k n", k=8)

    pbank = [ps(f"pb{i}", (128, 512)) for i in range(8)]
    pctr = [0]

    def bank():
        b = pbank[pctr[0] % 8]
        pctr[0] += 1
        return b

    def sidx(h):
        return (h % 2) * 4 + h // 2

    def bank_bf(b):
        return b[:, :].bitcast(BF16)

    C_G1, C_G2 = 0, 8
    C_SH1, C_SH2 = 16, 24
    C_GT1, C_GT2 = 32, 40
    C_MU = 48
    C_MUV = 63
    C_W0, C_A0, C_V0, C_KK, C_KA, C_RK, C_LG, C_LB = 64, 68, 72, 76, 80, 84, 88, 92
    cols2 = [sb(f"cols2_{l}", (128, 16)) for l in range(L)]

    sem_cv = P.newsem("d_cv")
    sem_cst = P.newsem("d_cst")
    sem_x = [P.newsem(f"d_x{i}") for i in range(NB)]
    sem_o = [P.newsem(f"d_o{i}") for i in range(NB)]
    sem_w = [P.newsem(f"d_w{i}") for i in range(NW)]
    sem_misc = P.newsem("d_misc")
    sem_ada2 = [P.newsem("d_ada0"), P.newsem("d_ada1")]

    P.op("pool", lambda e: e.iota(iota_p[:, :], pattern=[[0, 1]], base=0, channel_multiplier=1, allow_small_or_imprecise_dtypes=True), [], [iota_p[:, :]])
    P.op("pool", lambda e: e.iota(iota_f[:, :], pattern=[[1, 128]], base=0, channel_multiplier=0, allow_small_or_imprecise_dtypes=True), [], [iota_f[:, :]])
    P.ts(identf[:, :], iota_f[:, :], iota_p[:, 0:1], None, ALU.is_equal)
    P.copy(identb[:, :], identf[:, :])
    P.ts(m_strict[:, 0, :], iota_f[:, :], iota_p[:, 0:1], None, ALU.is_gt)
    P.ts(m_incl[:, 0, :], iota_f[:, :], iota_p[:, 0:1], None, ALU.is_ge)
    P.ts(m_lower[:, 0, :], iota_f[:, :], iota_p[:, 0:1], None, ALU.is_lt)
    P.ts(rs[0][:, 0:128], iota_f[:, :], 64.0, None, ALU.is_ge)
    P.ts(rs[1][:, 0:1], iota_p[:, 0:1], 64.0, None, ALU.is_ge)
    P.ts(m_bd[:, 0, :], rs[0][:, 0:128], rs[1][:, 0:1], None, ALU.is_equal)
    for m in (m_strict, m_incl, m_lower, m_bd):
        for j in range(1, 4):
            P.copy(m[:, j, :], m[:, 0, :])
    P.copy(onesblk[:, :], m_bd[:, 0, :])
    HB = (8, 16, 32, 64)
    scr = [rs[i][:, 0:128] for i in range(10)] + [rsB[i][:, 0:128] for i in range(10)]
    Fb, PBt = {}, {}
    si = 0
    for b_ in HB:
        F = scr[si]
        si += 1
        P.op("pool", lambda e, F=F, b_=b_: e.iota(F, pattern=[[1, 128 // b_], [0, b_]], base=0, channel_multiplier=0,
                                                  allow_small_or_imprecise_dtypes=True), [], [F])
        Fb[b_] = F
        bk = bank()
        P.tr(bk[:, 0:128], F, identf[:, :])
        PB = scr[si]
        si += 1
        P.copy(PB, bk[:, 0:128])
        PBt[b_] = PB
    t1, t2 = scr[si], scr[si + 1]
    mD_l = sb("mD_l", (128, 128), BF16)
    mD_u = sb("mD_u", (128, 128), BF16)
    P.tt(t1, Fb[8], PBt[8], ALU.is_equal)
    P.tt(mD_l[:, :], t1, m_lower[:, 0, :], ALU.mult)
    P.tt(mD_u[:, :], t1, m_strict[:, 0, :], ALU.mult)
    ML, MU = {}, {}
    for b_ in HB:
        if b_ < 64:
            P.tt(t1, Fb[2 * b_], PBt[2 * b_], ALU.is_equal)
        else:
            P.memset(t1, 1.0)
        if b_ < 64:
            ML[b_] = sb(f"ML{b_}", (128, 128), BF16)
            P.tt(t2, Fb[b_], PBt[b_], ALU.is_lt)
            P.tt(ML[b_][:, :], t1, t2, ALU.mult)
        MU[b_] = sb(f"MU{b_}", (128, 128), BF16)
        P.tt(t2, Fb[b_], PBt[b_], ALU.is_gt)
        P.tt(MU[b_][:, :], t1, t2, ALU.mult)

    def bc4(m):
        return m[:, :].unsqueeze(1).broadcast_to([128, 4, 128])
    P.memset(onesmean[:, :], 1.0 / D)
    P.memset(scanmask[:, :], 1.0)
    for j in range(NB):
        P.memset(scanmask[:, j * CH:j * CH + 1], 0.0)
    for gi, w in enumerate(POOL_W):
        P.memset(invc0[:, gi, :], 1.0 / w)
        for t in range(w - 1):
            P.memset(invc0[:, gi, t:t + 1], 1.0 / (t + 1))
    P.memset(pscr[0][:, :], 0.0)
    P.memset(pscr[1][:, :], 0.0)
    for l in range(L):
        P.memset(carry[l][:, :], 0.0)
        P.memset(zcarry[l][:, :, :], 0.0)
        P.memset(Hs[l][:, :, :], 0.0)

    cst_reqs = []
    NC_KW = dict(allow_slow_non_contiguous=True)

    def colload(dst, src_row, n):
        for k0 in range(0, n // 128, 8):
            k1 = min(k0 + 8, n // 128)
            cst_reqs.append((dst[:, k0:k1], src_row[k0 * 128:k1 * 128].rearrange("(k p) -> p k", p=128), NC_KW))

    cst_reqs.append((cact[:, :], c_d[0].rearrange("(k p) -> p k", p=128), NC_KW))
    cst_reqs.append((fgc[:, :], final_g[0].rearrange("(k p) -> p k", p=128), NC_KW))
    mcols = [sb(f"mcol{l}", (128, 48)) for l in range(L)]
    for l in range(L):
        cl = cols[l]
        c2 = cols2[l]
        colload(c2[:, 8:16], norm1_g[l], D)
        colload(cl[:, C_G2:C_G2 + 8], norm2_g[l], D)
        colload(cl[:, C_MU:C_MU + 14], mu_shift[l, 0:1792], 1792)
        cst_reqs.append((cl[0:32, C_MU + 14:C_MU + 15], mu_shift[l, 1792:1824].rearrange("(p k) -> p k", k=1), NC_KW))
        if l >= 1:
            cst_reqs.append((cl[0:32, C_MUV:C_MUV + 1], mu_v[l - 1].rearrange("(p k) -> p k", k=1), NC_KW))
            colload(cl[:, C_V0:C_V0 + 4], v0[l - 1], DR)
        colload(cl[:, C_W0:C_W0 + 4], w0[l], DR)
        colload(cl[:, C_A0:C_A0 + 4], a0[l], DR)
        colload(cl[:, C_KK:C_KK + 4], k_k[l], DR)
        colload(cl[:, C_KA:C_KA + 4], k_a[l], DR)
        colload(cl[:, C_RK:C_RK + 4], r_k[l], DR)
        colload(cl[:, C_LG:C_LG + 4], lnx_g[l], DR)
        colload(cl[:, C_LB:C_LB + 4], lnx_b[l], DR)
        colload(c2[:, 0:4], pool_scale[l], DR)
        colload(mcols[l][:, :], ada_b[l], 6 * D)
    for l in range(L):
        P.memset(cols[l][:, :], 0.0)
    P.dma_group(sem_cst, cst_reqs)
    sem_cvl = [P.newsem(f"d_cv{l}") for l in range(L)]
    curl = [0]

    def cast(dst, src):
        P.op("pool", lambda e: e.dma_start(out=dst, in_=src), [], [], dma_sem=sem_cvl[curl[0]])

    for l in range(L):
        curl[0] = l
        for i in range(8):
            cast(win_b[l, i * 128:(i + 1) * 128, 0:N_IN], w_in[l, i * 128:(i + 1) * 128, :])
        if l >= 1:
            cast(win_b[l, :, N_IN:NINX], v_down[l - 1])
            cast(vup_b[l - 1], v_up[l - 1])
        cast(wup_b[l], w_up[l])
        cast(aup_b[l], a_up[l])
        cast(gup_b[l], g_up[l])
        cast(poolw_b[l], pool_w[l])
        for i in range(4):
            cast(pa_b[l, i * 128:(i + 1) * 128, :], proj_a[l, i * 128:(i + 1) * 128, :])
            cast(pb_b[l, i * 128:(i + 1) * 128, :], proj_b[l, i * 128:(i + 1) * 128, :])
        for i in range(8):
            cast(wout_b[l, i * 128:(i + 1) * 128, :], w_out[l, i * 128:(i + 1) * 128, :])
            cast(wgu_b[l, i * 128:(i + 1) * 128, :], w_gu[l, i * 128:(i + 1) * 128, :])
        for i in range(NFF):
            cast(wdn_b[l, i * 128:(i + 1) * 128, :], w_down[l, i * 128:(i + 1) * 128, :])
    CVL = [[(sem_cvl[l], P.cnt[sem_cvl[l]])] for l in range(L)]

    for l in range(L):
        P.ts(cols2[l][:, 4:8], cols[l][:, C_KA:C_KA + 4], -1.0, 1.0, ALU.mult, ALU.add)
    P.act(cact[:, :], cact[:, :], AF.Silu)
    def layer_units(l):
        u = []
        wl = win_b[l]

        def kview(w, c0, n):
            return w[:, c0:c0 + n].rearrange("(k p) n -> p k n", p=128)
        u.append(("r", [(kview(wl, 0, 512), 8, 512, 0)]))
        u.append(("k", [(kview(wl, 512, 512), 8, 512, 0)]))
        u.append(("v", [(kview(wl, 1024, 512), 8, 512, 0)]))
        u.append(("lora", [(kview(wl, 1536, 288), 8, 512, 0)]))
        if l >= 1:
            u.append(("vl", [(kview(wl, N_IN, 32), 8, 32, 0)]))
        u.append(("pool", [(kview(wl, 1824, 512), 8, 512, 0)]))
        u.append(("ga0", [(kview(wl, 2336, 512), 8, 512, 0)]))
        u.append(("gb0", [(kview(wl, 3360, 512), 8, 512, 0)]))
        u.append(("ga1", [(kview(wl, 2336 + 512, 512), 8, 512, 0)]))
        u.append(("gb1", [(kview(wl, 3360 + 512, 512), 8, 512, 0)]))
        u.append(("pa", [(pa_b[l].rearrange("(k p) n -> p k n", p=128), 4, 1024, 0)]))
        u.append(("pb", [(pb_b[l].rearrange("(k p) n -> p k n", p=128), 4, 1024, 0)]))
        for h in range(2):
            u.append((f"wo{h}", [(kview(wout_b[l], h * 512, 512), 8, 512, 0)]))
        for f0 in range(0, NFF, 4):
            n = min(4, NFF - f0) * 128
            u.append((f"fg{f0}", [(kview(wgu_b[l], f0 * 128, n), 8, 512, 0)]))
            u.append((f"fu{f0}", [(kview(wgu_b[l], DFF + f0 * 128, n), 8, 512, 0)]))
        for h in range(2):
            for k0 in range(0, NFF, 8):
                nk = min(8, NFF - k0)
                src = wdn_b[l, k0 * 128:(k0 + nk) * 128, h * 512:(h + 1) * 512].rearrange("(k p) n -> p k n", p=128)
                u.append((f"fd{h}_{k0}", [(src, nk, 512, 0)]))
        return u

    units = []
    for i in range(NT):
        for l in range(L):
            units += [(l, nm, lst) for nm, lst in layer_units(l)]
    wstate = {"loaded": 0, "cur": 0}

    def wload_next():
        n = wstate["loaded"]
        if n >= len(units):
            return
        l, nm, lst = units[n]
        slot = wring[n % NW]
        reqs = []
        for (src, nk, ncol, c0) in lst:
            w = src.shape[2]
            dst = slot[:, 0:nk * ncol].rearrange("p (k n) -> p k n", n=ncol)[:, :, c0:c0 + w]
            reqs.append((dst, src, {}))
        P.dma_group(sem_w[n % NW], reqs, extra_waits=CVL[l])
        wstate["loaded"] = n + 1

    for _ in range(NW):
        wload_next()

    def wget(l, nm):
        for n in range(wstate["cur"], min(wstate["cur"] + NW, len(units))):
            ul, unm, lst = units[n]
            if (ul, unm) == (l, nm):
                assert n < wstate["loaded"]
                return wring[n % NW]
        raise AssertionError((l, nm, wstate))

    def wdone():
        wstate["cur"] += 1
        wload_next()

    def load_misc(l):
        reqs = [(misc[0:64, 0, :], wup_b[l], {}), (misc[64:128, 1, :], aup_b[l], {})]
        if l >= 1:
            reqs.append((misc[0:32, 2, :], vup_b[l - 1], {}))
        reqs.append((misc[:, 3, :], gup_b[l, 0:128, :], {}))
        reqs.append((misc[0:32, 4, :], gup_b[l, 128:160, :], {}))
        reqs.append((misc[:, 5, :].rearrange("p (g n) -> p g n", n=128),
                     poolw_b[l].rearrange("(g p) n -> p g n", p=128), {}))
        P.dma_group(sem_misc, reqs, extra_waits=CVL[l])

    def compute_mod(l):
        cl = cols[l]
        mcol = mcols[l]
        xflat = xstage[:, :, :].rearrange("p a b -> p (a b)")
        HN = 128
        halves = [xflat[:, h * 8 * HN:(h + 1) * 8 * HN].rearrange("p (k n) -> p k n", k=8) for h in range(2)]
        rowt = rs[9]
        for g in range(6 * D // HN):
            st = halves[g % 2]
            P.dma(st, ada_w[l, :, g * HN:(g + 1) * HN].rearrange("(k p) n -> p k n", p=128), sem_ada2[g % 2])
            b = bank()
            for k in range(8):
                P.mm(b[0:1, 0:HN], cact[:, k:k + 1], st[:, k, :], start=(k == 0), stop=(k == 7))
            P.copy(rowt[0:1, 0:HN], b[0:1, 0:HN], eng="act")
            b2 = bank()
            P.mm(b2[:, 0:1], rowt[0:1, 0:HN], identf[0:1, 0:1])
            P.tt(mcol[:, g:g + 1], mcol[:, g:g + 1], b2[:, 0:1], ALU.add)
        P.copy(cl[:, C_SH1:C_SH1 + 8], mcol[:, 0:8])
        P.copy(cl[:, C_GT1:C_GT1 + 8], mcol[:, 16:24])
        P.copy(cl[:, C_SH2:C_SH2 + 8], mcol[:, 24:32])
        P.copy(cl[:, C_GT2:C_GT2 + 8], mcol[:, 40:48])
        P.stt(cl[:, C_G1:C_G1 + 8], mcol[:, 8:16], 1.0, cols2[l][:, 8:16], ALU.add, ALU.mult)
        P.stt(cl[:, C_G2:C_G2 + 8], mcol[:, 32:40], 1.0, cl[:, C_G2:C_G2 + 8], ALU.add, ALU.mult)


    eps_t = sb("eps_t", (128, 3))
    P.memset(eps_t[:, 2:3], 1e-24)
    P.memset(eps_t[:, 0:1], EPS_RMS)
    P.memset(eps_t[:, 1:2], EPS_GN)

    def squares():
        for k in range(8):
            if k % 2 == 0:
                P.act(sqb[:, k, :], xs[:, k, :], AF.Square)
            else:
                P.tt(sqb[:, k, :], xs[:, k, :], xs[:, k, :], ALU.mult, eng=("dve" if k % 4 == 1 else "pool"))

    def rmsnorm_mod(gcol, shcol, cl):
        squares()
        b = bank()
        for k in range(8):
            P.mm(b[:, 0:T], onesmean[:, :], sqb[:, k, :], start=(k == 0), stop=(k == 7))
        P.act(rstd[:, :], b[:, 0:T], AF.Ln, bias=eps_t[:, 0:1])
        P.act(rstd[:, :], rstd[:, :], AF.Exp, scale=-0.5)
        for k in range(8):
            sc = fsc[k % 2]
            P.stt(sc[:, :], xs[:, k, :], cl[:, gcol + k:gcol + k + 1], rstd[:, :], ALU.mult, ALU.mult)
            P.act(hb[:, k, :], sc[:, :], AF.Identity, bias=cl[:, shcol + k:shcol + k + 1])

    def proj_chunk(slot, c0, ncol, nk=8, kstride=512, rhs=None):
        b = bank()
        sv = slot[:, 0:nk * kstride].rearrange("p (k n) -> p k n", n=kstride)
        for k in range(nk):
            P.mm(b[0:ncol, 0:T], sv[:, k, c0:c0 + ncol], (hb if rhs is None else rhs)[:, k, :],
                 start=(k == 0), stop=(k == nk - 1))
        return b

    pr_ctr = [0]

    def shift_chunk(b, nrow, mucol, cidx, cl, l, first_tile, out_ap=None, post=None):
        pr = praw[pr_ctr[0] % 3]
        pr_ctr[0] += 1
        P.copy(pr[0:nrow, 0:1], carry[l][0:nrow, cidx:cidx + 1], eng="pool")
        P.copy(pr[0:nrow, 1:T + 1], b[0:nrow, 0:T], eng="act")
        P.copy(carry[l][0:nrow, cidx:cidx + 1], pr[0:nrow, T:T + 1], eng="pool")
        d = fsc[pr_ctr[0] % 2]
        P.tt(d[0:nrow, :], pr[0:nrow, 0:T], pr[0:nrow, 1:T + 1], ALU.subtract)
        P.stt(out_ap, d[0:nrow, :], cl[0:nrow, mucol:mucol + 1], pr[0:nrow, 1:T + 1], ALU.mult, ALU.add)

    n_out = [0]
    for ti in range(NT):
        t0 = ti * T
        for tb in range(NB):
            P.dma(xstage[:, tb, :], x_d[t0 + tb * 128:t0 + (tb + 1) * 128, :], sem_x[tb])
        for k in range(8):
            b = bank()
            for tb in range(NB):
                P.tr(b[:, tb * 128:(tb + 1) * 128], xstage[:, tb, k * 128:(k + 1) * 128], identf[:, :],
                     inc=(tb == NB - 1))
            P.copy(xs[:, k, :], b[:, 0:T], eng=("act" if k % 2 else "dve"))

        def layer_body(l):
            done0 = wstate['cur']
            nunits = len(layer_units(l))
            def skip_rest():
                while wstate['cur'] < done0 + nunits:
                    wdone()
            cl = cols[l]
            c2 = cols2[l]
            if ti == 0:
                compute_mod(l)
            load_misc(l)
            rmsnorm_mod(C_G1, C_SH1, cl)
            for nm, dstbuf, cbase in (("r", rsh, 0), ("k", ksh, 4), ("v", vsh, 8)):
                slot = wget(l, nm)
                for j in range(4):
                    b = proj_chunk(slot, j * 128, 128)
                    shift_chunk(b, 128, C_MU + cbase + j, cbase + j, cl, l, ti == 0, out_ap=dstbuf[:, j, :])
                wdone()
            slot = wget(l, "lora")
            b = proj_chunk(slot, 0, 128)
            shift_chunk(b, 128, C_MU + 12, 12, cl, l, ti == 0, out_ap=rs[0][:, :])
            P.act(wab[0:64, :], rs[0][0:64, :], AF.Tanh)
            P.copy(wab[64:128, :], rs[0][64:128, :], eng="dve")
            b = proj_chunk(slot, 128, 128)
            shift_chunk(b, 128, C_MU + 13, 13, cl, l, ti == 0, out_ap=rs[1][:, :])
            P.act(glb[:, 0, :], rs[1][:, :], AF.Sigmoid)
            b = proj_chunk(slot, 256, 32)
            shift_chunk(b, 32, C_MU + 14, 14, cl, l, ti == 0, out_ap=rs[2][0:32, :])
            P.act(glb[0:32, 1, :], rs[2][0:32, :], AF.Sigmoid)
            wdone()
            if l >= 1:
                slot = wget(l, "vl")
                b = proj_chunk(slot, 0, 32, kstride=32)
                shift_chunk(b, 32, C_MUV, 15, cl, l, ti == 0, out_ap=rs[3][0:32, :])
                P.copy(vlb[0:32, :], rs[3][0:32, :], eng="dve")
                wdone()
            if STAGE < 4:
                skip_rest()
                return
            slot = wget(l, "pool")
            for g in range(4):
                b = proj_chunk(slot, g * 128, 128)
                P.copy(zext[:, g, 0:16], zcarry[l][:, g, :], eng="pool")
                P.copy(zext[:, g, 16:16 + T], b[:, 0:T], eng="act")
                P.copy(zcarry[l][:, g, :], zext[:, g, T:T + 16], eng="pool")
            wdone()
            for g, w in enumerate(POOL_W):
                src = zext[:, g, :]
                sh = 1
                n = 0
                W_ = T + 16
                while sh < w:
                    dst = pscr[n % 2]
                    P.tt(dst[:, sh:W_], src[:, sh:W_], src[:, 0:W_ - sh], ALU.add, eng="pool")
                    src = dst
                    sh *= 2
                    n += 1
                P.stt(pooled[:, g, :], src[:, 16:16 + T], 1.0 / w, zext[:, g, 16:16 + T], ALU.mult, ALU.subtract)
                if ti == 0:
                    P.tt(rs[4][:, 0:16], src[:, 16:32], invc0[:, g, :], ALU.mult)
                    P.tt(pooled[:, g, 0:16], rs[4][:, 0:16], zext[:, g, 16:32], ALU.subtract)
            pw = misc[:, 5, :].rearrange("p (g n) -> p g n", n=128)
            for g in range(4):
                b = bank()
                P.mm(b[:, 0:T], pw[:, g, :], pooled[:, g, :])
                P.act(ybb[:, g, :], b[:, 0:T], AF.Identity, scale=c2[:, g:g + 1])

            if STAGE < 5:
                skip_rest()
                return
            for cp in range(2):
                cs = (2 * cp, 2 * cp + 1)
                RS = {cs[0]: rs, cs[1]: rsB}
                FM = {cs[0]: fmtmp, cs[1]: fmtmpB}

                def col(base, c):
                    return cl[:, base + c:base + c + 1]
                bsw, bav, bvg, b4 = {}, {}, {}, {}
                for c in cs:
                    bsw[c] = bank()
                    P.mm(bsw[c][:, 0:T], misc[0:64, 0, c * 128:(c + 1) * 128], wab[0:64, :])
                for c in cs:
                    bav[c] = bank()
                    P.mm(bav[c][:, 0:T], misc[64:128, 1, c * 128:(c + 1) * 128], wab[64:128, :])
                if l >= 1:
                    for c in cs:
                        bvg[c] = bank()
                        P.mm(bvg[c][:, 0:T], misc[0:32, 2, c * 128:(c + 1) * 128], vlb[0:32, :])
                for c in cs:
                    sw, cum, cumx, Ep, Em, Epv, kkn, av, mm_, bb = RS[c]
                    P.act(sw[:, :], bsw[c][:, 0:T], AF.Sigmoid, bias=col(C_W0, c))
                for c in cs:
                    sw, cum, cumx, Ep, Em, Epv, kkn, av, mm_, bb = RS[c]
                    P.act(av[:, :], bav[c][:, 0:T], AF.Sigmoid, bias=col(C_A0, c))
                if l >= 1:
                    for c in cs:
                        sw, cum, cumx, Ep, Em, Epv, kkn, av, mm_, bb = RS[c]
                        P.act(mm_[:, :], bvg[c][:, 0:T], AF.Sigmoid, bias=col(C_V0, c))
                for c in cs:
                    P.act(FM[c][0][:, :], ksh[:, c, :], AF.Square, scale=col(C_KK, c))
                for c in cs:
                    sw, cum, cumx, Ep, Em, Epv, kkn, av, mm_, bb = RS[c]
                    P.scan(cum[:, :], scanmask[:, :], sw[:, :], 0.0, ALU.mult, ALU.add)
                for c in cs:
                    sw, cum, cumx, Ep, Em, Epv, kkn, av, mm_, bb = RS[c]
                    P.tt(cumx[:, :], cum[:, :], sw[:, :], ALU.subtract, eng="pool")
                for c in cs:
                    b4[c] = bank()
                    P.mm(b4[c][:, 0:T], onesblk[:, :], FM[c][0][:, :])
                for c in cs:
                    sw, cum, cumx, Ep, Em, Epv, kkn, av, mm_, bb = RS[c]
                    if l == 0:
                        P.copy(vfirst[:, c, :], vsh[:, c, :], eng="pool")
                    else:
                        P.tt(bb[:, :], vfirst[:, c, :], vsh[:, c, :], ALU.subtract, eng="pool")
                        P.tt(bb[:, :], bb[:, :], mm_[:, :], ALU.mult)
                        P.tt(vsh[:, c, :], vsh[:, c, :], bb[:, :], ALU.add, eng="pool")
                for c in cs:
                    sw, cum, cumx, Ep, Em, Epv, kkn, av, mm_, bb = RS[c]
                    P.act(Ep[:, :], cum[:, :], AF.Exp, scale=KDEC)
                    P.act(Em[:, :], cum[:, :], AF.Exp, scale=-KDEC)
                    P.act(Epv[:, :], cumx[:, :], AF.Exp, scale=KDEC)
                for c in cs:
                    sw, cum, cumx, Ep, Em, Epv, kkn, av, mm_, bb = RS[c]
                    P.act(kkn[:, :], b4[c][:, 0:T], AF.Ln, bias=eps_t[:, 2:3])
                    P.act(kkn[:, :], kkn[:, :], AF.Exp, scale=-0.5)
                slot_ga = wget(l, f"ga{cp}")
                slot_gb = wget(l, f"gb{cp}")
                for jj in range(4):
                    jo = cp * 4 + jj
                    bga = proj_chunk(slot_ga, jj * 128, 128)
                    bgb = proj_chunk(slot_gb, jj * 128, 128)
                    P.act(sgab[:, 0, jo, :], bga[:, 0:T], AF.Sigmoid)
                    P.act(sgab[:, 1, jo, :], bgb[:, 0:T], AF.Sigmoid)
                wdone()
                wdone()
                for c in cs:
                    sw, cum, cumx, Ep, Em, Epv, kkn, av, mm_, bb = RS[c]
                    for j in range(NB):
                        P.copy(pcs[:, j, c:c + 1], Ep[:, (j + 1) * CH - 1:(j + 1) * CH], eng="pool")
                for c in cs:
                    sw, cum, cumx, Ep, Em, Epv, kkn, av, mm_, bb = RS[c]
                    P.stt(kkn[:, :], ksh[:, c, :], col(C_KK, c), kkn[:, :], ALU.mult, ALU.mult)
                for c in cs:
                    sw, cum, cumx, Ep, Em, Epv, kkn, av, mm_, bb = RS[c]
                    P.ts(mm_[:, :], av[:, :], col(C_KA, c), c2[:, 4 + c:5 + c], ALU.mult, ALU.add)
                    P.tt(mm_[:, :], mm_[:, :], ksh[:, c, :], ALU.mult)
                    P.stt(rkb[:, c, :], rsh[:, c, :], col(C_RK, c), mm_[:, :], ALU.mult, ALU.mult)
                for c in cs:
                    sw, cum, cumx, Ep, Em, Epv, kkn, av, mm_, bb = RS[c]
                    P.tt(bb[:, :], av[:, :], kkn[:, :], ALU.mult, eng="pool")
                for c in cs:
                    sw, cum, cumx, Ep, Em, Epv, kkn, av, mm_, bb = RS[c]
                    P.tt(rt_b[:, c, :], rsh[:, c, :], Ep[:, :], ALU.mult)
                    P.tt(kt_b[:, c, :], mm_[:, :], Em[:, :], ALU.mult)
                    P.stt(at_b[:, c, :], kkn[:, :], -1.0, Epv[:, :], ALU.mult, ALU.mult)
                for c in cs:
                    sw, cum, cumx, Ep, Em, Epv, kkn, av, mm_, bb = RS[c]
                    P.tt(bt_b[:, c, :], bb[:, :], Em[:, :], ALU.mult, eng="pool")
                for c in cs:
                    sw, cum, cumx, Ep, Em, Epv, kkn, av, mm_, bb = RS[c]
                    for j in range(NB):
                        js = slice(j * CH, (j + 1) * CH)
                        P.ts(cumx[:, js], Em[:, js], pcs[:, j, c:c + 1], None, ALU.mult)
                    P.tt(FM[c][0][:, :], mm_[:, :], cumx[:, :], ALU.mult)
                for c in cs:
                    sw, cum, cumx, Ep, Em, Epv, kkn, av, mm_, bb = RS[c]
                    P.tt(FM[c][1][:, :], bb[:, :], cumx[:, :], ALU.mult, eng="pool")
                    P.copy(FM[c][2][:, :], vsh[:, c, :], eng="pool")
                for c in cs:
                    for srcb, dstb in ((at_b[:, c, :], atm), (FM[c][0][:, :], kendtm), (FM[c][1][:, :], bendtm),
                                       (FM[c][2][:, :], vtm)):
                        bt = bank_bf(bank())
                        for j in range(NB):
                            P.tr(bt[:, j * 128:(j + 1) * 128], srcb[:, j * CH:(j + 1) * CH], identb[:, :],
                                 inc=(j == NB - 1))
                        P.copy(dstb[:, :, c * 128:(c + 1) * 128],
                               bt[:, 0:NB * 128].rearrange("p (j n) -> p j n", n=128), eng="act")

            if STAGE < 6:
                skip_rest()
                return
            H = Hs[l]
            Hbf = Hb[l]
            P.copy(Hbf[:, :, :], H[:, :, :], eng="pool")
            for j in range(NB):
                js = slice(j * CH, (j + 1) * CH)
                X, XT = Xf, XTf
                for hg in range(2):
                    bX, bXT, bAK, bRB, bRK = bank(), bank(), bank(), bank(), bank()
                    for hh in range(4):
                        h = 2 * hh + hg
                        c, hp = h // 2, (h % 2) * 64
                        A_ = at_b[hp:hp + 64, c, js]
                        B_ = bt_b[hp:hp + 64, c, js]
                        K_ = kt_b[hp:hp + 64, c, js]
                        R_ = rt_b[hp:hp + 64, c, js]
                        o = slice(hh * 128, (hh + 1) * 128)
                        P.mm(bX[:, o], A_, B_)
                        P.mm(bXT[:, o], B_, A_)
                        P.mm(bAK[:, o], K_, A_)
                        P.mm(bRB[:, o], B_, R_)
                        P.mm(bRK[:, o], K_, R_)
                    hsl = slice(hg * 4, hg * 4 + 4)
                    P.tt(X[:, hsl, :], bX[:, :].rearrange("p (h n) -> p h n", n=128), m_lower[:, :, :], ALU.mult)
                    P.tt(XT[:, hsl, :], bXT[:, :].rearrange("p (h n) -> p h n", n=128), m_strict[:, :, :], ALU.mult)
                    P.tt(Xb[0][:, hsl, :], bX[:, :].rearrange("p (h n) -> p h n", n=128), bc4(mD_l), ALU.mult)
                    P.tt(XTb[0][:, hsl, :], bXT[:, :].rearrange("p (h n) -> p h n", n=128), bc4(mD_u), ALU.mult)
                    P.tt(akT[:, hsl, :], bAK[:, :].rearrange("p (h n) -> p h n", n=128), m_strict[:, :, :], ALU.mult)
                    P.tt(rbT[:, hsl, :], bRB[:, :].rearrange("p (h n) -> p h n", n=128), m_incl[:, :, :], ALU.mult)
                    P.tt(rkT[:, hsl, :], bRK[:, :].rearrange("p (h n) -> p h n", n=128), m_incl[:, :, :], ALU.mult)
                idb8 = identb[:, :].unsqueeze(1).broadcast_to([128, 8, 128])
                P.tt(Tb[0][:, :, :], Xb[0][:, :, :], idb8, ALU.add, eng="pool")
                P.tt(TTb2[0][:, :, :], XTb[0][:, :, :], idb8, ALU.add, eng="pool")
                tcur = 0

                def v4(bk_):
                    return bk_[:, :].rearrange("p (h n) -> p h n", n=128)

                def acc_group(bk_, o, lhs, rhs_t, base=None):
                    P.mm(bk_[:, o], identb[:, :], rhs_t if base is None else base, start=True, stop=False)
                    P.mm(bk_[:, o], lhs, rhs_t, start=False, stop=True)
                for lev in (1, 2):
                    Xc, XTc, Xn, XTn = Xb[(lev - 1) % 2], XTb[(lev - 1) % 2], Xb[lev % 2], XTb[lev % 2]
                    Tc, TTc, Tn, TTn = Tb[tcur], TTb2[tcur], Tb[1 - tcur], TTb2[1 - tcur]
                    for hg in range(2):
                        hsl = slice(hg * 4, hg * 4 + 4)
                        bX = bank()
                        for hh in range(4):
                            h = hg * 4 + hh
                            P.mm(bX[:, hh * 128:(hh + 1) * 128], XTc[:, h, :], Xc[:, h, :])
                        P.copy(Xn[:, hsl, :], v4(bX), eng="act")
                        bXT = bank()
                        for hh in range(4):
                            h = hg * 4 + hh
                            P.mm(bXT[:, hh * 128:(hh + 1) * 128], Xc[:, h, :], XTc[:, h, :])
                        P.copy(XTn[:, hsl, :], v4(bXT), eng="dve")
                        bT = bank()
                        for hh in range(4):
                            h = hg * 4 + hh
                            acc_group(bT, slice(hh * 128, (hh + 1) * 128), XTn[:, h, :], Tc[:, h, :])
                        P.copy(Tn[:, hsl, :], v4(bT), eng="act")
                        bTT = bank()
                        for hh in range(4):
                            h = hg * 4 + hh
                            acc_group(bTT, slice(hh * 128, (hh + 1) * 128), Xn[:, h, :], TTc[:, h, :])
                        P.copy(TTn[:, hsl, :], v4(bTT), eng="dve")
                    tcur = 1 - tcur
                for b_ in HB:
                    Tc, TTc, Tn, TTn = Tb[tcur], TTb2[tcur], Tb[1 - tcur], TTb2[1 - tcur]
                    last = (b_ == 64)
                    for hg in range(2):
                        hsl = slice(hg * 4, hg * 4 + 4)
                        if not last:
                            bQ = bank()
                            for hh in range(4):
                                h = hg * 4 + hh
                                P.mm(bQ[:, hh * 128:(hh + 1) * 128], XTf[:, h, :], Tc[:, h, :])
                            P.tt(Qb[:, hsl, :], v4(bQ), bc4(ML[b_]), ALU.mult)
                        bP = bank()
                        for hh in range(4):
                            h = hg * 4 + hh
                            P.mm(bP[:, hh * 128:(hh + 1) * 128], Xf[:, h, :], TTc[:, h, :])
                        P.tt(Pb[:, hsl, :], v4(bP), bc4(MU[b_]), ALU.mult)
                        if not last:
                            bT = bank()
                            for hh in range(4):
                                h = hg * 4 + hh
                                acc_group(bT, slice(hh * 128, (hh + 1) * 128), TTc[:, h, :], Qb[:, h, :], base=Tc[:, h, :])
                            P.copy(Tn[:, hsl, :], v4(bT), eng="act")
                        bTT = bank()
                        for hh in range(4):
                            h = hg * 4 + hh
                            acc_group(bTT, slice(hh * 128, (hh + 1) * 128), Tc[:, h, :], Pb[:, h, :], base=TTc[:, h, :])
                        P.copy(TTn[:, hsl, :], v4(bTT), eng=("act" if last and hg else "dve"))
                    tcur = 1 - tcur
                TT = TTb2[tcur]
                bA = bank()
                for h in range(8):
                    P.mm(bA[:, h * 64:(h + 1) * 64], akT[:, sidx(h), :], vtm[:, j, h * 64:(h + 1) * 64])
                P.copy(akv[:, :], bA[:, :], eng="act")
                bV = bank()
                for h in range(8):
                    P.mm(bV[:, h * 64:(h + 1) * 64], TT[:, sidx(h), :], akv[:, h * 64:(h + 1) * 64])
                P.copy(vhat[:, :], bV[:, :], eng="dve")
                if SUB < 4:
                    continue
                for hg in range(2):
                    bH = bank()
                    for hh in range(4):
                        c = hh
                        P.mm(bH[:, hh * 128:(hh + 1) * 128], atm[:, j, c * 128:(c + 1) * 128], TT[:, hg * 4 + hh, :])
                    rows = slice(hg * 64, hg * 64 + 64)
                    P.copy(ahT[rows, :, :], bH[rows, :].rearrange("p (c n) -> p c n", n=128),
                           eng=("act" if hg else "dve"))
                if STAGE < 7:
                    continue
                bU = bank()
                for c in range(4):
                    P.mm(bU[:, c * 128:(c + 1) * 128], ahT[:, c, :], Hbf[:, c, :])
                P.tt(ub[:, :], bU[:, :], vhat[:, :], ALU.add)
                bY = bank()
                for c in range(4):
                    P.mm(bY[:, c * 128:(c + 1) * 128], rt_b[:, c, js], Hbf[:, c, :], start=True, stop=False)
                    for h in (2 * c, 2 * c + 1):
                        o = slice(h * 64, (h + 1) * 64)
                        P.mm(bY[:, o], rbT[:, sidx(h), :], ub[:, o], start=False, stop=False)
                        P.mm(bY[:, o], rkT[:, sidx(h), :], vtm[:, j, o], start=False, stop=(h == 2 * c + 1))
                bS = bank()
                for c in range(4):
                    o = slice(c * 128, (c + 1) * 128)
                    P.mm(bS[:, o], kendtm[:, j, o], vtm[:, j, o], start=True, stop=False)
                    P.mm(bS[:, o], bendtm[:, j, o], ub[:, o], start=False, stop=True)
                P.tt(hx[:, :, :], bS[:, :].rearrange("p (c n) -> p c n", n=128), m_bd[:, :, :], ALU.mult)
                P.tt(H[:, :, :], H[:, :, :], pcs[:, j, :].unsqueeze(2).broadcast_to([128, 4, 128]), ALU.mult,
                     eng="pool")
                P.tt(H[:, :, :], H[:, :, :], hx[:, :, :], ALU.add, eng="pool")
                P.copy(Hbf[:, :, :], H[:, :, :], eng="pool")
                P.copy(ysb[:, :], bY[:, :], eng="act")
                P.act(ysq[:, :], bY[:, :], AF.Square)
                y3 = ysb[:, :].rearrange("p (h n) -> p h n", n=64)
                q3 = ysq[:, :].rearrange("p (h n) -> p h n", n=64)
                P.reduce(gst[:, :, 0], y3, ALU.add)
                P.reduce(gst[:, :, 1], q3, ALU.add)
                P.ts(gst[:, :, 0], gst[:, :, 0], 1.0 / 64, None, ALU.mult)
                P.tt(gst[:, :, 2], gst[:, :, 0], gst[:, :, 0], ALU.mult)
                P.stt(gst[:, :, 1], gst[:, :, 1], 1.0 / 64, gst[:, :, 2], ALU.mult, ALU.subtract)
                P.act(gst[:, :, 3], gst[:, :, 1], AF.Ln, bias=eps_t[:, 1:2])
                P.act(gst[:, :, 3], gst[:, :, 3], AF.Exp, scale=-0.5)
                P.tt(y3, y3, gst[:, :, 0].unsqueeze(2).broadcast_to([128, 8, 64]), ALU.subtract)
                P.tt(ynb[:, :].rearrange("p (h n) -> p h n", n=64), y3,
                     gst[:, :, 3].unsqueeze(2).broadcast_to([128, 8, 64]), ALU.mult)
                bt = bank_bf(bank())
                for c in range(4):
                    P.tr(bt[:, c * 128:(c + 1) * 128], ynb[:, c * 128:(c + 1) * 128], identb[:, :], inc=(c == 3))
                for c in range(4):
                    P.act(yab[:, c, js], bt[:, c * 128:(c + 1) * 128], AF.Identity,
                          bias=cl[:, C_LB + c:C_LB + c + 1], scale=cl[:, C_LG + c:C_LG + c + 1])
            if STAGE < 8:
                skip_rest()
                return
            for c in range(4):
                bb_ = bank()
                P.mm(bb_[:, 0:T], onesblk[:, :], rkb[:, c, :])
                bg = bank()
                P.mm(bg[:, 0:T], misc[:, 3, c * 128:(c + 1) * 128], glb[:, 0, :], start=True, stop=False)
                P.mm(bg[:, 0:T], misc[0:32, 4, c * 128:(c + 1) * 128], glb[0:32, 1, :], start=False, stop=True)
                s0 = ya_s[0]
                P.tt(s0[:, :], bb_[:, 0:T], vsh[:, c, :], ALU.mult)
                P.tt(s0[:, :], s0[:, :], yab[:, c, :], ALU.add, eng="pool")
                P.tt(yab[:, c, :], s0[:, :], bg[:, 0:T], ALU.mult)
            slot_pa = wget(l, "pa")
            slot_pb = wget(l, "pb")
            for jo in range(8):
                bpa = proj_chunk(slot_pa, jo * 128, 128, nk=4, kstride=1024, rhs=yab)
                bpb = proj_chunk(slot_pb, jo * 128, 128, nk=4, kstride=1024, rhs=ybb)
                P.tt(mx[0][:, :], sgab[:, 0, jo, :], bpa[:, 0:T], ALU.mult)
                P.tt(mx[1][:, :], sgab[:, 1, jo, :], bpb[:, 0:T], ALU.mult)
                P.tt(mixed[:, jo, :], mx[0][:, :], mx[1][:, :], ALU.add, eng="pool")
            wdone()
            wdone()
            for hh in range(2):
                slot = wget(l, f"wo{hh}")
                for jj in range(4):
                    jo = hh * 4 + jj
                    b = proj_chunk(slot, jj * 128, 128, rhs=mixed)
                    P.stt(xs[:, jo, :], b[:, 0:T], cl[:, C_GT1 + jo:C_GT1 + jo + 1], xs[:, jo, :], ALU.mult, ALU.add)
                wdone()

            if STAGE < 9:
                skip_rest()
                return
            rmsnorm_mod(C_G2, C_SH2, cl)
            for f0 in range(0, NFF, 4):
                nf = min(4, NFF - f0)
                slot_g = wget(l, f"fg{f0}")
                slot_u = wget(l, f"fu{f0}")
                for ff in range(nf):
                    bg = proj_chunk(slot_g, ff * 128, 128)
                    bu = proj_chunk(slot_u, ff * 128, 128)
                    s = fsc[ff % 2]
                    P.act(s[:, :], bg[:, 0:T], AF.Silu)
                    P.tt(actb[:, f0 + ff, :], s[:, :], bu[:, 0:T], ALU.mult)
                wdone()
                wdone()
            for hh in range(2):
                accs = [bank() for _ in range(4)]
                for k0 in range(0, NFF, 8):
                    nk = min(8, NFF - k0)
                    slot = wget(l, f"fd{hh}_{k0}")
                    sv = slot[:, 0:nk * 512].rearrange("p (k n) -> p k n", n=512)
                    for jj in range(4):
                        for k in range(nk):
                            P.mm(accs[jj][:, 0:T], sv[:, k, jj * 128:(jj + 1) * 128], actb[:, k0 + k, :],
                                 start=(k0 + k == 0), stop=(k0 + k == NFF - 1))
                    wdone()
                for jj in range(4):
                    jo = hh * 4 + jj
                    P.stt(xs[:, jo, :], accs[jj][:, 0:T], cl[:, C_GT2 + jo:C_GT2 + jo + 1], xs[:, jo, :],
                          ALU.mult, ALU.add)


        for l in range(L if STAGE >= 2 else 0):
            layer_body(l)
        squares()
        b = bank()
        for k in range(8):
            P.mm(b[:, 0:T], onesmean[:, :], sqb[:, k, :], start=(k == 0), stop=(k == 7))
        P.act(rstd[:, :], b[:, 0:T], AF.Ln, bias=eps_t[:, 0:1])
        P.act(rstd[:, :], rstd[:, :], AF.Exp, scale=-0.5)
        for k in range(8):
            P.stt(xs[:, k, :], xs[:, k, :], fgc[:, k:k + 1], rstd[:, :], ALU.mult, ALU.mult)
        for tb in range(NB):
            for kh in range(2):
                b = bank()
                for kk in range(4):
                    k = kh * 4 + kk
                    P.tr(b[:, kk * 128:(kk + 1) * 128], xs[:, k, tb * 128:(tb + 1) * 128], identf[:, :],
                         inc=(kk == 3))
                P.copy(ostage[:, tb, kh * 512:(kh + 1) * 512], b[:, :], eng=("act" if kh else "dve"))
            P.dma(out_d[t0 + tb * 128:t0 + (tb + 1) * 128, :], ostage[:, tb, :], sem_o[tb])
            n_out[0] += 1

    assert STAGE < 2 or wstate["cur"] == len(units)
    if debug_taps:
        dbg = nc.dram_tensor("dbg", [128, 96 + 10 * T + 512 * 3 + 8 * T], F32, kind="ExternalOutput").ap()
        sem_dbg = P.newsem("d_dbg")
        reqs = [(dbg[:, 0:96], cols[0][:, :], {})]
        for i in range(10):
            reqs.append((dbg[:, 96 + i * T:96 + (i + 1) * T], rs[i][:, :], {}))
        o = 96 + 10 * T
        reqs.append((dbg[:, o:o + 512], ysb[:, :], {}))
        reqs.append((dbg[:, o + 512:o + 1024], vhat[:, :], {}))
        reqs.append((dbg[:, o + 1024:o + 1536], Hs[0][:, :, :].rearrange("p a b -> p (a b)"), {}))
        o += 1536
        reqs.append((dbg[:, o:o + 8 * T].rearrange("p (k t) -> p k t", k=8), xs[:, :, :], {}))
        P.dma_group(sem_dbg, reqs)
        final_extra = [(sem_dbg, P.cnt[sem_dbg])]
    else:
        final_extra = []
    P.emit(final_waits=[(sem_o[tb], P.cnt[sem_o[tb]]) for tb in range(NB)] + final_extra)
    es.close()
    return nc, P


S_FULL, L_FULL = 4096, 4
_CACHE = {}


def run_cores(inputs, S, L, T=256):
    key = (S, L, T)
    if key not in _CACHE:
        _CACHE[key] = build_program(S, L, T)[0]
    nc = _CACHE[key]
    B = inputs["x"].shape[0]
    shared = {}
    for k, v in inputs.items():
        if k in ("x", "c"):
            continue
        a = np.ascontiguousarray(np.asarray(v, dtype=np.float32))
        if k == "r_k":
            a = a.reshape(a.shape[0], DR)
        elif k == "pool_w":
            a = a.reshape(a.shape[0], 512, 128)
        elif k == "final_g":
            a = a.reshape(1, D)
        shared[k] = a
    if L == 1:
        for k, shp in (("v_down", (1, D, 32)), ("mu_v", (1, 32)), ("v0", (1, DR)), ("v_up", (1, 32, DR))):
            if shared[k].shape[0] == 0:
                shared[k] = np.zeros(shp, np.float32)
    x = np.asarray(inputs["x"], dtype=np.float32)
    c = np.asarray(inputs["c"], dtype=np.float32)
    in_maps = []
    for b in range(B):
        m = dict(shared)
        m["x"] = np.ascontiguousarray(x[b])
        m["c"] = np.ascontiguousarray(c[b:b + 1])
        in_maps.append(m)
    res = run_bass_kernel_spmd(nc, in_maps, core_ids=list(range(B)))
    return np.stack([np.asarray(r["out"]) for r in res.results], axis=0).astype(np.float32)


def kernel(**inputs):
    return run_cores(inputs, S_FULL, L_FULL)
```
